# Optimizing a Trainium2 kernel written in Bass

```python
import math
import jax
import jax.numpy as jnp
from jax import lax
import numpy as np

D_MODEL = 1024
BATCH = 8
SEQ = 4096
DEPTH = 4

NORM_EPS = 1e-6
RWKV_HEAD_DIM = 64
RWKV_WIDTH = D_MODEL // 2
RWKV_HEADS = RWKV_WIDTH // RWKV_HEAD_DIM
DECAY_LORA = 32
ICL_LORA = 32
GATE_LORA = 64
RWKV_GN_EPS = 64e-5
RWKV_COLS = 3 * RWKV_WIDTH + 2 * DECAY_LORA + 2 * ICL_LORA + GATE_LORA
SSM_WIDTH = D_MODEL
SSM_HEAD_DIM = 64
SSM_HEADS = SSM_WIDTH // SSM_HEAD_DIM
SSM_GROUPS = 2
HEADS_PER_GROUP = SSM_HEADS // SSM_GROUPS
SSM_STATE = 128
SSM_CONV = 5
SSM_CHUNK = 128
SSM_XBC = SSM_WIDTH + 2 * SSM_GROUPS * SSM_STATE
SSM_COLS = SSM_WIDTH + SSM_XBC + 2 * SSM_HEADS
RET_WIDTH = D_MODEL // 2
RET_HEADS = 4
RET_KEY_DIM = RET_WIDTH // RET_HEADS
RET_VAL_DIM = RET_WIDTH // RET_HEADS
RET_CHUNK = 128
ROPE_BASE = 10000.0
RET_COLS = 4 * RET_WIDTH
N_BRANCH = 3
GATE_COLS = N_BRANCH * D_MODEL
N_IN = RWKV_COLS + SSM_COLS + RET_COLS + GATE_COLS
BRANCH_IN = RWKV_WIDTH + SSM_WIDTH + RET_WIDTH
D_FF = 4 * D_MODEL

kernel_name = 'hybrid_rwkv7_mamba2_retnet_encoder'


def split_cols(x, widths):
    offs = np.cumsum([0] + list(widths))
    return [x[..., int(offs[i]):int(offs[i + 1])] for i in range(len(widths))]


def rms_norm(x, w):
    xf = x.astype(jnp.float32)
    y = xf * lax.rsqrt(jnp.mean(xf * xf, axis=-1, keepdims=True) + NORM_EPS)
    return (y * w.astype(jnp.float32)).astype(x.dtype)


def flip_time(t, rev):
    return jnp.flip(t, axis=1) if rev else t


def centred_token_shift(p, mu):
    prev = jnp.pad(p[:, :-1], ((0, 0), (1, 0), (0, 0)))
    nxt = jnp.pad(p[:, 1:], ((0, 0), (0, 1), (0, 0)))
    return p + mu * (0.5 * (prev + nxt) - p)


def rotary(x):
    seq, dim = x.shape[1], x.shape[-1]
    half = dim // 2
    inv_freq = 1.0 / (ROPE_BASE ** jnp.linspace(0.0, 1.0, half, dtype=jnp.float32))
    ang = jnp.arange(seq, dtype=jnp.float32)[:, None] * inv_freq[None, :]
    cos = jnp.cos(ang)[None, :, None, :]
    sin = jnp.sin(ang)[None, :, None, :]
    xf = x.astype(jnp.float32)
    x1, x2 = xf[..., :half], xf[..., half:]
    return jnp.concatenate([x1 * cos - x2 * sin, x2 * cos + x1 * sin], axis=-1).astype(x.dtype)


def depthwise_conv_centred(x, w, b):
    k = w.shape[0]
    y = lax.conv_general_dilated(x, w[:, None, :], window_strides=(1,), padding=[(k // 2, k // 2)],
                                 dimension_numbers=('NWC', 'WIO', 'NWC'), feature_group_count=x.shape[-1])
    return y + b


def rwkv7_mixer(p, mu, w0, w_up, a0, a_up, g_up, k_k, k_a, r_k, ln_w, ln_b):
    bsz, seq, _ = p.shape
    f32 = jnp.float32
    hd = (RWKV_HEADS, RWKV_HEAD_DIM)
    u = centred_token_shift(p, mu)
    r, k, v, wd, ad, gd = split_cols(u, [RWKV_WIDTH, RWKV_WIDTH, RWKV_WIDTH, 2 * DECAY_LORA, 2 * ICL_LORA, GATE_LORA])
    wd = wd.reshape(bsz, seq, 2, DECAY_LORA)
    ad = ad.reshape(bsz, seq, 2, ICL_LORA)
    w_log = -jax.nn.softplus(-(w0 + jnp.einsum('btdl,dlc->btdc', jnp.tanh(wd), w_up)).astype(f32)) - 0.5
    decay = jnp.exp(-jnp.exp(w_log))
    a = jax.nn.sigmoid((a0 + jnp.einsum('btdl,dlc->btdc', ad, a_up)).astype(f32))
    g = jax.nn.sigmoid(gd) @ g_up
    kk = (k * k_k).astype(f32).reshape(bsz, seq, *hd)
    kk = kk / jnp.maximum(jnp.sqrt(jnp.sum(kk * kk, axis=-1, keepdims=True)), 1e-12)
    k_dir = k.astype(f32)[:, :, None] * (1.0 + (a - 1.0) * k_a.astype(f32))
    r_h = r.astype(f32).reshape(bsz, seq, *hd)
    v_h = v.astype(f32).reshape(bsz, seq, *hd)

    def shared(t):
        return jnp.stack([t, jnp.flip(t, axis=1)], axis=2)

    def per_dir(t):
        t = t.reshape(bsz, seq, 2, *hd)
        return jnp.stack([t[:, :, 0], jnp.flip(t[:, :, 1], axis=1)], axis=2)

    inputs = (shared(r_h), per_dir(decay), per_dir(k_dir), shared(v_h), shared(kk), per_dir(a))
    inputs = tuple(jnp.moveaxis(t, 1, 0) for t in inputs)

    def step(state, inp):
        r_t, w_t, k_t, v_t, kk_t, a_t = inp
        sa = jnp.einsum('bdhij,bdhj->bdhi', state, -kk_t)
        state = (state * w_t[..., None, :]
                 + sa[..., :, None] * (kk_t * a_t)[..., None, :]
                 + v_t[..., :, None] * k_t[..., None, :])
        return state, jnp.einsum('bdhij,bdhj->bdhi', state, r_t)

    init = jnp.zeros((bsz, 2, RWKV_HEADS, RWKV_HEAD_DIM, RWKV_HEAD_DIM), f32)
    _, ys = lax.scan(step, init, inputs)
    ys = jnp.moveaxis(ys, 0, 1)
    y = ys[:, :, 0] + jnp.flip(ys[:, :, 1], axis=1)
    mean = jnp.mean(y, axis=-1, keepdims=True)
    var = jnp.mean(jnp.square(y - mean), axis=-1, keepdims=True)
    yn = ((y - mean) * lax.rsqrt(var + RWKV_GN_EPS)).reshape(bsz, seq, RWKV_WIDTH)
    yn = yn * ln_w.astype(f32) + ln_b.astype(f32)
    k_bonus = (0.5 * (k_dir[:, :, 0] + k_dir[:, :, 1])).reshape(bsz, seq, *hd)
    bonus = jnp.sum(r_h * k_bonus * r_k.astype(f32), axis=-1, keepdims=True) * v_h
    out = (yn + bonus.reshape(bsz, seq, RWKV_WIDTH)).astype(p.dtype)
    return out * g


def ssd_chunked(x, log_a, b_in, c_in):
    bsz, seq, g, e, p = x.shape
    n = b_in.shape[-1]
    nc, L = seq // SSM_CHUNK, SSM_CHUNK
    x = x.reshape(bsz, nc, L, g, e, p)
    b_in = b_in.reshape(bsz, nc, L, g, n)
    c_in = c_in.reshape(bsz, nc, L, g, n)
    a_cum = jnp.cumsum(log_a.reshape(bsz, nc, L, g, e).astype(jnp.float32), axis=2)
    a_cum = jnp.moveaxis(a_cum, 2, -1)
    lower = jnp.tril(jnp.ones((L, L), dtype=bool))
    seg = a_cum[..., :, None] - a_cum[..., None, :]
    decay_ls = jnp.exp(jnp.where(lower, seg, -jnp.inf)).astype(x.dtype)
    scores = jnp.einsum('bclgn,bcsgn->bcgls', c_in, b_in)
    y_diag = jnp.einsum('bcgels,bcsgep->bclgep', decay_ls * scores[:, :, :, None], x)
    decay_end = jnp.exp(a_cum[..., -1:] - a_cum).astype(x.dtype)
    chunk_states = jnp.einsum('bclgn,bcgel,bclgep->bcgepn', b_in, decay_end, x)
    chunk_decay = jnp.exp(a_cum[..., -1]).astype(x.dtype)

    def carry(h, inp):
        s_c, d_c = inp
        return h * d_c[..., None, None] + s_c, h

    h0 = jnp.zeros((bsz, g, e, p, n), x.dtype)
    _, h_in = lax.scan(carry, h0, (jnp.moveaxis(chunk_states, 1, 0), jnp.moveaxis(chunk_decay, 1, 0)))
    h_in = jnp.moveaxis(h_in, 0, 1)
    y_off = jnp.einsum('bclgn,bcgepn,bcgel->bclgep', c_in, h_in, jnp.exp(a_cum).astype(x.dtype))
    return (y_diag + y_off).reshape(bsz, seq, g, e, p)


def mamba2_mixer(p, conv_w, conv_b, dt_bias, a_log, d_skip, norm_w):
    bsz, seq, _ = p.shape
    gn = SSM_GROUPS * SSM_STATE
    z, xbc, dt_raw = split_cols(p, [SSM_WIDTH, SSM_XBC, 2 * SSM_HEADS])
    xbc = jax.nn.silu(depthwise_conv_centred(xbc, conv_w, conv_b))
    xs, b_in, c_in = split_cols(xbc, [SSM_WIDTH, gn, gn])
    xs = xs.reshape(bsz, seq, SSM_GROUPS, HEADS_PER_GROUP, SSM_HEAD_DIM)
    b_in = b_in.reshape(bsz, seq, SSM_GROUPS, SSM_STATE)
    c_in = c_in.reshape(bsz, seq, SSM_GROUPS, SSM_STATE)
    dt = jax.nn.softplus(dt_raw.reshape(bsz, seq, 2, SSM_HEADS) + dt_bias)
    dt = dt.reshape(bsz, seq, 2, SSM_GROUPS, HEADS_PER_GROUP)
    log_a = -jnp.exp(a_log).reshape(2, SSM_GROUPS, HEADS_PER_GROUP) * dt
    y = d_skip.reshape(SSM_GROUPS, HEADS_PER_GROUP)[..., None] * xs
    for direction in range(2):
        rev = direction == 1
        y_dir = ssd_chunked(flip_time(xs * dt[:, :, direction, :, :, None], rev),
                            flip_time(log_a[:, :, direction], rev),
                            flip_time(b_in, rev), flip_time(c_in, rev))
        y = y + flip_time(y_dir, rev)
    y = y.reshape(bsz, seq, SSM_WIDTH) * jax.nn.silu(z)
    yf = y.astype(jnp.float32).reshape(bsz, seq, SSM_GROUPS, SSM_WIDTH // SSM_GROUPS)
    yf = yf * lax.rsqrt(jnp.mean(yf * yf, axis=-1, keepdims=True) + NORM_EPS)
    return (yf.reshape(bsz, seq, SSM_WIDTH) * norm_w.astype(jnp.float32)).astype(p.dtype)


def retention_mixer(p):
    bsz, seq, _ = p.shape
    dt = p.dtype
    q, k, v, g = split_cols(p, [RET_WIDTH] * 4)
    q = rotary(q.reshape(bsz, seq, RET_HEADS, RET_KEY_DIM))
    k = rotary(k.reshape(bsz, seq, RET_HEADS, RET_KEY_DIM)) * (RET_KEY_DIM ** -0.5)
    nc, L = seq // RET_CHUNK, RET_CHUNK
    q = q.reshape(bsz, nc, L, RET_HEADS, RET_KEY_DIM)
    k = k.reshape(bsz, nc, L, RET_HEADS, RET_KEY_DIM)
    v = v.reshape(bsz, nc, L, RET_HEADS, RET_VAL_DIM)
    log_gamma = jnp.log(1.0 - 2.0 ** (-5.0 - jnp.arange(RET_HEADS, dtype=jnp.float32)))
    pos = jnp.arange(L, dtype=jnp.float32)

    def pw(expo):
        return jnp.exp(log_gamma[:, None] * expo[None, :]).astype(dt)

    intra = jnp.exp(log_gamma[:, None, None] * jnp.abs(pos[:, None] - pos[None, :])[None]).astype(dt)
    scores = jnp.einsum('bclhd,bcshd->bchls', q, k) * intra
    out = jnp.einsum('bchls,bcshe->bclhe', scores, v)
    kv_fwd = jnp.einsum('bclhd,hl,bclhe->bchde', k, pw(L - 1.0 - pos), v)
    kv_bwd = jnp.einsum('bclhd,hl,bclhe->bchde', k, pw(pos), v)
    chunk_decay = jnp.exp(log_gamma * L).astype(dt)

    def carry(state, kv_c):
        return state * chunk_decay[:, None, None] + kv_c, state

    s0 = jnp.zeros((bsz, RET_HEADS, RET_KEY_DIM, RET_VAL_DIM), dt)
    _, st_fwd = lax.scan(carry, s0, jnp.moveaxis(kv_fwd, 1, 0))
    _, st_bwd = lax.scan(carry, s0, jnp.moveaxis(kv_bwd, 1, 0), reverse=True)
    out = (out
           + jnp.einsum('bclhd,hl,cbhde->bclhe', q, pw(pos + 1.0), st_fwd)
           + jnp.einsum('bclhd,hl,cbhde->bclhe', q, pw(L - pos), st_bwd))
    of = out.reshape(bsz, seq, RET_HEADS, RET_VAL_DIM).astype(jnp.float32)
    of = of * lax.rsqrt(jnp.mean(of * of, axis=-1, keepdims=True) + NORM_EPS)
    return jax.nn.silu(g) * of.reshape(bsz, seq, RET_WIDTH).astype(dt)


def setup_inputs(seed: int = 0) -> dict:
    key = jax.random.key(seed)
    ks = iter(jax.random.split(key, 40))
    f32 = jnp.float32
    nl = DEPTH

    def normal(shape, scale):
        return scale * jax.random.normal(next(ks), shape, f32)

    def uniform(shape, lo, hi):
        return jax.random.uniform(next(ks), shape, f32, lo, hi)

    x = normal((BATCH, SEQ, D_MODEL), 1.0)
    attn_norm = 1.0 + normal((nl, D_MODEL), 0.02)
    w_in = normal((nl, D_MODEL, N_IN), D_MODEL ** -0.5)
    rwkv_mu = uniform((nl, RWKV_COLS), 0.0, 1.0)
    rwkv_w0 = uniform((nl, 2, RWKV_WIDTH), -6.0, -1.0)
    rwkv_w_up = normal((nl, 2, DECAY_LORA, RWKV_WIDTH), 0.5 * DECAY_LORA ** -0.5)
    rwkv_a0 = normal((nl, 2, RWKV_WIDTH), 0.5)
    rwkv_a_up = normal((nl, 2, ICL_LORA, RWKV_WIDTH), ICL_LORA ** -0.5)
    rwkv_g_up = normal((nl, GATE_LORA, RWKV_WIDTH), GATE_LORA ** -0.5)
    rwkv_k_k = 0.85 + normal((nl, RWKV_WIDTH), 0.02)
    rwkv_k_a = 1.0 + normal((nl, RWKV_WIDTH), 0.02)
    rwkv_r_k = normal((nl, RWKV_HEADS, RWKV_HEAD_DIM), 0.1)
    rwkv_ln_w = 1.0 + normal((nl, RWKV_WIDTH), 0.02)
    rwkv_ln_b = normal((nl, RWKV_WIDTH), 0.02)
    ssm_conv_w = normal((nl, SSM_CONV, SSM_XBC), SSM_CONV ** -0.5)
    ssm_conv_b = normal((nl, SSM_XBC), 0.02)
    dt0 = jnp.exp(uniform((nl, 2, SSM_HEADS), math.log(1e-3), math.log(1e-1)))
    ssm_dt_bias = dt0 + jnp.log(-jnp.expm1(-dt0))
    ssm_a_log = jnp.log(uniform((nl, 2, SSM_HEADS), 1.0, 16.0))
    ssm_d = 1.0 + normal((nl, SSM_HEADS), 0.02)
    ssm_norm_w = 1.0 + normal((nl, SSM_WIDTH), 0.02)
    w_branch = jnp.concatenate([normal((nl, RWKV_WIDTH, D_MODEL), RWKV_WIDTH ** -0.5),
                                normal((nl, SSM_WIDTH, D_MODEL), SSM_WIDTH ** -0.5),
                                normal((nl, RET_WIDTH, D_MODEL), RET_WIDTH ** -0.5)], axis=1)
    w_out = normal((nl, D_MODEL, D_MODEL), D_MODEL ** -0.5)
    mlp_norm = 1.0 + normal((nl, D_MODEL), 0.02)
    w_mlp_in = normal((nl, D_MODEL, D_FF), D_MODEL ** -0.5)
    w_mlp_out = normal((nl, D_FF, D_MODEL), D_FF ** -0.5)
    final_norm = 1.0 + normal((D_MODEL,), 0.02)
    return {'x': x, 'attn_norm': attn_norm, 'w_in': w_in, 'rwkv_mu': rwkv_mu, 'rwkv_w0': rwkv_w0,
            'rwkv_w_up': rwkv_w_up, 'rwkv_a0': rwkv_a0, 'rwkv_a_up': rwkv_a_up, 'rwkv_g_up': rwkv_g_up,
            'rwkv_k_k': rwkv_k_k, 'rwkv_k_a': rwkv_k_a, 'rwkv_r_k': rwkv_r_k, 'rwkv_ln_w': rwkv_ln_w,
            'rwkv_ln_b': rwkv_ln_b, 'ssm_conv_w': ssm_conv_w, 'ssm_conv_b': ssm_conv_b,
            'ssm_dt_bias': ssm_dt_bias, 'ssm_a_log': ssm_a_log, 'ssm_d': ssm_d, 'ssm_norm_w': ssm_norm_w,
            'w_branch': w_branch, 'w_out': w_out, 'mlp_norm': mlp_norm, 'w_mlp_in': w_mlp_in,
            'w_mlp_out': w_mlp_out, 'final_norm': final_norm}


def reference(x, attn_norm, w_in, rwkv_mu, rwkv_w0, rwkv_w_up, rwkv_a0, rwkv_a_up, rwkv_g_up,
              rwkv_k_k, rwkv_k_a, rwkv_r_k, rwkv_ln_w, rwkv_ln_b, ssm_conv_w, ssm_conv_b,
              ssm_dt_bias, ssm_a_log, ssm_d, ssm_norm_w, w_branch, w_out, mlp_norm, w_mlp_in,
              w_mlp_out, final_norm):
    bsz, seq, _ = x.shape
    for layer in range(DEPTH):
        h = rms_norm(x, attn_norm[layer])
        proj = h @ w_in[layer]
        p_rwkv, p_ssm, p_ret, p_gate = split_cols(proj, [RWKV_COLS, SSM_COLS, RET_COLS, GATE_COLS])
        o_rwkv = rwkv7_mixer(p_rwkv, rwkv_mu[layer], rwkv_w0[layer], rwkv_w_up[layer], rwkv_a0[layer],
                             rwkv_a_up[layer], rwkv_g_up[layer], rwkv_k_k[layer], rwkv_k_a[layer],
                             rwkv_r_k[layer], rwkv_ln_w[layer], rwkv_ln_b[layer])
        o_ssm = mamba2_mixer(p_ssm, ssm_conv_w[layer], ssm_conv_b[layer], ssm_dt_bias[layer],
                             ssm_a_log[layer], ssm_d[layer], ssm_norm_w[layer])
        o_ret = retention_mixer(p_ret)
        wb = w_branch[layer]
        gates = jax.nn.sigmoid(p_gate).reshape(bsz, seq, N_BRANCH, D_MODEL)
        merged = (gates[:, :, 0] * (o_rwkv @ wb[:RWKV_WIDTH])
                  + gates[:, :, 1] * (o_ssm @ wb[RWKV_WIDTH:RWKV_WIDTH + SSM_WIDTH])
                  + gates[:, :, 2] * (o_ret @ wb[RWKV_WIDTH + SSM_WIDTH:]))
        x = x + merged @ w_out[layer]
        h = rms_norm(x, mlp_norm[layer])
        x = x + jnp.square(jax.nn.relu(h @ w_mlp_in[layer])) @ w_mlp_out[layer]
    return rms_norm(x, final_norm)
```

```python
import math
import contextlib
import numpy as np
import concourse.bass as bass
import concourse.mybir as mybir
from concourse.bass_utils import run_bass_kernel_spmd

F32 = mybir.dt.float32
AF = mybir.ActivationFunctionType
ALU = mybir.AluOpType
AX = mybir.AxisListType

D = 1024
SEQ = 4096
DEPTH = 4
N_IN = 9440
EPS = 1e-6
C_RWKV, C_SSM, C_RET, C_GATE = 0, 1728, 4320, 6368
EPOCH = 20000
import os
NO_POOL = bool(os.environ.get('NO_POOL'))
OPLOG = bool(os.environ.get('OPLOG'))
MAXOPS = int(os.environ.get('MAXOPS', '100000000'))


class Buf:
    __slots__ = ("last_w", "readers", "excl")

    def __init__(self, excl=False):
        self.last_w = None
        self.readers = {}
        self.excl = excl


class Prog:
    ENGS = ("pe", "act", "dve", "pool", "sp")

    def __init__(self, nc, n_dma_sems=32):
        self.nc = nc
        self.streams = {e: [] for e in self.ENGS}
        self.cnt = {e: 0 for e in self.ENGS}
        self.epoch = {e: 0 for e in self.ENGS}
        self.seen = {e: {} for e in self.ENGS}
        self.nd = n_dma_sems
        self.dma_i = 0
        self.nops = 0
        self.dma_last = {}

    def _wait(self, eng, tok):
        key, val = tok
        if self.seen[eng].get(key, 0) >= val:
            return
        self.seen[eng][key] = val
        self.streams[eng].append(("w", key, val))

    def _deps(self, reads, writes):
        deps = set()
        for b in reads:
            if b.last_w is not None:
                deps.add(b.last_w)
            if b.excl:
                for t in b.readers.values():
                    deps.add(t)
        for b in writes:
            if b.last_w is not None:
                deps.add(b.last_w)
            for t in b.readers.values():
                deps.add(t)
        return deps

    def op(self, eng, fn, reads=(), writes=()):
        if self.nops >= MAXOPS:
            return None
        if eng == "pool" and NO_POOL:
            eng = "dve"
        deps = self._deps(reads, writes)
        for tok in sorted(deps, key=lambda t: (str(t[0]), t[1])):
            if eng == "pe" and tok[0][0] == "pe":
                continue
            self._wait(eng, tok)
        self.cnt[eng] += 1
        if self.cnt[eng] > EPOCH:
            self.epoch[eng] += 1
            self.cnt[eng] = 1
        key = (eng, self.epoch[eng])
        tok = (key, self.cnt[eng])
        self.streams[eng].append(("op", fn, key, 1))
        for b in writes:
            b.last_w = tok
            b.readers = {}
        for b in reads:
            b.readers[eng] = tok
        self.nops += 1
        if OPLOG:
            import inspect
            fr = inspect.stack()[1]
            print("OP", self.nops, eng, fr.lineno, fr.code_context[0].strip()[:90])
        return tok

    def dma(self, out, in_, reads=(), writes=(), eng="sp"):
        if self.nops >= MAXOPS:
            return None
        deps = self._deps(reads, writes)
        i = self.dma_i
        self.dma_i += 1
        slot = i % self.nd
        rnd = i // self.nd
        key = ("d", slot)
        if rnd > 0:
            deps.add((key, 16 * rnd))
        for tok in sorted(deps, key=lambda t: (str(t[0]), t[1])):
            self._wait(eng, tok)
        tok = (key, 16 * (rnd + 1))
        self.streams[eng].append(("op", lambda e, o=out, i_=in_: e.dma_start(out=o, in_=i_), key, 16))
        self.dma_last[key] = tok
        for b in writes:
            b.last_w = tok
            b.readers = {}
        for b in reads:
            b.readers[("dma", slot)] = tok
        self.nops += 1
        if OPLOG:
            import inspect
            fr = inspect.stack()[1]
            print("OP", self.nops, "dma", fr.lineno, fr.code_context[0].strip()[:90])
        return tok

    def barrier(self):
        toks = []
        for e in self.ENGS:
            if self.cnt[e] > 0:
                toks.append(((e, self.epoch[e]), self.cnt[e]))
        toks += list(self.dma_last.values())
        for e in self.ENGS:
            for t in toks:
                if e == "pe" and t[0][0] == "pe":
                    continue
                self._wait(e, t)

    def emit(self):
        nc = self.nc
        final = list(self.dma_last.values())
        keys = set()
        for e in self.ENGS:
            for it in self.streams[e]:
                keys.add(it[1] if it[0] == "w" else it[2])
        keys = sorted(keys, key=str)
        with contextlib.ExitStack() as st:
            semh = {}
            for k in keys:
                semh[k] = st.enter_context(nc.semaphore("s_" + "_".join(str(x) for x in k)))
            block = st.enter_context(nc.Block())
            streams = self.streams

            def run(engname):
                def f(e):
                    for it in streams[engname]:
                        if it[0] == "w":
                            e.wait_ge(semh[it[1]], it[2])
                        else:
                            it[1](e).then_inc(semh[it[2]], it[3])
                    if engname == "sp":
                        for (k, v) in final:
                            e.wait_ge(semh[k], v)
                return f
            block.tensor(run("pe"))
            block.scalar(run("act"))
            block.vector(run("dve"))
            block.gpsimd(run("pool"))
            block.sync(run("sp"))


class Arena:
    def __init__(self, nc, words):
        self.t = nc.alloc_sbuf_tensor("arena", [128, words], F32)
        self.words = words
        self.off = 0

    def mark(self):
        return self.off

    def reset(self, m):
        self.off = m

    def alloc(self, n):
        o = self.off
        self.off += n
        assert self.off <= self.words, "SBUF arena overflow %d > %d" % (self.off, self.words)
        return self.t[:, o:o + n]


class Ctx:
    pass


def load_cst(X, names):
    out = {}
    for (n, w) in names:
        t = X.A.alloc(w)
        X.P.dma(t, X.cst_d[:, X.co[n]:X.co[n] + w], writes=[X.bconst])
        out[n] = t
    X.P.barrier()
    return out


OFFS = {}


RET_SCALE = 128 ** -0.5


def host_consts(T):
    parts = []
    co = {}
    off = [0]

    def add(name, arr):
        arr = np.asarray(arr, np.float64).reshape(128, -1)
        co[name] = off[0]
        parts.append(arr)
        off[0] += arr.shape[1]
    add("ones", np.ones((128, 128)))
    add("ident", np.eye(128))
    sw = np.zeros((128, 128))
    for m in range(128):
        sw[(m + 64) % 128, m] = 1.0
    add("swap", sw)
    lg = np.log(1.0 - 2.0 ** (-5.0 - np.arange(4, dtype=np.float64)))
    pos = np.arange(128, dtype=np.float64)
    mask = np.exp(lg[None, :, None] * np.abs(pos[:, None, None] - pos[None, None, :])) * RET_SCALE
    add("ret_mask", mask)
    add("ret_vf", np.exp(lg[None, :] * (127.0 - pos[:, None])) * RET_SCALE)
    add("ret_vb", np.exp(lg[None, :] * pos[:, None]) * RET_SCALE)
    add("ret_gf", np.broadcast_to(np.exp(lg[None, :, None] * (pos[None, None, :] + 1.0)), (128, 4, 128)))
    add("ret_gb", np.broadcast_to(np.exp(lg[None, :, None] * (128.0 - pos[None, None, :])), (128, 4, 128)))
    selu = np.zeros((128, 32, 128))
    for hd in range(32):
        dr, hh = hd // 16, hd % 16
        selu[dr * 32 + hh, hd, :] = 1.0
    add("ssm_selu", selu)
    sl = pos[:, None] - pos[None, :]
    add("ssm_maskf", np.tile(np.where(sl <= 0, 0.0, -30000.0), (1, 4)))
    add("ssm_maskb", np.tile(np.where(sl >= 0, 0.0, -30000.0), (1, 4)))
    pf = pos[:, None] - pos[None, :]
    low = (pf > 0) * 1.0
    up = (pf < 0) * 1.0
    lowi = (pf >= 0) * 1.0
    upi = (pf <= 0) * 1.0
    add("rw_low4", np.tile(low, (1, 4)))
    add("rw_up4", np.tile(up, (1, 4)))
    add("rw_upupi2", np.tile(np.concatenate([up, upi], axis=1), (1, 2)))
    add("rw_lowlowi2", np.tile(np.concatenate([low, lowi], axis=1), (1, 2)))
    bi = np.arange(128)
    bd32 = (bi[:, None] // 32 == bi[None, :] // 32) * 1.0
    bd64 = (bi[:, None] // 64 == bi[None, :] // 64) * 1.0
    add("rw_bd_low4", np.tile(bd32 * low, (1, 4)))
    add("rw_bd_up4", np.tile(bd32 * up, (1, 4)))
    add("rw_l32_low4", np.tile(bd64 * (1 - bd32) * low, (1, 4)))
    add("rw_l32_up4", np.tile(bd64 * (1 - bd32) * up, (1, 4)))
    add("rw_l64_low4", np.tile((1 - bd64) * low, (1, 4)))
    add("rw_l64_up4", np.tile((1 - bd64) * up, (1, 4)))
    blk = np.zeros((128, 128))
    blk[0:64, 0:64] = 1.0
    blk[64:128, 64:128] = 1.0
    add("rw_blk", blk)
    hs = np.zeros((128, 2))
    hs[0:64, 0] = 1.0
    hs[64:128, 1] = 1.0
    add("rw_halfsel", hs)
    cst = np.ascontiguousarray(np.concatenate(parts, axis=1).astype(np.float32))
    co["ret_gl"] = [float(np.exp(lg[h] * 128.0)) for h in range(4)]
    inv_freq = 1.0 / (10000.0 ** np.linspace(0.0, 1.0, 64))
    ang = np.arange(T, dtype=np.float64)[None, :] * inv_freq[:, None]
    ang = (np.arange(T, dtype=np.float32)[None, :] * inv_freq.astype(np.float32)[:, None]).astype(np.float64)
    cos, sin = np.cos(ang), np.sin(ang)
    rope = np.zeros((128, 2, T), np.float32)
    rope[0:64, 0] = cos
    rope[64:128, 0] = cos
    rope[0:64, 1] = -sin
    rope[64:128, 1] = sin
    return cst, co, rope


def col_tiles():
    tiles = []

    def rng(c0, n):
        o = 0
        while o < n:
            w = min(128, n - o)
            tiles.append((c0 + o, w))
            o += w
    rng(C_RWKV, 1728)
    rng(C_SSM, 1024)
    rng(C_SSM + 1024, 1536)
    rng(C_SSM + 2560, 32)
    rng(C_RET, 2048)
    rng(C_GATE, 3072)
    return tiles


def col_groups():
    groups = []
    cur = []
    for (c0, w) in col_tiles():
        if cur and (cur[-1][0] + cur[-1][1] == c0) and (sum(t[1] for t in cur) + w <= 512):
            cur.append((c0, w))
        else:
            if cur:
                groups.append(cur)
            cur = [(c0, w)]
    groups.append(cur)
    return groups


def stage_in(X, layer):
    nc, P, A = X.nc, X.P, X.A
    T = X.T
    HB = min(T, 2048)
    NH = T // HB
    NTBH = HB // 512
    m0 = A.mark()
    xt = [A.alloc(8 * 512) for _ in range(2)]
    bxt = [Buf() for _ in range(2)]
    sq = A.alloc(8 * 512)
    bsq = Buf()
    rstd = A.alloc(512)
    brstd = Buf()
    hT = A.alloc(8 * HB)
    bh = Buf()
    hTv = hT.rearrange("p (k t) -> p k t", k=8)
    wt = [A.alloc(8 * 512) for _ in range(2)]
    bwt = [Buf() for _ in range(2)]
    stg = [A.alloc(512) for _ in range(3)]
    bstg = [Buf() for _ in range(3)]
    groups = col_groups()
    w_in = X.w_in
    ps = X.ps
    bps = X.bps
    psi = 0
    wi = 0
    si = 0
    xi = 0
    for hf in range(NH):
        for tb in range(NTBH):
            j = xi % 2
            xi += 1
            t0 = hf * HB + tb * 512
            xv = xt[j].rearrange("p (k t) -> p k t", k=8)
            P.dma(xv, X.xcur[:, t0:t0 + 512].rearrange("(k p) t -> p k t", p=128), writes=[bxt[j]])
            P.op("act", lambda e, j=j: e.activation(out=sq, in_=xt[j], func=AF.Square), reads=[bxt[j]], writes=[bsq])
            pb = psi % 8
            psi += 1
            for kt in range(8):
                P.op("pe", lambda e, kt=kt, pb=pb: e.matmul(ps[pb], X.ones, sq[:, kt * 512:(kt + 1) * 512],
                                                             start=(kt == 0), stop=(kt == 7)),
                     reads=[bsq, X.bconst], writes=[bps[pb]])
            P.op("dve", lambda e, pb=pb: e.tensor_scalar(out=rstd, in0=ps[pb], scalar1=1.0 / D, scalar2=EPS,
                                                          op0=ALU.mult, op1=ALU.add), reads=[bps[pb]], writes=[brstd])
            P.op("act", lambda e: e.activation(out=rstd, in_=rstd, func=AF.Sqrt), reads=[brstd], writes=[brstd])
            P.op("dve", lambda e: e.reciprocal(out=rstd, in_=rstd), reads=[brstd], writes=[brstd])
            for kt in range(8):
                P.op("dve" if kt % 2 == 0 else "dve", lambda e, kt=kt, j=j, tb=tb: e.scalar_tensor_tensor(
                    out=hTv[:, kt, tb * 512:(tb + 1) * 512], in0=xt[j][:, kt * 512:(kt + 1) * 512],
                    scalar=X.small[:, X.so["attn_norm"] + layer * 8 + kt: X.so["attn_norm"] + layer * 8 + kt + 1],
                    in1=rstd, op0=ALU.mult, op1=ALU.mult), reads=[bxt[j], brstd, X.bconst], writes=[bh])
        for grp in groups:
            g0 = grp[0][0]
            gw = sum(t[1] for t in grp)
            k = wi % 2
            wi += 1
            wtv = wt[k].rearrange("p (k c) -> p k c", k=8)
            P.dma(wtv[:, :, 0:gw], w_in[layer, :, g0:g0 + gw].rearrange("(k p) c -> p k c", p=128), writes=[bwt[k]])
            for tb in range(NTBH):
                t0 = hf * HB + tb * 512
                for (c0, w) in grp:
                    off = c0 - g0
                    pb = psi % 8
                    psi += 1
                    for kt in range(8):
                        P.op("pe", lambda e, kt=kt, pb=pb, wtv=wtv, w=w, off=off, tb=tb: e.matmul(
                            ps[pb][0:w, :], wtv[:, kt, off:off + w], hTv[:, kt, tb * 512:(tb + 1) * 512], start=(kt == 0), stop=(kt == 7)),
                            reads=[bwt[k], bh], writes=[bps[pb]])
                    s = si % 3
                    si += 1
                    if si % 2 == 0:
                        P.op("act", lambda e, s=s, pb=pb, w=w: e.activation(out=stg[s][0:w, :], in_=ps[pb][0:w, :], func=AF.Copy),
                             reads=[bps[pb]], writes=[bstg[s]])
                    else:
                        P.op("dve", lambda e, s=s, pb=pb, w=w: e.tensor_copy(out=stg[s][0:w, :], in_=ps[pb][0:w, :]),
                             reads=[bps[pb]], writes=[bstg[s]])
                    P.dma(X.PT[c0:c0 + w, t0:t0 + 512], stg[s][0:w, :], reads=[bstg[s]])
        k = wi % 2
        wi += 1
        wvv = wt[k].rearrange("p (k c) -> p k c", k=8)
        P.dma(wvv, w_in[layer, :, C_RET + 1024:C_RET + 1536].rearrange("(k p) c -> p k c", p=128), writes=[bwt[k]])
        for sub in range(HB // 128):
            t0 = hf * HB + sub * 128
            pb = psi % 8
            psi += 1
            for kt in range(8):
                P.op("pe", lambda e, kt=kt, pb=pb, sub=sub, wvv=wvv: e.matmul(
                    ps[pb], hTv[:, kt, sub * 128:(sub + 1) * 128], wvv[:, kt, :],
                    start=(kt == 0), stop=(kt == 7)), reads=[bwt[k], bh], writes=[bps[pb]])
            s = si % 3
            si += 1
            P.op("dve", lambda e, s=s, pb=pb: e.tensor_copy(out=stg[s], in_=ps[pb]), reads=[bps[pb]], writes=[bstg[s]])
            P.dma(X.Vtm[t0:t0 + 128, :], stg[s], reads=[bstg[s]])
    P.barrier()
    A.reset(m0)


OT_RWKV, OT_SSM, OT_RET = 0, 512, 1536


def stage_ret(X, layer):
    nc, P, A = X.nc, X.P, X.A
    T = X.T
    NT = T // 128
    NTB = T // 512
    co = X.co
    m0 = A.mark()
    ps, bps = X.ps, X.bps
    cs_ = load_cst(X, [("ret_mask", 512), ("ret_vf", 4), ("ret_vb", 4), ("ret_gf", 512), ("ret_gb", 512), ("swap", 128)])
    mask, vf, vb, gf, gb, swap = (cs_[k] for k in ("ret_mask", "ret_vf", "ret_vb", "ret_gf", "ret_gb", "swap"))
    ident, ones = X.ident, X.ones
    gl = co["ret_gl"]
    bc = X.bconst
    SB = A.alloc(NT * 512)
    bSB = Buf()
    sfc = [A.alloc(512) for _ in range(2)]
    bsfc = [Buf() for _ in range(2)]
    rope = A.alloc(2 * 512)
    brope = Buf()
    kin = A.alloc(4 * 512)
    bkin = Buf()
    qin = A.alloc(4 * 512)
    bqin = Buf()
    kr = A.alloc(4 * 512)
    bkr = Buf()
    qr = A.alloc(4 * 512)
    bqr = Buf()
    tmp = A.alloc(4 * 512)
    btmp = Buf()
    ktm = A.alloc(512)
    bktm = Buf()
    vt = [A.alloc(512) for _ in range(2)]
    bvt = [Buf() for _ in range(2)]
    vfb = A.alloc(1024)
    bvfb = Buf()
    srun = [A.alloc(512) for _ in range(2)]
    bsrun = [Buf() for _ in range(2)]
    psi = [0]

    def nb():
        b = psi[0] % 8
        psi[0] += 1
        return b

    def rotary(src, bsrc, dst, bdst, row0, t0):
        P.dma(src.rearrange("p (h t) -> p h t", h=4),
              X.PT[row0:row0 + 512, t0:t0 + 512].rearrange("(h p) t -> p h t", p=128), writes=[bsrc])
        P.op("dve", lambda e: e.tensor_tensor(
            out=tmp.rearrange("p (h t) -> p h t", h=4), in0=src.rearrange("p (h t) -> p h t", h=4),
            in1=rope[:, 0:512].unsqueeze(1).broadcast_to([128, 4, 512]), op=ALU.mult),
            reads=[bsrc, brope], writes=[btmp])
        for hp in range(2):
            b0 = nb()
            b1 = nb()
            for hh, b in ((0, b0), (1, b1)):
                h = hp * 2 + hh
                P.op("pe", lambda e, h=h, b=b: e.matmul(ps[b], swap, src[:, h * 512:(h + 1) * 512], start=True, stop=True),
                     reads=[bsrc, bc], writes=[bps[b]])
            for hh, b in ((0, b0), (1, b1)):
                h = hp * 2 + hh
                P.op("dve", lambda e, h=h, b=b: e.tensor_tensor(out=dst[:, h * 512:(h + 1) * 512], in0=ps[b],
                                                               in1=rope[:, 512:1024], op=ALU.mult),
                     reads=[bps[b], brope], writes=[bdst])
        P.op("pool", lambda e: e.tensor_tensor(out=dst, in0=dst, in1=tmp, op=ALU.add), reads=[bdst, btmp], writes=[bdst])

    cur = 0
    P.op("dve", lambda e: e.memset(srun[0], 0.0), writes=[bsrun[0]])
    for tb in range(NTB):
        t0 = tb * 512
        P.dma(rope.rearrange("p (a t) -> p a t", a=2), X.rope[:, :, t0:t0 + 512], writes=[brope])
        rotary(kin, bkin, kr, bkr, C_RET + 512, t0)
        for ci in range(4):
            c = tb * 4 + ci
            j = c % 2
            P.dma(vt[j], X.Vtm[c * 128:(c + 1) * 128, :], writes=[bvt[j]])
            b = nb()
            for h in range(4):
                P.op("pe", lambda e, h=h, b=b, ci=ci: e.matmul(
                    ps[b][:, h * 128:(h + 1) * 128], kr[:, h * 512 + ci * 128: h * 512 + (ci + 1) * 128], ident,
                    start=True, stop=True), reads=[bkr, bc], writes=[bps[b]])
            P.op("act", lambda e, b=b: e.activation(out=ktm, in_=ps[b], func=AF.Copy), reads=[bps[b]], writes=[bktm])
            vfbv = vfb.rearrange("p (h a e) -> p h a e", h=4, a=2)
            P.op("dve", lambda e, j=j, vfbv=vfbv: e.tensor_tensor(
                out=vfbv[:, :, 0, :], in0=vt[j].rearrange("p (h e) -> p h e", h=4),
                in1=vf.unsqueeze(2).broadcast_to([128, 4, 128]), op=ALU.mult), reads=[bvt[j], bc], writes=[bvfb])
            P.op("pool", lambda e, j=j, vfbv=vfbv: e.tensor_tensor(
                out=vfbv[:, :, 1, :], in0=vt[j].rearrange("p (h e) -> p h e", h=4),
                in1=vb.unsqueeze(2).broadcast_to([128, 4, 128]), op=ALU.mult), reads=[bvt[j], bc], writes=[bvfb])
            b0, b1 = nb(), nb()
            for h in range(4):
                b = b0 if h < 2 else b1
                P.op("pe", lambda e, h=h, b=b: e.matmul(ps[b][:, (h % 2) * 256:(h % 2 + 1) * 256], ktm[:, h * 128:(h + 1) * 128],
                                                         vfb[:, h * 256:(h + 1) * 256], start=True, stop=True),
                     reads=[bktm, bvfb], writes=[bps[b]])
            P.dma(X.RSF[c], srun[cur], reads=[bsrun[cur]])
            nxt = 1 - cur
            for h in range(4):
                b = b0 if h < 2 else b1
                P.op("dve", lambda e, h=h, b=b, cur=cur, nxt=nxt: e.scalar_tensor_tensor(
                    out=srun[nxt][:, h * 128:(h + 1) * 128], in0=srun[cur][:, h * 128:(h + 1) * 128], scalar=gl[h],
                    in1=ps[b][:, (h % 2) * 256:(h % 2) * 256 + 128], op0=ALU.mult, op1=ALU.add),
                    reads=[bsrun[cur], bps[b]], writes=[bsrun[nxt]])
                P.op("act", lambda e, h=h, b=b, c=c: e.activation(
                    out=SB[:, c * 512 + h * 128: c * 512 + (h + 1) * 128],
                    in_=ps[b][:, (h % 2) * 256 + 128:(h % 2) * 256 + 256], func=AF.Copy),
                    reads=[bps[b]], writes=[bSB])
            cur = nxt
    if os.environ.get('RET_STOP') == '1':
        P.barrier(); A.reset(m0); return
    P.op("dve", lambda e, cur=cur: e.memset(srun[cur], 0.0), writes=[bsrun[cur]])
    for c in range(NT - 1, -1, -1):
        nxt = 1 - cur
        for h in range(4):
            P.op("dve", lambda e, h=h, c=c, cur=cur, nxt=nxt: e.scalar_tensor_tensor(
                out=srun[nxt][:, h * 128:(h + 1) * 128], in0=srun[cur][:, h * 128:(h + 1) * 128], scalar=gl[h],
                in1=SB[:, c * 512 + h * 128: c * 512 + (h + 1) * 128], op0=ALU.mult, op1=ALU.add),
                reads=[bsrun[cur], bSB], writes=[bsrun[nxt]])
        P.op("pool", lambda e, c=c, cur=cur: e.tensor_copy(out=SB[:, c * 512:(c + 1) * 512], in_=srun[cur]),
             reads=[bsrun[cur], bSB], writes=[bSB])
        cur = nxt
    if os.environ.get('RET_STOP') == '2':
        P.barrier(); A.reset(m0); return
    P.barrier()
    gin = A.alloc(4 * 512)
    bgin = Buf()
    qf = A.alloc(4 * 512)
    bqf = Buf()
    qb = A.alloc(4 * 512)
    bqb = Buf()
    sm = A.alloc(512)
    bsm = Buf()
    sqo = A.alloc(512)
    bsqo = Buf()
    rs = A.alloc(512)
    brs = Buf()
    ot = [A.alloc(512) for _ in range(2)]
    bot = [Buf() for _ in range(2)]
    for tb in range(NTB):
        t0 = tb * 512
        P.dma(rope.rearrange("p (a t) -> p a t", a=2), X.rope[:, :, t0:t0 + 512], writes=[brope])
        rotary(kin, bkin, kr, bkr, C_RET + 512, t0)
        rotary(qin, bqin, qr, bqr, C_RET, t0)
        P.dma(gin.rearrange("p (h t) -> p h t", h=4),
              X.PT[C_RET + 1536:C_RET + 2048, t0:t0 + 512].rearrange("(h p) t -> p h t", p=128), writes=[bgin])
        P.op("act", lambda e: e.activation(out=gin, in_=gin, func=AF.Silu), reads=[bgin], writes=[bgin])
        qr4 = qr.rearrange("p (h c l) -> p h c l", h=4, c=4)
        for ci in range(4):
            P.op("dve", lambda e, ci=ci: e.tensor_tensor(
                out=qf.rearrange("p (h c l) -> p h c l", h=4, c=4)[:, :, ci, :], in0=qr4[:, :, ci, :],
                in1=gf.rearrange("p (h l) -> p h l", h=4), op=ALU.mult), reads=[bqr, bc], writes=[bqf])
            P.op("pool", lambda e, ci=ci: e.tensor_tensor(
                out=qb.rearrange("p (h c l) -> p h c l", h=4, c=4)[:, :, ci, :], in0=qr4[:, :, ci, :],
                in1=gb.rearrange("p (h l) -> p h l", h=4), op=ALU.mult), reads=[bqr, bc], writes=[bqb])
        for ci in range(4):
            c = tb * 4 + ci
            j = c % 2
            P.dma(vt[j], X.Vtm[c * 128:(c + 1) * 128, :], writes=[bvt[j]])
            P.dma(sfc[j], X.RSF[c], writes=[bsfc[j]])
            b = nb()
            for h in range(4):
                sl = slice(h * 512 + ci * 128, h * 512 + (ci + 1) * 128)
                P.op("pe", lambda e, h=h, b=b, sl=sl: e.matmul(ps[b][:, h * 128:(h + 1) * 128], kr[:, sl], qr[:, sl],
                                                               start=True, stop=True), reads=[bkr, bqr], writes=[bps[b]])
            P.op("dve", lambda e, b=b: e.tensor_tensor(out=sm, in0=ps[b], in1=mask, op=ALU.mult),
                 reads=[bps[b], bc], writes=[bsm])
            b = nb()
            for h in range(4):
                sl = slice(h * 512 + ci * 128, h * 512 + (ci + 1) * 128)
                o = ps[b][:, h * 128:(h + 1) * 128]
                P.op("pe", lambda e, h=h, o=o, j=j: e.matmul(o, vt[j][:, h * 128:(h + 1) * 128], sm[:, h * 128:(h + 1) * 128],
                                                             start=True, stop=False), reads=[bvt[j], bsm], writes=[bps[b]])
                P.op("pe", lambda e, h=h, o=o, j=j, sl=sl: e.matmul(o, sfc[j][:, h * 128:(h + 1) * 128], qf[:, sl],
                                                                    start=False, stop=False), reads=[bsfc[j], bqf], writes=[bps[b]])
                P.op("pe", lambda e, h=h, o=o, c=c, sl=sl: e.matmul(o, SB[:, c * 512 + h * 128:c * 512 + (h + 1) * 128], qb[:, sl],
                                                                    start=False, stop=True), reads=[bSB, bqb], writes=[bps[b]])
            P.op("act", lambda e, b=b: e.activation(out=sqo, in_=ps[b], func=AF.Square), reads=[bps[b]], writes=[bsqo])
            b2 = nb()
            P.op("pe", lambda e, b2=b2: e.matmul(ps[b2], ones, sqo, start=True, stop=True), reads=[bsqo, bc], writes=[bps[b2]])
            P.op("dve", lambda e, b2=b2: e.tensor_scalar(out=rs, in0=ps[b2], scalar1=1.0 / 128, scalar2=EPS,
                                                          op0=ALU.mult, op1=ALU.add), reads=[bps[b2]], writes=[brs])
            P.op("act", lambda e: e.activation(out=rs, in_=rs, func=AF.Sqrt), reads=[brs], writes=[brs])
            P.op("dve", lambda e: e.reciprocal(out=rs, in_=rs), reads=[brs], writes=[brs])
            P.op("dve", lambda e, b=b, j=j: e.tensor_tensor(out=ot[j], in0=ps[b], in1=rs, op=ALU.mult),
                 reads=[bps[b], brs], writes=[bot[j]])
            P.op("pool", lambda e, j=j, ci=ci: e.tensor_tensor(
                out=ot[j].rearrange("p (h l) -> p h l", h=4), in0=ot[j].rearrange("p (h l) -> p h l", h=4),
                in1=gin.rearrange("p (h c l) -> p h c l", h=4, c=4)[:, :, ci, :], op=ALU.mult),
                reads=[bot[j], bgin], writes=[bot[j]])
            P.dma(X.OT[OT_RET:OT_RET + 512, c * 128:(c + 1) * 128].rearrange("(h p) t -> p h t", p=128),
                  ot[j].rearrange("p (h l) -> p h l", h=4), reads=[bot[j]])
    P.barrier()
    A.reset(m0)


def stage_ssm(X, layer):
    nc, P, A = X.nc, X.P, X.A
    T = X.T
    NT = T // 128
    co, so = X.co, X.so
    cst, small = X.cst, X.small
    bc = X.bconst
    ps, bps = X.ps, X.bps
    m0 = A.mark()
    psi = [0]

    def nb():
        b = psi[0] % 8
        psi[0] += 1
        return b
    ident, ones = X.ident, X.ones
    cs_ = load_cst(X, [("ssm_selu", 4096), ("ssm_maskf", 512), ("ssm_maskb", 512)])
    selu = cs_["ssm_selu"][0:64, :]
    maskf, maskb = cs_["ssm_maskf"], cs_["ssm_maskb"]
    XS0 = C_SSM + 1024
    m1 = A.mark()
    xp = [A.alloc(T + 4) for _ in range(2)]
    bxp = [Buf() for _ in range(2)]
    acc = [A.alloc(T) for _ in range(2)]
    bacc = [Buf() for _ in range(2)]
    for j in range(2):
        P.op("pool", lambda e, j=j: e.memset(xp[j][:, 0:2], 0.0), writes=[bxp[j]])
        P.op("pool", lambda e, j=j: e.memset(xp[j][:, T + 2:T + 4], 0.0), writes=[bxp[j]])
    for i in range(12):
        j = i % 2
        P.dma(xp[j][:, 2:T + 2], X.PT[XS0 + i * 128:XS0 + (i + 1) * 128, :], writes=[bxp[j]])
        wcol = so["ssm_conv_w"] + (layer * 12 + i) * 5
        bcol = so["ssm_conv_b"] + layer * 12 + i
        P.op("dve", lambda e, j=j, wcol=wcol, bcol=bcol: e.tensor_scalar(
            out=acc[j], in0=xp[j][:, 0:T], scalar1=small[:, wcol:wcol + 1], scalar2=small[:, bcol:bcol + 1],
            op0=ALU.mult, op1=ALU.add), reads=[bxp[j], bc], writes=[bacc[j]])
        for k in range(1, 5):
            P.op("dve", lambda e, j=j, k=k, wcol=wcol: e.scalar_tensor_tensor(
                out=acc[j], in0=xp[j][:, k:k + T], scalar=small[:, wcol + k:wcol + k + 1], in1=acc[j],
                op0=ALU.mult, op1=ALU.add), reads=[bxp[j], bacc[j], bc], writes=[bacc[j]])
        P.op("act", lambda e, j=j: e.activation(out=acc[j], in_=acc[j], func=AF.Silu), reads=[bacc[j]], writes=[bacc[j]])
        P.dma(X.XC[i * 128:(i + 1) * 128, :], acc[j], reads=[bacc[j]])
    P.barrier()
    A.reset(m1)
    q4 = A.alloc(4 * T)
    bq4 = Buf()
    la = A.alloc(T)
    bla = Buf()
    cum = A.alloc(T)
    bcum = Buf()
    rmask = A.alloc(T)
    brm = Buf()
    nA = A.alloc(1)
    bnA = Buf()
    q4v = q4.rearrange("p (q t) -> p q t", q=4)
    dtq = q4v[0:64, 0, :]
    uq = q4v[0:64, 1, :]
    eaq = q4v[0:64, 2, :]
    deq = q4v[0:64, 3, :]
    P.op("pool", lambda e: e.memset(q4[0:64, :], 0.0), writes=[bq4])
    DT0 = C_SSM + 2560
    P.dma(q4v[0:16, 0, :], X.PT[DT0:DT0 + 16, :], writes=[bq4])
    P.dma(q4v[32:48, 0, :], X.PT[DT0 + 16:DT0 + 32, :], writes=[bq4])
    P.op("pool", lambda e: e.memset(rmask[0:64, :], 1.0), writes=[brm])
    P.op("pool", lambda e: e.memset(rmask[0:64, :].rearrange("p (c l) -> p c l", l=128)[:, :, 0:1], 0.0), writes=[brm])
    dbc = so["ssm_dt_bias"] + layer
    alc = so["ssm_a_log"] + layer
    P.op("act", lambda e: e.activation(out=dtq, in_=dtq, func=AF.Exp, bias=small[0:64, dbc:dbc + 1]), reads=[bq4, bc], writes=[bq4])
    P.op("act", lambda e: e.activation(out=dtq, in_=dtq, func=AF.Ln, bias=1.0), reads=[bq4], writes=[bq4])
    P.op("act", lambda e: e.activation(out=nA[0:64, :], in_=small[0:64, alc:alc + 1], func=AF.Exp), reads=[bc], writes=[bnA])
    P.op("dve", lambda e: e.tensor_scalar(out=nA[0:64, :], in0=nA[0:64, :], scalar1=-1.0, scalar2=None, op0=ALU.mult),
         reads=[bnA], writes=[bnA])
    P.op("dve", lambda e: e.tensor_scalar(out=la[0:64, :], in0=dtq, scalar1=nA[0:64, :], scalar2=None, op0=ALU.mult),
         reads=[bq4, bnA], writes=[bla])
    P.op("dve", lambda e: e.tensor_tensor_scan(out=cum[0:64, :], data0=rmask[0:64, :], data1=la[0:64, :], initial=0.0,
                                               op0=ALU.mult, op1=ALU.add), reads=[brm, bla], writes=[bcum])
    cum3 = cum.rearrange("p (c l) -> p c l", l=128)
    atot = cum3[:, :, 127:128].broadcast_to([128, NT, 128])
    P.op("dve", lambda e: e.tensor_copy(out=q4v[0:32, 1, :], in_=cum[0:32, :]), reads=[bcum], writes=[bq4])
    P.op("dve", lambda e: e.tensor_tensor(out=q4v[0:32, 3, :].rearrange("p (c l) -> p c l", l=128), in0=atot[0:32],
                                          in1=cum3[0:32], op=ALU.subtract), reads=[bcum], writes=[bq4])
    P.op("dve", lambda e: e.tensor_tensor(out=q4v[32:64, 3, :], in0=cum[32:64, :], in1=la[32:64, :], op=ALU.subtract),
         reads=[bcum, bla], writes=[bq4])
    P.op("dve", lambda e: e.tensor_tensor(out=q4v[32:64, 1, :].rearrange("p (c l) -> p c l", l=128), in0=atot[32:64],
                                          in1=q4v[32:64, 3, :].rearrange("p (c l) -> p c l", l=128), op=ALU.subtract),
         reads=[bcum, bq4], writes=[bq4])
    P.op("act", lambda e: e.activation(out=eaq, in_=uq, func=AF.Exp), reads=[bq4], writes=[bq4])
    P.op("act", lambda e: e.activation(out=deq, in_=deq, func=AF.Exp), reads=[bq4], writes=[bq4])
    P.dma(X.Q4[:, :, :], q4v[0:64, :, :], reads=[bq4])
    P.barrier()
    A.reset(m1)
    EA = A.alloc(NT * 64)
    CD = A.alloc(NT * 64)
    bEA, bCD = Buf(), Buf()
    xin = [A.alloc(1024) for _ in range(2)]
    bxin = [Buf() for _ in range(2)]
    bcin = [A.alloc(512) for _ in range(2)]
    bbcin = [Buf() for _ in range(2)]
    qin = [A.alloc(512) for _ in range(2)]
    bqin = [Buf() for _ in range(2)]
    xs = A.alloc(1024)
    bxs = Buf()
    btm = A.alloc(256)
    bbtm = Buf()
    tmq = A.alloc(192)
    btmq = Buf()
    gt = A.alloc(256)
    bgt = Buf()
    rhsU = A.alloc(4096)
    brhsU = Buf()
    negu = A.alloc(128)
    bnegu = Buf()
    Eall = A.alloc(4096)
    bE = [Buf() for _ in range(8)]
    xdt = [A.alloc(1024) for _ in range(2)]
    bxdt = [Buf() for _ in range(2)]
    xdd = [A.alloc(1024) for _ in range(2)]
    bxdd = [Buf() for _ in range(2)]
    yp = [A.alloc(1024) for _ in range(2)]
    byp = [Buf() for _ in range(2)]
    hrun = A.alloc(1024)
    bhrun = Buf()
    sbst = [A.alloc(1024) for _ in range(2)]
    bsbst = [Buf() for _ in range(2)]
    dcol = so["ssm_d"] + layer * 16
    P.op("pool", lambda e: e.memset(hrun, 0.0), writes=[bhrun])
    for c in range(NT):
        j = c % 2
        cs = slice(c * 128, (c + 1) * 128)
        P.dma(xin[j].rearrange("p (i t) -> p i t", i=8), X.XC[0:1024, cs].rearrange("(i p) t -> p i t", p=128), writes=[bxin[j]])
        P.dma(bcin[j].rearrange("p (i t) -> p i t", i=4), X.XC[1024:1536, cs].rearrange("(i p) t -> p i t", p=128), writes=[bbcin[j]])
        P.dma(qin[j][0:64, :].rearrange("p (q t) -> p q t", q=4), X.Q4[:, :, cs], writes=[bqin[j]])
        for half in range(2):
            b = nb()
            for ii in range(4):
                i = half * 4 + ii
                P.op("pe", lambda e, b=b, ii=ii, i=i, j=j: e.matmul(ps[b][:, ii * 128:(ii + 1) * 128], xin[j][:, i * 128:(i + 1) * 128],
                                                                   ident, start=True, stop=True), reads=[bxin[j], bc], writes=[bps[b]])
            if half == 0:
                P.op("act", lambda e, b=b: e.activation(out=xs[:, 0:512], in_=ps[b], func=AF.Copy), reads=[bps[b]], writes=[bxs])
            else:
                P.op("dve", lambda e, b=b: e.tensor_copy(out=xs[:, 512:1024], in_=ps[b]), reads=[bps[b]], writes=[bxs])
        b = nb()
        for g in range(2):
            P.op("pe", lambda e, b=b, g=g, j=j: e.matmul(ps[b][:, g * 128:(g + 1) * 128], bcin[j][:, g * 128:(g + 1) * 128], ident,
                                                         start=True, stop=True), reads=[bbcin[j], bc], writes=[bps[b]])
        for qi, q in enumerate((0, 2, 3)):
            P.op("pe", lambda e, b=b, qi=qi, q=q, j=j: e.matmul(ps[b][:, 256 + qi * 64:256 + (qi + 1) * 64],
                                                               qin[j][0:64, q * 128:(q + 1) * 128], ident[0:64, 0:64],
                                                               start=True, stop=True), reads=[bqin[j], bc], writes=[bps[b]])
        P.op("act", lambda e, b=b: e.activation(out=btm, in_=ps[b][:, 0:256], func=AF.Copy), reads=[bps[b]], writes=[bbtm])
        P.op("dve", lambda e, b=b: e.tensor_copy(out=tmq, in_=ps[b][:, 256:448]), reads=[bps[b]], writes=[btmq])
        P.op("pool", lambda e, c=c: e.tensor_copy(out=EA[:, c * 64:(c + 1) * 64], in_=tmq[:, 64:128]), reads=[btmq], writes=[bEA])
        P.op("pool", lambda e, c=c: e.tensor_tensor(out=CD[:, c * 64:(c + 1) * 64], in0=tmq[:, 64:128], in1=tmq[:, 128:192], op=ALU.mult),
             reads=[btmq], writes=[bCD])
        b = nb()
        for g in range(2):
            P.op("pe", lambda e, b=b, g=g, j=j: e.matmul(ps[b][:, g * 128:(g + 1) * 128], bcin[j][:, g * 128:(g + 1) * 128],
                                                         bcin[j][:, 256 + g * 128:256 + (g + 1) * 128], start=True, stop=True),
                 reads=[bbcin[j]], writes=[bps[b]])
        P.op("act", lambda e, b=b: e.activation(out=gt, in_=ps[b][:, 0:256], func=AF.Copy), reads=[bps[b]], writes=[bgt])
        P.op("pool", lambda e, j=j: e.tensor_tensor(
            out=rhsU[0:64, :].rearrange("p (h l) -> p h l", h=32), in0=selu.rearrange("p (h l) -> p h l", h=32),
            in1=qin[j][0:64, 128:256].unsqueeze(1).broadcast_to([64, 32, 128]), op=ALU.mult),
            reads=[bqin[j], bc], writes=[brhsU])
        P.op("dve", lambda e, j=j: e.tensor_scalar(out=negu[0:64, :], in0=qin[j][0:64, 128:256], scalar1=-1.0, scalar2=None, op0=ALU.mult),
             reads=[bqin[j]], writes=[bnegu])
        for k in range(8):
            b = nb()
            sl = slice(k * 512, (k + 1) * 512)
            P.op("pe", lambda e, b=b, sl=sl: e.matmul(ps[b], ones[0:64, :], rhsU[0:64, sl], start=True, stop=False),
                 reads=[brhsU, bc], writes=[bps[b]])
            P.op("pe", lambda e, b=b, sl=sl: e.matmul(ps[b], negu[0:64, :], selu[:, sl], start=False, stop=False),
                 reads=[bnegu, bc], writes=[bps[b]])
            mk = maskf if k < 4 else maskb
            P.op("pe", lambda e, b=b, mk=mk: e.matmul(ps[b], ident, mk, start=False, stop=True), reads=[bc], writes=[bps[b]])
            P.op("act", lambda e, b=b, sl=sl: e.activation(out=Eall[:, sl], in_=ps[b], func=AF.Exp), reads=[bps[b]], writes=[bE[k]])
            g = (k % 4) // 2
            eng = "dve" if k % 2 == 0 else "pool"
            P.op(eng, lambda e, sl=sl, g=g: e.tensor_tensor(
                out=Eall[:, sl].rearrange("p (h l) -> p h l", h=4), in0=Eall[:, sl].rearrange("p (h l) -> p h l", h=4),
                in1=gt[:, g * 128:(g + 1) * 128].unsqueeze(1).broadcast_to([128, 4, 128]), op=ALU.mult),
                reads=[bE[k], bgt], writes=[bE[k]])
        for dr in range(2):
            eng = "dve" if dr == 0 else "pool"
            P.op(eng, lambda e, dr=dr: e.tensor_tensor(
                out=xdt[dr].rearrange("p (h q) -> p h q", h=16), in0=xs.rearrange("p (h q) -> p h q", h=16),
                in1=tmq[:, dr * 32:dr * 32 + 16].unsqueeze(2).broadcast_to([128, 16, 64]), op=ALU.mult),
                reads=[bxs, btmq], writes=[bxdt[dr]])
            P.op(eng, lambda e, dr=dr: e.tensor_tensor(
                out=xdd[dr].rearrange("p (h q) -> p h q", h=16), in0=xdt[dr].rearrange("p (h q) -> p h q", h=16),
                in1=tmq[:, 128 + dr * 32:128 + dr * 32 + 16].unsqueeze(2).broadcast_to([128, 16, 64]), op=ALU.mult),
                reads=[bxdt[dr], btmq], writes=[bxdd[dr]])
        P.op("pool", lambda e, j=j: e.tensor_tensor(
            out=yp[j].rearrange("p (h q) -> p h q", h=16), in0=xs.rearrange("p (h q) -> p h q", h=16),
            in1=small[:, dcol:dcol + 16].unsqueeze(2).broadcast_to([128, 16, 64]), op=ALU.mult),
            reads=[bxs, bc], writes=[byp[j]])
        for g in range(2):
            b = nb()
            for e8 in range(8):
                h = g * 8 + e8
                for dr in range(2):
                    hd = dr * 16 + h
                    P.op("pe", lambda e, b=b, e8=e8, h=h, hd=hd, dr=dr: e.matmul(
                        ps[b][:, e8 * 64:(e8 + 1) * 64], Eall[:, hd * 128:(hd + 1) * 128], xdt[dr][:, h * 64:(h + 1) * 64],
                        start=(dr == 0), stop=(dr == 1)), reads=[bE[hd // 4], bxdt[dr]], writes=[bps[b]])
            P.op("dve", lambda e, b=b, g=g, j=j: e.tensor_tensor(out=yp[j][:, g * 512:(g + 1) * 512], in0=ps[b],
                                                                 in1=yp[j][:, g * 512:(g + 1) * 512], op=ALU.add),
                 reads=[bps[b], byp[j]], writes=[byp[j]])
        P.dma(X.YP[cs, :], yp[j], reads=[byp[j]])
        P.dma(X.HINF[c], hrun, reads=[bhrun])
        for g in range(2):
            b = nb()
            P.op("pe", lambda e, b=b, g=g: e.matmul(ps[b], btm[:, g * 128:(g + 1) * 128], xdd[0][:, g * 512:(g + 1) * 512],
                                                    start=True, stop=True), reads=[bbtm, bxdd[0]], writes=[bps[b]])
            P.op("dve", lambda e, g=g, c=c: e.tensor_tensor(
                out=hrun[:, g * 512:(g + 1) * 512].rearrange("p (h q) -> p h q", h=8),
                in0=hrun[:, g * 512:(g + 1) * 512].rearrange("p (h q) -> p h q", h=8),
                in1=CD[:, c * 64 + g * 8:c * 64 + g * 8 + 8].unsqueeze(2).broadcast_to([128, 8, 64]), op=ALU.mult),
                reads=[bhrun, bCD], writes=[bhrun])
            P.op("dve", lambda e, b=b, g=g: e.tensor_tensor(out=hrun[:, g * 512:(g + 1) * 512], in0=ps[b],
                                                            in1=hrun[:, g * 512:(g + 1) * 512], op=ALU.add),
                 reads=[bps[b], bhrun], writes=[bhrun])
            b = nb()
            P.op("pe", lambda e, b=b, g=g: e.matmul(ps[b], btm[:, g * 128:(g + 1) * 128], xdd[1][:, g * 512:(g + 1) * 512],
                                                    start=True, stop=True), reads=[bbtm, bxdd[1]], writes=[bps[b]])
            P.op("act", lambda e, b=b, g=g, j=j: e.activation(out=sbst[j][:, g * 512:(g + 1) * 512], in_=ps[b], func=AF.Copy),
                 reads=[bps[b]], writes=[bsbst[j]])
        P.dma(X.HINB[c], sbst[j], reads=[bsbst[j]])
    P.barrier()
    P.op("pool", lambda e: e.memset(hrun, 0.0), reads=[bhrun], writes=[bhrun])
    for c in range(NT - 1, -1, -1):
        j = c % 2
        P.dma(sbst[j], X.HINB[c], writes=[bsbst[j]])
        P.dma(X.HINB[c], hrun, reads=[bhrun, bsbst[j]])
        for g in range(2):
            P.op("dve", lambda e, g=g, c=c: e.tensor_tensor(
                out=hrun[:, g * 512:(g + 1) * 512].rearrange("p (h q) -> p h q", h=8),
                in0=hrun[:, g * 512:(g + 1) * 512].rearrange("p (h q) -> p h q", h=8),
                in1=CD[:, c * 64 + 32 + g * 8:c * 64 + 32 + g * 8 + 8].unsqueeze(2).broadcast_to([128, 8, 64]), op=ALU.mult),
                reads=[bhrun, bCD], writes=[bhrun])
        P.op("pool", lambda e, j=j: e.tensor_tensor(out=hrun, in0=hrun, in1=sbst[j], op=ALU.add), reads=[bhrun, bsbst[j]], writes=[bhrun])
    P.barrier()
    zin = [A.alloc(1024) for _ in range(2)]
    bzin = [Buf() for _ in range(2)]
    hf = [A.alloc(1024) for _ in range(2)]
    bhf = [Buf() for _ in range(2)]
    hb = [A.alloc(1024) for _ in range(2)]
    bhb = [Buf() for _ in range(2)]
    tmp = A.alloc(512)
    btmp = Buf()
    yz = A.alloc(1024)
    byz = Buf()
    sq = A.alloc(1024)
    bsq = Buf()
    rs = A.alloc(256)
    brs = Buf()
    nwc = so["ssm_norm_w"] + layer * 8
    for c in range(NT):
        j = c % 2
        cs = slice(c * 128, (c + 1) * 128)
        P.dma(yp[j], X.YP[cs, :], writes=[byp[j]])
        P.dma(bcin[j].rearrange("p (i t) -> p i t", i=4), X.XC[1024:1536, cs].rearrange("(i p) t -> p i t", p=128), writes=[bbcin[j]])
        P.dma(hf[j], X.HINF[c], writes=[bhf[j]])
        P.dma(hb[j], X.HINB[c], writes=[bhb[j]])
        P.dma(zin[j].rearrange("p (i t) -> p i t", i=8), X.PT[C_SSM:C_SSM + 1024, cs].rearrange("(i p) t -> p i t", p=128), writes=[bzin[j]])
        P.op("act", lambda e, j=j: e.activation(out=zin[j], in_=zin[j], func=AF.Silu), reads=[bzin[j]], writes=[bzin[j]])
        for dr in range(2):
            hsrc, bh_ = (hf[j], bhf[j]) if dr == 0 else (hb[j], bhb[j])
            for g in range(2):
                b = nb()
                P.op("pe", lambda e, b=b, g=g, j=j, hsrc=hsrc: e.matmul(ps[b], bcin[j][:, 256 + g * 128:256 + (g + 1) * 128],
                                                                        hsrc[:, g * 512:(g + 1) * 512], start=True, stop=True),
                     reads=[bbcin[j], bh_], writes=[bps[b]])
                P.op("dve", lambda e, b=b, g=g, dr=dr, c=c: e.tensor_tensor(
                    out=tmp.rearrange("p (h q) -> p h q", h=8), in0=ps[b].rearrange("p (h q) -> p h q", h=8),
                    in1=EA[:, c * 64 + dr * 32 + g * 8:c * 64 + dr * 32 + g * 8 + 8].unsqueeze(2).broadcast_to([128, 8, 64]),
                    op=ALU.mult), reads=[bps[b], bEA], writes=[btmp])
                P.op("pool", lambda e, g=g, j=j: e.tensor_tensor(out=yp[j][:, g * 512:(g + 1) * 512], in0=yp[j][:, g * 512:(g + 1) * 512],
                                                                  in1=tmp, op=ALU.add), reads=[btmp, byp[j]], writes=[byp[j]])
        for half in range(2):
            b = nb()
            for ii in range(4):
                i = half * 4 + ii
                P.op("pe", lambda e, b=b, ii=ii, i=i, j=j: e.matmul(ps[b][:, ii * 128:(ii + 1) * 128], yp[j][:, i * 128:(i + 1) * 128],
                                                                   ident, start=True, stop=True), reads=[byp[j], bc], writes=[bps[b]])
            P.op("dve", lambda e, b=b, half=half, j=j: e.tensor_tensor(out=yz[:, half * 512:(half + 1) * 512], in0=ps[b],
                                                                       in1=zin[j][:, half * 512:(half + 1) * 512], op=ALU.mult),
                 reads=[bps[b], bzin[j]], writes=[byz])
        P.op("act", lambda e: e.activation(out=sq, in_=yz, func=AF.Square), reads=[byz], writes=[bsq])
        b = nb()
        for g in range(2):
            for ii in range(4):
                i = g * 4 + ii
                P.op("pe", lambda e, b=b, g=g, ii=ii, i=i: e.matmul(ps[b][:, g * 128:(g + 1) * 128], ones, sq[:, i * 128:(i + 1) * 128],
                                                                   start=(ii == 0), stop=(ii == 3)), reads=[bsq, bc], writes=[bps[b]])
        P.op("dve", lambda e, b=b: e.tensor_scalar(out=rs, in0=ps[b][:, 0:256], scalar1=1.0 / 512, scalar2=EPS,
                                                    op0=ALU.mult, op1=ALU.add), reads=[bps[b]], writes=[brs])
        P.op("act", lambda e: e.activation(out=rs, in_=rs, func=AF.Sqrt), reads=[brs], writes=[brs])
        P.op("dve", lambda e: e.reciprocal(out=rs, in_=rs), reads=[brs], writes=[brs])
        for g in range(2):
            P.op("dve", lambda e, g=g: e.tensor_tensor(
                out=yz[:, g * 512:(g + 1) * 512].rearrange("p (i t) -> p i t", i=4),
                in0=yz[:, g * 512:(g + 1) * 512].rearrange("p (i t) -> p i t", i=4),
                in1=rs[:, g * 128:(g + 1) * 128].unsqueeze(1).broadcast_to([128, 4, 128]), op=ALU.mult),
                reads=[byz, brs], writes=[byz])
        P.op("pool", lambda e: e.tensor_tensor(
            out=yz.rearrange("p (i t) -> p i t", i=8), in0=yz.rearrange("p (i t) -> p i t", i=8),
            in1=small[:, nwc:nwc + 8].unsqueeze(2).broadcast_to([128, 8, 128]), op=ALU.mult), reads=[byz, bc], writes=[byz])
        P.dma(X.OT[OT_SSM:OT_SSM + 1024, cs].rearrange("(i p) t -> p i t", p=128), yz.rearrange("p (i t) -> p i t", i=8), reads=[byz])
    P.barrier()
    A.reset(m0)


NEG_EXP_HALF = -math.exp(-0.5)
GN_EPS = 64e-5


def stage_rwkv(X, layer):
    nc, P, A = X.nc, X.P, X.A
    T = X.T
    NT = T // 128
    TB = 512
    NB = T // TB
    CB = TB // 128
    co, so = X.co, X.so
    cst, small = X.cst, X.small
    bc = X.bconst
    ps, bps = X.ps, X.bps
    m0 = A.mark()
    psi = [0]

    def nb():
        b = psi[0] % 8
        psi[0] += 1
        return b

    def nb2():
        if psi[0] % 2 == 1:
            psi[0] += 1
        b = psi[0] % 8
        psi[0] += 2
        return b
    ident = X.ident
    cs_ = load_cst(X, [("rw_blk", 128), ("rw_halfsel", 2), ("rw_low4", 512), ("rw_up4", 512), ("rw_upupi2", 512), ("rw_lowlowi2", 512),
                       ("rw_bd_low4", 512), ("rw_bd_up4", 512), ("rw_l32_low4", 512), ("rw_l32_up4", 512), ("rw_l64_low4", 512), ("rw_l64_up4", 512)])
    blk, halfsel, low4, up4, upupi2, lowlowi2 = (cs_[k] for k in ("rw_blk", "rw_halfsel", "rw_low4", "rw_up4", "rw_upupi2", "rw_lowlowi2"))

    def scol(name, idx):
        o = so[name] + idx
        return small[:, o:o + 1]
    m1 = A.mark()
    wup = A.alloc(512)
    aup = A.alloc(512)
    gup = A.alloc(512)
    bw = Buf()
    P.dma(wup[0:64, :], X.rw_w_up[layer].rearrange("d k c -> (d k) c"), writes=[bw])
    P.dma(aup[0:64, :], X.rw_a_up[layer].rearrange("d k c -> (d k) c"), writes=[bw])
    P.dma(gup[0:64, :], X.rw_g_up[layer], writes=[bw])
    P.barrier()
    hmu = A.alloc(16)
    omu = A.alloc(16)
    bmu = Buf()
    muo = so["rw_mu"] + layer * 16
    P.op("dve", lambda e: e.tensor_scalar(out=hmu, in0=small[:, muo:muo + 16], scalar1=0.5, scalar2=None, op0=ALU.mult),
         reads=[bc], writes=[bmu])
    P.op("dve", lambda e: e.tensor_scalar(out=omu, in0=small[:, muo:muo + 16], scalar1=-1.0, scalar2=1.0, op0=ALU.mult, op1=ALU.add),
         reads=[bc], writes=[bmu])
    oka = A.alloc(4)
    hrk = A.alloc(4)
    kao = so["rw_ka"] + layer * 4
    rko = so["rw_rk"] + layer * 4
    P.op("dve", lambda e: e.tensor_scalar(out=oka, in0=small[:, kao:kao + 4], scalar1=-1.0, scalar2=1.0, op0=ALU.mult, op1=ALU.add),
         reads=[bc], writes=[bmu])
    P.op("dve", lambda e: e.tensor_scalar(out=hrk, in0=small[:, rko:rko + 4], scalar1=0.5, scalar2=None, op0=ALU.mult),
         reads=[bc], writes=[bmu])
    rmask = A.alloc(TB)
    brm = Buf()
    P.op("pool", lambda e: e.memset(rmask, 1.0), writes=[brm])
    P.op("pool", lambda e: e.memset(rmask.rearrange("p (c l) -> p c l", l=128)[:, :, 0:1], 0.0), writes=[brm])
    xp = [A.alloc(TB + 2) for _ in range(2)]
    bxp = [Buf() for _ in range(2)]
    xpi = [0]
    t1 = A.alloc(TB)
    bt1 = Buf()

    def shifted(row0, nrows, mucol, t0, out, bout, func=None):
        j = xpi[0] % 2
        xpi[0] += 1
        lo = t0 - 1
        hi = t0 + TB + 1
        dlo, dhi = 0, TB + 2
        if lo < 0:
            P.op("pool", lambda e, j=j: e.memset(xp[j][0:nrows, 0:1], 0.0), writes=[bxp[j]])
            lo, dlo = 0, 1
        if hi > T:
            P.op("pool", lambda e, j=j: e.memset(xp[j][0:nrows, TB + 1:TB + 2], 0.0), writes=[bxp[j]])
            hi, dhi = T, TB + 1
        P.dma(xp[j][0:nrows, dlo:dhi], X.PT[row0:row0 + nrows, lo:hi], writes=[bxp[j]])
        P.op("pool", lambda e, j=j: e.tensor_tensor(out=t1[0:nrows, :], in0=xp[j][0:nrows, 0:TB], in1=xp[j][0:nrows, 2:TB + 2], op=ALU.add),
             reads=[bxp[j]], writes=[bt1])
        P.op("dve", lambda e: e.tensor_scalar(out=t1[0:nrows, :], in0=t1[0:nrows, :], scalar1=hmu[0:nrows, mucol:mucol + 1], scalar2=None,
                                              op0=ALU.mult), reads=[bt1, bmu], writes=[bt1])
        P.op("dve", lambda e, j=j: e.scalar_tensor_tensor(out=out[0:nrows, :], in0=xp[j][0:nrows, 1:TB + 1],
                                                          scalar=omu[0:nrows, mucol:mucol + 1], in1=t1[0:nrows, :],
                                                          op0=ALU.mult, op1=ALU.add), reads=[bxp[j], bt1, bmu], writes=[bout])
        if func is not None:
            P.op("act", lambda e: e.activation(out=out[0:nrows, :], in_=out[0:nrows, :], func=func), reads=[bout], writes=[bout])

    def tile(n=TB):
        return A.alloc(n), Buf()
    tw, btw = tile()
    adt, badt = tile()
    sg, bsg = tile()
    ur, bur = tile()
    uk, buk = tile()
    uv, buv = tile()
    gti, bgti = tile()
    kk, bkk = tile()
    sq, bsq = tile()
    sgm, bsgm = tile()
    ad, bad = tile()
    cum, bcum = tile()
    et, bet = tile()
    ct, bct = tile()
    E1, bE1 = tile()
    E2, bE2 = tile()
    E3, bE3 = tile()
    be_, bbe = tile()
    kd, bkd = tile()
    kbs, bkbs = tile()
    glt, bglt = tile(CB)
    R0 = C_RWKV
    for blkk in range(NB):
        t0 = blkk * TB
        shifted(R0 + 1536, 64, 12, t0, tw, btw, AF.Tanh)
        shifted(R0 + 1600, 64, 13, t0, adt, badt, None)
        shifted(R0 + 1664, 64, 14, t0, sg, bsg, AF.Sigmoid)
        for p in range(4):
            pc = slice(p * 128, (p + 1) * 128)
            shifted(R0 + p * 128, 128, p, t0, ur, bur)
            shifted(R0 + 512 + p * 128, 128, 4 + p, t0, uk, buk)
            shifted(R0 + 1024 + p * 128, 128, 8 + p, t0, uv, buv)
            P.dma(X.RVT[pc, t0:t0 + TB], uv, reads=[buv])
            b = nb()
            P.op("pe", lambda e, b=b, pc=pc: e.matmul(ps[b], gup[0:64, pc], sg[0:64, :], start=True, stop=True), reads=[bw, bsg], writes=[bps[b]])
            P.op("act", lambda e, b=b: e.activation(out=gti, in_=ps[b], func=AF.Copy), reads=[bps[b]], writes=[bgti])
            P.dma(X.RGT[pc, t0:t0 + TB], gti, reads=[bgti])
            P.op("dve", lambda e, p=p: e.tensor_scalar(out=kk, in0=uk, scalar1=scol("rw_kk", layer * 4 + p), scalar2=None, op0=ALU.mult),
                 reads=[buk, bc], writes=[bkk])
            P.op("act", lambda e: e.activation(out=sq, in_=kk, func=AF.Square), reads=[bkk], writes=[bsq])
            b = nb()
            P.op("pe", lambda e, b=b: e.matmul(ps[b], blk, sq, start=True, stop=True), reads=[bsq, bc], writes=[bps[b]])
            P.op("act", lambda e, b=b: e.activation(out=sq, in_=ps[b], func=AF.Sqrt), reads=[bps[b]], writes=[bsq])
            P.op("dve", lambda e: e.tensor_scalar(out=sq, in0=sq, scalar1=1e-12, scalar2=None, op0=ALU.max), reads=[bsq], writes=[bsq])
            P.op("dve", lambda e: e.reciprocal(out=sq, in_=sq), reads=[bsq], writes=[bsq])
            P.op("dve", lambda e: e.tensor_tensor(out=kk, in0=kk, in1=sq, op=ALU.mult), reads=[bkk, bsq], writes=[bkk])
            for d in range(2):
                ds_ = slice(d * 32, (d + 1) * 32)
                b = nb()
                P.op("pe", lambda e, b=b, pc=pc, ds_=ds_: e.matmul(ps[b], wup[ds_, pc], tw[ds_, :], start=True, stop=True),
                     reads=[bw, btw], writes=[bps[b]])
                P.op("act", lambda e, b=b, d=d, p=p: e.activation(out=sgm, in_=ps[b], func=AF.Sigmoid,
                                                                 bias=scol("rw_w0", (layer * 2 + d) * 4 + p)),
                     reads=[bps[b], bc], writes=[bsgm])
                b = nb()
                P.op("pe", lambda e, b=b, pc=pc, ds_=ds_: e.matmul(ps[b], aup[ds_, pc], adt[ds_, :], start=True, stop=True),
                     reads=[bw, badt], writes=[bps[b]])
                P.op("act", lambda e, b=b, d=d, p=p: e.activation(out=ad, in_=ps[b], func=AF.Sigmoid,
                                                                 bias=scol("rw_a0", (layer * 2 + d) * 4 + p)),
                     reads=[bps[b], bc], writes=[bad])
                P.op("dve", lambda e: e.tensor_scalar(out=sgm, in0=sgm, scalar1=NEG_EXP_HALF, scalar2=None, op0=ALU.mult),
                     reads=[bsgm], writes=[bsgm])
                P.op("dve", lambda e: e.tensor_tensor_scan(out=cum, data0=rmask, data1=sgm, initial=0.0, op0=ALU.mult, op1=ALU.add),
                     reads=[brm, bsgm], writes=[bcum])
                cum3 = cum.rearrange("p (c l) -> p c l", l=128)
                ltot = cum3[:, :, 127:128]
                P.op("act", lambda e, ltot=ltot: e.activation(out=glt.unsqueeze(2), in_=ltot, func=AF.Exp), reads=[bcum], writes=[bglt])
                P.dma(X.RGL[d, pc, blkk * CB:(blkk + 1) * CB], glt, reads=[bglt])
                if d == 0:
                    P.op("pool", lambda e: e.tensor_tensor(out=et, in0=cum, in1=sgm, op=ALU.subtract), reads=[bcum, bsgm], writes=[bet])
                    cc, bcc = cum, bcum
                else:
                    P.op("pool", lambda e, ltot=ltot, cum3=cum3: e.tensor_tensor(
                        out=et.rearrange("p (c l) -> p c l", l=128), in0=ltot.broadcast_to([128, CB, 128]), in1=cum3, op=ALU.subtract),
                        reads=[bcum], writes=[bet])
                    P.op("pool", lambda e: e.tensor_tensor(out=ct, in0=et, in1=sgm, op=ALU.add), reads=[bet, bsgm], writes=[bct])
                    cc, bcc = ct, bct
                P.op("act", lambda e: e.activation(out=E1, in_=et, func=AF.Exp), reads=[bet], writes=[bE1])
                P.op("act", lambda e, cc=cc: e.activation(out=E2, in_=cc, func=AF.Exp, scale=-1.0), reads=[bcc], writes=[bE2])
                P.op("act", lambda e, cc=cc: e.activation(out=E3, in_=cc, func=AF.Exp), reads=[bcc], writes=[bE3])
                P.op("dve", lambda e: e.scalar_tensor_tensor(out=E1, in0=kk, scalar=-1.0, in1=E1, op0=ALU.mult, op1=ALU.mult),
                     reads=[bkk, bE1], writes=[bE1])
                P.op("pool", lambda e: e.tensor_tensor(out=E3, in0=ur, in1=E3, op=ALU.mult), reads=[bur, bE3], writes=[bE3])
                P.op("dve", lambda e: e.tensor_tensor(out=be_, in0=kk, in1=ad, op=ALU.mult), reads=[bkk, bad], writes=[bbe])
                P.op("pool", lambda e: e.tensor_tensor(out=be_, in0=be_, in1=E2, op=ALU.mult), reads=[bbe, bE2], writes=[bbe])
                P.op("dve", lambda e, p=p: e.tensor_scalar(out=ad, in0=ad, scalar1=scol("rw_ka", layer * 4 + p), scalar2=oka[:, p:p + 1],
                                                           op0=ALU.mult, op1=ALU.add), reads=[bad, bc, bmu], writes=[bad])
                P.op("dve", lambda e: e.tensor_tensor(out=kd, in0=uk, in1=ad, op=ALU.mult), reads=[buk, bad], writes=[bkd])
                if d == 0:
                    P.op("pool", lambda e: e.tensor_copy(out=kbs, in_=kd), reads=[bkd], writes=[bkbs])
                else:
                    P.op("pool", lambda e: e.tensor_tensor(out=kbs, in0=kbs, in1=kd, op=ALU.add), reads=[bkd, bkbs], writes=[bkbs])
                P.op("dve", lambda e: e.tensor_tensor(out=kd, in0=kd, in1=E2, op=ALU.mult), reads=[bkd, bE2], writes=[bkd])
                for q, (src, bsrc) in enumerate(((E1, bE1), (E3, bE3), (be_, bbe), (kd, bkd))):
                    P.dma(X.RWQ[d, pc, blkk * CB:(blkk + 1) * CB, q, :], src.rearrange("p (c l) -> p c l", l=128), reads=[bsrc])
            P.op("dve", lambda e, p=p: e.scalar_tensor_tensor(out=kbs, in0=kbs, scalar=hrk[:, p:p + 1], in1=ur, op0=ALU.mult, op1=ALU.mult),
                 reads=[bkbs, bur, bmu], writes=[bkbs])
            P.dma(X.RKB[pc, t0:t0 + TB], kbs, reads=[bkbs])
    P.barrier()
    A.reset(m1)
    if os.environ.get("RW_STOP") == "A":
        A.reset(m0)
        return
    GL = A.alloc(4 * NT)
    bGL = Buf()
    qt = [A.alloc(2048) for _ in range(2)]
    bqt = [Buf() for _ in range(2)]
    vin = [A.alloc(512) for _ in range(2)]
    bvin = [Buf() for _ in range(2)]
    OFFS["Pb"] = A.off
    Pb = A.alloc(1024)
    bPb = [Buf() for _ in range(2)]
    OFFS["QT"] = A.off
    QT = A.alloc(2048)
    bQ = [Buf() for _ in range(2)]
    bT = [Buf() for _ in range(2)]
    ARB = A.alloc(1024)
    bARB = Buf()
    Qf = A.alloc(1024)
    bQf = [Buf() for _ in range(2)]
    PD = A.alloc(2048)
    bPk = [Buf() for _ in range(2)]
    bD = [Buf() for _ in range(2)]
    Ws = [A.alloc(1024) for _ in range(2)]
    bWs = [[Buf() for _ in range(2)] for _ in range(2)]
    Ls = [A.alloc(1024) for _ in range(2)]
    bLs = [[Buf() for _ in range(2)] for _ in range(2)]
    AK = A.alloc(2048)
    bAK = Buf()
    vtm = A.alloc(512)
    bvtm = Buf()
    btmk = A.alloc(1024)
    bbtmk = Buf()
    Xs = A.alloc(512)
    bXs = Buf()
    Us = A.alloc(512)
    bUs = Buf()
    Hst = [A.alloc(512) for _ in range(2)]
    qm = A.alloc(2048)
    bqm = Buf()
    bH = [Buf() for _ in range(2)]
    Ysb = [A.alloc(512) for _ in range(2)]
    bY = [Buf() for _ in range(2)]
    y0 = [A.alloc(512) for _ in range(2)]
    by0 = [Buf() for _ in range(2)]
    Pv = Pb.rearrange("p (h l) -> p h l", h=8)
    QTv = QT.rearrange("p (h a l) -> p h a l", h=8, a=2)
    AKv = AK.rearrange("p (h a l) -> p h a l", h=8, a=2)
    ARBv = ARB.rearrange("p (h l) -> p h l", h=8)
    Qfv = Qf.rearrange("p (h l) -> p h l", h=8)
    PDv = PD.rearrange("p (h a l) -> p h a l", h=8, a=2)
    lnwb = A.alloc(1024)
    blnwb = Buf()
    P.dma(lnwb.rearrange("p (a c) -> p a c", a=2), X.rw_lnwb[layer:layer + 1].broadcast_to([128, 2, 512]), writes=[blnwb])
    rkbin = [A.alloc(512) for _ in range(2)]
    brkbin = [Buf() for _ in range(2)]
    gin = [A.alloc(512) for _ in range(2)]
    bgin = [Buf() for _ in range(2)]
    st8 = A.alloc(64)
    bst8 = Buf()
    ysq = A.alloc(512)
    bysq = Buf()
    oT = A.alloc(512)
    boT = Buf()
    for d in range(2):
        maskP = low4 if d == 0 else up4
        maskQ2 = upupi2 if d == 0 else lowlowi2
        P.dma(GL.rearrange("p (a c) -> p a c", a=4), X.RGL[d].rearrange("(a p) c -> p a c", p=128), writes=[bGL])
        cur = 0
        P.op("pool", lambda e: e.memset(Hst[0], 0.0), writes=[bH[0]])
        P.op("pool", lambda e: e.memset(Hst[1], 0.0), writes=[bH[1]])
        order = range(NT) if d == 0 else range(NT - 1, -1, -1)
        for ci_, c in enumerate(order):
            if d * NT + ci_ >= int(os.environ.get("RW_NCH", "1000")):
                break
            j = c % 2
            cs = slice(c * 128, (c + 1) * 128)
            P.dma(qt[j].rearrange("p (a q l) -> p a q l", a=4, q=4), X.RWQ[d, :, c, :, :].rearrange("(a p) q l -> p a q l", p=128), writes=[bqt[j]])
            P.dma(vin[j].rearrange("p (a l) -> p a l", a=4), X.RVT[:, cs].rearrange("(a p) l -> p a l", p=128), writes=[bvin[j]])
            q4 = qt[j].rearrange("p (a q l) -> p a q l", a=4, q=4)

            def hq(h, q0, q1=None, q4=q4):
                a, half = h // 2, h % 2
                rows = slice(half * 64, half * 64 + 64)
                if q1 is None:
                    return q4[rows, a, q0, :]
                return q4[rows, a, q0:q1, :].rearrange("p q l -> p (q l)")
            qmv = qm.rearrange("p (x a q l) -> p x a q l", x=2, a=4, q=2)
            for half in range(2):
                P.op("dve" if half == 0 else "pool", lambda e, half=half, q4=q4, qmv=qmv: e.tensor_scalar(
                    out=qmv[:, half], in0=q4[:, :, 2:4, :], scalar1=halfsel[:, half:half + 1], scalar2=None, op0=ALU.mult),
                    reads=[bqt[j], bc], writes=[bqm])
            b = nb()
            for a in range(4):
                P.op("pe", lambda e, b=b, a=a, j=j: e.matmul(ps[b][:, a * 128:(a + 1) * 128], vin[j][:, a * 128:(a + 1) * 128], ident,
                                                             start=True, stop=True), reads=[bvin[j], bc], writes=[bps[b]])
            P.op("act", lambda e, b=b: e.activation(out=vtm, in_=ps[b], func=AF.Copy), reads=[bps[b]], writes=[bvtm])
            for qi, q in enumerate((2, 3)):
                b = nb()
                for a in range(4):
                    P.op("pe", lambda e, b=b, a=a, q=q, q4=q4: e.matmul(ps[b][:, a * 128:(a + 1) * 128], q4[:, a, q, :], ident,
                                                                       start=True, stop=True), reads=[bqt[j], bc], writes=[bps[b]])
                eng = "dve" if qi == 0 else "act"
                if eng == "dve":
                    P.op("dve", lambda e, b=b, qi=qi: e.tensor_copy(out=btmk[:, qi * 512:(qi + 1) * 512], in_=ps[b]), reads=[bps[b]], writes=[bbtmk])
                else:
                    P.op("act", lambda e, b=b, qi=qi: e.activation(out=btmk[:, qi * 512:(qi + 1) * 512], in_=ps[b], func=AF.Copy),
                         reads=[bps[b]], writes=[bbtmk])
            for gi in range(2):
                hs = range(gi * 4, gi * 4 + 4)
                b = nb()
                for h in hs:
                    P.op("pe", lambda e, b=b, h=h, q4=q4, qmv=qmv: e.matmul(ps[b][:, (h % 4) * 128:(h % 4 + 1) * 128], q4[:, h // 2, 0, :],
                                                                            qmv[:, h % 2, h // 2, 0, :], start=True, stop=True),
                         reads=[bqt[j], bqm], writes=[bps[b]])
                P.op("dve", lambda e, b=b, gi=gi, maskP=maskP: e.tensor_tensor(out=Pb[:, gi * 512:(gi + 1) * 512], in0=ps[b], in1=maskP, op=ALU.mult),
                     reads=[bps[b], bc], writes=[bPb[gi]])
                for hp in range(2):
                    h0 = gi * 4 + hp * 2
                    b = nb()
                    for hh in range(2):
                        h = h0 + hh
                        P.op("pe", lambda e, b=b, h=h, hh=hh, q4=q4, qmv=qmv: e.matmul(
                            ps[b][:, hh * 256:(hh + 1) * 256], qmv[:, h % 2, h // 2, 0, :], q4[:, h // 2, 0:2, :].rearrange("p q l -> p (q l)"),
                            start=True, stop=True), reads=[bqt[j], bqm], writes=[bps[b]])
                    pv = ps[b].rearrange("p (h a l) -> p h a l", h=2, a=2)
                    mv = maskQ2.rearrange("p (h a l) -> p h a l", h=2, a=2)
                    P.op("dve", lambda e, pv=pv, mv=mv, h0=h0: e.tensor_tensor(out=Qfv[:, h0:h0 + 2, :], in0=pv[:, :, 0, :], in1=mv[:, :, 0, :],
                                                                               op=ALU.mult), reads=[bps[b], bc], writes=[bQf[gi]])
                    P.op("dve", lambda e, pv=pv, mv=mv, h0=h0: e.tensor_tensor(out=ARBv[:, h0:h0 + 2, :], in0=pv[:, :, 1, :], in1=mv[:, :, 1, :],
                                                                               op=ALU.mult), reads=[bps[b], bc], writes=[bARB])
                    b = nb()
                    for hh in range(2):
                        h = h0 + hh
                        P.op("pe", lambda e, b=b, h=h, hh=hh, q4=q4, qmv=qmv: e.matmul(
                            ps[b][:, hh * 256:(hh + 1) * 256], qmv[:, h % 2, h // 2, 1, :], q4[:, h // 2, 0:2, :].rearrange("p q l -> p (q l)"),
                            start=True, stop=True), reads=[bqt[j], bqm], writes=[bps[b]])
                    P.op("dve", lambda e, b=b, h0=h0, maskQ2=maskQ2: e.tensor_tensor(out=AK[:, h0 * 256:(h0 + 2) * 256], in0=ps[b], in1=maskQ2, op=ALU.mult),
                         reads=[bps[b], bc], writes=[bAK])
            if os.environ.get("RW_DBG") == "1":
                P.dma(X.DBG[:, 0:1024], Pb, reads=bPb)
                P.dma(X.DBG[:, 1024:3072], QT, reads=bQ + bT)
                P.dma(X.DBG[:, 3072:5120], AK, reads=[bAK])
                P.dma(X.DBG[:, 5120:6144], ARB, reads=[bARB])
                P.barrier(); A.reset(m0); return
            lo_ = d == 0
            bdP = cs_["rw_bd_low4"] if lo_ else cs_["rw_bd_up4"]
            bdQ = cs_["rw_bd_up4"] if lo_ else cs_["rw_bd_low4"]
            lvP = [cs_["rw_l32_low4"] if lo_ else cs_["rw_l32_up4"], cs_["rw_l64_low4"] if lo_ else cs_["rw_l64_up4"]]
            lvQ = [cs_["rw_l32_up4"] if lo_ else cs_["rw_l32_low4"], None]
            id4 = ident.unsqueeze(1).broadcast_to([128, 4, 128])
            for gi in range(2):
                g4 = slice(gi * 4, gi * 4 + 4)
                gs = slice(gi * 512, (gi + 1) * 512)
                P.op("dve", lambda e, g4=g4, gs=gs, bdP=bdP: e.tensor_tensor(out=PDv[:, g4, 0, :], in0=Pb[:, gs].rearrange("p (h l) -> p h l", h=4),
                                                                             in1=bdP.rearrange("p (h l) -> p h l", h=4), op=ALU.mult),
                     reads=[bPb[gi], bc], writes=[bPk[gi]])
                P.op("pool", lambda e, g4=g4, gs=gs, bdQ=bdQ: e.tensor_tensor(out=QTv[:, g4, 0, :], in0=Qf[:, gs].rearrange("p (h l) -> p h l", h=4),
                                                                              in1=bdQ.rearrange("p (h l) -> p h l", h=4), op=ALU.mult),
                     reads=[bQf[gi], bc], writes=[bQ[gi]])
                P.op("pool", lambda e, g4=g4: e.tensor_tensor(out=PDv[:, g4, 1, :], in0=PDv[:, g4, 0, :], in1=id4, op=ALU.add),
                     reads=[bPk[gi], bc], writes=[bD[gi]])
                P.op("dve", lambda e, g4=g4: e.tensor_tensor(out=QTv[:, g4, 1, :], in0=QTv[:, g4, 0, :], in1=id4, op=ALU.add),
                     reads=[bQ[gi], bc], writes=[bT[gi]])
            for step in range(1, 6):
                for gi in range(2):
                    g4 = slice(gi * 4, gi * 4 + 4)
                    if step in (1, 5):
                        bA, bB = nb(), nb()
                        for h in range(gi * 4, gi * 4 + 4):
                            hl = h % 4
                            oa = ps[bA][:, hl * 128:(hl + 1) * 128]
                            ob = ps[bB][:, hl * 128:(hl + 1) * 128]
                            ra = QTv[:, h, 0, :] if step == 1 else QTv[:, h, 1, :]
                            rb = PDv[:, h, 0, :] if step == 1 else PDv[:, h, 1, :]
                            P.op("pe", lambda e, oa=oa, h=h, ra=ra: e.matmul(oa, PDv[:, h, 0, :], ra, start=True, stop=True),
                                 reads=[bPk[gi], bQ[gi], bT[gi]], writes=[bps[bA]])
                            P.op("pe", lambda e, ob=ob, h=h, rb=rb: e.matmul(ob, QTv[:, h, 0, :], rb, start=True, stop=True),
                                 reads=[bQ[gi], bPk[gi], bD[gi]], writes=[bps[bB]])
                        pa = ps[bA].rearrange("p (h l) -> p h l", h=4)
                        pb_ = ps[bB].rearrange("p (h l) -> p h l", h=4)
                        if step == 1:
                            P.op("dve", lambda e, pa=pa, g4=g4: e.tensor_copy(out=QTv[:, g4, 0, :], in_=pa), reads=[bps[bA]], writes=[bQ[gi]])
                            P.op("act", lambda e, pb_=pb_, g4=g4: e.activation(out=PDv[:, g4, 0, :], in_=pb_, func=AF.Copy), reads=[bps[bB]], writes=[bPk[gi]])
                        else:
                            P.op("dve", lambda e, pa=pa, g4=g4: e.tensor_tensor(out=QTv[:, g4, 1, :], in0=pa, in1=QTv[:, g4, 1, :], op=ALU.add),
                                 reads=[bps[bA], bT[gi]], writes=[bT[gi]])
                            P.op("dve", lambda e, pb_=pb_, g4=g4: e.tensor_tensor(out=PDv[:, g4, 1, :], in0=pb_, in1=PDv[:, g4, 1, :], op=ALU.add),
                                 reads=[bps[bB], bD[gi]], writes=[bD[gi]])
                    else:
                        bQD = nb2()
                        bPD_ = nb2()
                        for h in range(gi * 4, gi * 4 + 4):
                            hl = h % 4
                            oq = ps[bQD + hl // 2][:, (hl % 2) * 256:(hl % 2 + 1) * 256]
                            op_ = ps[bPD_ + hl // 2][:, (hl % 2) * 256:(hl % 2 + 1) * 256]
                            P.op("pe", lambda e, oq=oq, h=h: e.matmul(oq, PDv[:, h, 0, :], QT[:, h * 256:(h + 1) * 256], start=True, stop=True),
                                 reads=[bPk[gi], bQ[gi], bT[gi]], writes=[bps[bQD + hl // 2]])
                            P.op("pe", lambda e, op_=op_, h=h: e.matmul(op_, QTv[:, h, 0, :], PD[:, h * 256:(h + 1) * 256], start=True, stop=True),
                                 reads=[bQ[gi], bPk[gi], bD[gi]], writes=[bps[bPD_ + hl // 2]])
                        for k2 in range(2):
                            h0 = gi * 4 + k2 * 2
                            pq = ps[bQD + k2].rearrange("p (h a l) -> p h a l", h=2, a=2)
                            pp = ps[bPD_ + k2].rearrange("p (h a l) -> p h a l", h=2, a=2)
                            P.op("dve", lambda e, pq=pq, h0=h0: e.tensor_copy(out=QTv[:, h0:h0 + 2, 0, :], in_=pq[:, :, 0, :]),
                                 reads=[bps[bQD + k2]], writes=[bQ[gi]])
                            P.op("dve", lambda e, pq=pq, h0=h0: e.tensor_tensor(out=QTv[:, h0:h0 + 2, 1, :], in0=pq[:, :, 1, :], in1=QTv[:, h0:h0 + 2, 1, :],
                                                                                op=ALU.add), reads=[bps[bQD + k2], bT[gi]], writes=[bT[gi]])
                            P.op("act", lambda e, pp=pp, h0=h0: e.activation(out=PDv[:, h0:h0 + 2, 0, :], in_=pp[:, :, 0, :], func=AF.Copy),
                                 reads=[bps[bPD_ + k2]], writes=[bPk[gi]])
                            P.op("dve", lambda e, pp=pp, h0=h0: e.tensor_tensor(out=PDv[:, h0:h0 + 2, 1, :], in0=pp[:, :, 1, :], in1=PDv[:, h0:h0 + 2, 1, :],
                                                                                op=ALU.add), reads=[bps[bPD_ + k2], bD[gi]], writes=[bD[gi]])
            for lv in range(2):
                for gi in range(2):
                    g4 = slice(gi * 4, gi * 4 + 4)
                    gs = slice(gi * 512, (gi + 1) * 512)
                    mP, mQ = lvP[lv], lvQ[lv]
                    P.op("pool", lambda e, gs=gs, mP=mP: e.tensor_tensor(out=Ls[0][:, gs], in0=Pb[:, gs], in1=mP, op=ALU.mult),
                         reads=[bPb[gi], bc], writes=[bLs[0][gi]])
                    bA = nb()
                    for h in range(gi * 4, gi * 4 + 4):
                        hl = h % 4
                        P.op("pe", lambda e, bA=bA, h=h, hl=hl: e.matmul(ps[bA][:, hl * 128:(hl + 1) * 128], Ls[0][:, h * 128:(h + 1) * 128], QTv[:, h, 1, :],
                                                                       start=True, stop=True), reads=[bLs[0][gi], bT[gi]], writes=[bps[bA]])
                    P.op("act", lambda e, bA=bA, gs=gs: e.activation(out=Ws[0][:, gs], in_=ps[bA], func=AF.Copy), reads=[bps[bA]], writes=[bWs[0][gi]])
                    if lv == 0:
                        P.op("pool", lambda e, gs=gs, mQ=mQ: e.tensor_tensor(out=Ls[1][:, gs], in0=Qf[:, gs], in1=mQ, op=ALU.mult),
                             reads=[bQf[gi], bc], writes=[bLs[1][gi]])
                        bB = nb()
                        for h in range(gi * 4, gi * 4 + 4):
                            hl = h % 4
                            P.op("pe", lambda e, bB=bB, h=h, hl=hl: e.matmul(ps[bB][:, hl * 128:(hl + 1) * 128], Ls[1][:, h * 128:(h + 1) * 128], PDv[:, h, 1, :],
                                                                           start=True, stop=True), reads=[bLs[1][gi], bD[gi]], writes=[bps[bB]])
                        P.op("dve", lambda e, bB=bB, gs=gs: e.tensor_copy(out=Ws[1][:, gs], in_=ps[bB]), reads=[bps[bB]], writes=[bWs[1][gi]])
                    bA2 = nb()
                    for h in range(gi * 4, gi * 4 + 4):
                        hl = h % 4
                        P.op("pe", lambda e, bA2=bA2, h=h, hl=hl: e.matmul(ps[bA2][:, hl * 128:(hl + 1) * 128], PDv[:, h, 1, :], Ws[0][:, h * 128:(h + 1) * 128],
                                                                         start=True, stop=True), reads=[bD[gi], bWs[0][gi]], writes=[bps[bA2]])
                    if lv == 0:
                        bB2 = nb()
                        for h in range(gi * 4, gi * 4 + 4):
                            hl = h % 4
                            P.op("pe", lambda e, bB2=bB2, h=h, hl=hl: e.matmul(ps[bB2][:, hl * 128:(hl + 1) * 128], QTv[:, h, 1, :], Ws[1][:, h * 128:(h + 1) * 128],
                                                                             start=True, stop=True), reads=[bT[gi], bWs[1][gi]], writes=[bps[bB2]])
                    P.op("dve", lambda e, bA2=bA2, g4=g4: e.tensor_tensor(out=QTv[:, g4, 1, :], in0=ps[bA2].rearrange("p (h l) -> p h l", h=4),
                                                                          in1=QTv[:, g4, 1, :], op=ALU.add), reads=[bps[bA2], bT[gi]], writes=[bT[gi]])
                    if lv == 0:
                        P.op("dve", lambda e, bB2=bB2, g4=g4: e.tensor_tensor(out=PDv[:, g4, 1, :], in0=ps[bB2].rearrange("p (h l) -> p h l", h=4),
                                                                              in1=PDv[:, g4, 1, :], op=ALU.add), reads=[bps[bB2], bD[gi]], writes=[bD[gi]])
            if os.environ.get("RW_DBG") == "3":
                P.dma(X.DBG[:, 0:1024], Pb, reads=bPb)
                P.dma(X.DBG[:, 1024:3072], QT, reads=bQ + bT)
                P.barrier(); A.reset(m0); return
            Hc, bHc = Hst[cur], bH[cur]
            Hn, bHn = Hst[1 - cur], bH[1 - cur]
            bX = nb()
            for a in range(4):
                P.op("pe", lambda e, a=a, q4=q4, Hc=Hc, bX=bX: e.matmul(ps[bX][:, a * 128:(a + 1) * 128], q4[:, a, 0, :], Hc[:, a * 128:(a + 1) * 128],
                                                                      start=True, stop=False), reads=[bqt[j], bHc], writes=[bps[bX]])
                for half in range(2):
                    h = a * 2 + half
                    P.op("pe", lambda e, h=h, half=half, bX=bX: e.matmul(ps[bX][:, h * 64:(h + 1) * 64], AKv[:, h, 0, :], vtm[:, h * 64:(h + 1) * 64],
                                                                       start=False, stop=(half == 1)), reads=[bAK, bvtm], writes=[bps[bX]])
            P.op("act", lambda e, bX=bX: e.activation(out=Xs, in_=ps[bX], func=AF.Copy), reads=[bps[bX]], writes=[bXs])
            if os.environ.get('RW_B3') == '1':
                P.barrier(); A.reset(m0); return
            bU = nb()
            for h in range(8):
                P.op("pe", lambda e, bU=bU, h=h: e.matmul(ps[bU][:, h * 64:(h + 1) * 64], QTv[:, h, 1, :], Xs[:, h * 64:(h + 1) * 64],
                                                          start=True, stop=True), reads=[bT[h // 4], bXs], writes=[bps[bU]])
            P.op("dve", lambda e, bU=bU: e.tensor_copy(out=Us, in_=ps[bU]), reads=[bps[bU]], writes=[bUs])
            if os.environ.get('RW_B3') == '2':
                P.barrier(); A.reset(m0); return
            bYp = nb()
            for a in range(4):
                P.op("pe", lambda e, a=a, q4=q4, Hc=Hc, bYp=bYp: e.matmul(ps[bYp][:, a * 128:(a + 1) * 128], q4[:, a, 1, :], Hc[:, a * 128:(a + 1) * 128],
                                                                        start=True, stop=False), reads=[bqt[j], bHc], writes=[bps[bYp]])
                for half in range(2):
                    h = a * 2 + half
                    o = ps[bYp][:, h * 64:(h + 1) * 64]
                    P.op("pe", lambda e, o=o, h=h: e.matmul(o, ARBv[:, h, :], Us[:, h * 64:(h + 1) * 64], start=False, stop=False),
                         reads=[bARB, bUs], writes=[bps[bYp]])
                    P.op("pe", lambda e, o=o, h=h, half=half: e.matmul(o, AKv[:, h, 1, :], vtm[:, h * 64:(h + 1) * 64], start=False, stop=(half == 1)),
                         reads=[bAK, bvtm], writes=[bps[bYp]])
            if os.environ.get('RW_B3') == '3':
                P.barrier(); A.reset(m0); return
            bHp = nb()
            for a in range(4):
                o = ps[bHp][:, a * 128:(a + 1) * 128]
                P.op("pe", lambda e, o=o, a=a: e.matmul(o, btmk[:, a * 128:(a + 1) * 128], Us[:, a * 128:(a + 1) * 128], start=True, stop=False),
                     reads=[bbtmk, bUs], writes=[bps[bHp]])
                P.op("pe", lambda e, o=o, a=a: e.matmul(o, btmk[:, 512 + a * 128:512 + (a + 1) * 128], vtm[:, a * 128:(a + 1) * 128],
                                                        start=False, stop=True), reads=[bbtmk, bvtm], writes=[bps[bHp]])
            for half in range(2):
                rows = slice(half * 64, half * 64 + 64)
                pin = ps[bHp].rearrange("p (a x i) -> p a x i", a=4, x=2)[rows, :, half, :]
                P.op("dve", lambda e, pin=pin, rows=rows, half=half, Hn=Hn, Hc=Hc: e.tensor_tensor(
                    out=Hn.rearrange("p (a x i) -> p a x i", a=4, x=2)[rows, :, half, :], in0=pin,
                    in1=Hc.rearrange("p (a x i) -> p a x i", a=4, x=2)[rows, :, half, :], op=ALU.add),
                    reads=[bps[bHp], bHc], writes=[bHn])
            P.op("dve", lambda e, Hn=Hn, c=c: e.tensor_tensor(
                out=Hn.rearrange("p (a i) -> p a i", a=4), in0=Hn.rearrange("p (a i) -> p a i", a=4),
                in1=GL.rearrange("p (a c) -> p a c", a=4)[:, :, c:c + 1].broadcast_to([128, 4, 128]), op=ALU.mult),
                reads=[bHn, bGL], writes=[bHn])
            cur = 1 - cur
            if d == 0:
                P.op("act", lambda e, bYp=bYp, j=j: e.activation(out=Ysb[j], in_=ps[bYp], func=AF.Copy), reads=[bps[bYp]], writes=[bY[j]])
                P.dma(X.RY0[cs, :], Ysb[j], reads=[bY[j]])
                continue
            P.dma(y0[j], X.RY0[cs, :], writes=[by0[j]])
            P.dma(rkbin[j].rearrange("p (a l) -> p a l", a=4), X.RKB[:, cs].rearrange("(a p) l -> p a l", p=128), writes=[brkbin[j]])
            P.dma(gin[j].rearrange("p (a l) -> p a l", a=4), X.RGT[:, cs].rearrange("(a p) l -> p a l", p=128), writes=[bgin[j]])
            Y = Ysb[j]
            bYj = bY[j]
            P.op("dve", lambda e, bYp=bYp, Y=Y, j=j: e.tensor_tensor(out=Y, in0=ps[bYp], in1=y0[j], op=ALU.add),
                 reads=[bps[bYp], by0[j]], writes=[bYj])
            Y3 = Y.rearrange("p (h i) -> p h i", h=8)
            s8 = st8.rearrange("p (k h) -> p k h", k=8)
            P.op("dve", lambda e, Y3=Y3, s8=s8: e.tensor_reduce(out=s8[:, 0, :], in_=Y3, axis=AX.X, op=ALU.add), reads=[bYj], writes=[bst8])
            P.op("act", lambda e, Y=Y: e.activation(out=ysq, in_=Y, func=AF.Square), reads=[bYj], writes=[bysq])
            P.op("dve", lambda e, s8=s8: e.tensor_reduce(out=s8[:, 1, :], in_=ysq.rearrange("p (h i) -> p h i", h=8), axis=AX.X, op=ALU.add),
                 reads=[bysq], writes=[bst8])
            P.op("dve", lambda e, s8=s8: e.tensor_scalar(out=s8[:, 2, :], in0=s8[:, 0, :], scalar1=1.0 / 64, scalar2=None, op0=ALU.mult),
                 reads=[bst8], writes=[bst8])
            P.op("dve", lambda e, s8=s8: e.tensor_tensor(out=s8[:, 5, :], in0=s8[:, 2, :], in1=s8[:, 2, :], op=ALU.mult), reads=[bst8], writes=[bst8])
            P.op("dve", lambda e, s8=s8: e.scalar_tensor_tensor(out=s8[:, 3, :], in0=s8[:, 1, :], scalar=1.0 / 64, in1=s8[:, 5, :],
                                                                op0=ALU.mult, op1=ALU.subtract), reads=[bst8], writes=[bst8])
            P.op("dve", lambda e, s8=s8: e.tensor_scalar(out=s8[:, 3, :], in0=s8[:, 3, :], scalar1=GN_EPS, scalar2=None, op0=ALU.add),
                 reads=[bst8], writes=[bst8])
            P.op("act", lambda e, s8=s8: e.activation(out=s8[:, 3, :], in_=s8[:, 3, :], func=AF.Sqrt), reads=[bst8], writes=[bst8])
            P.op("dve", lambda e, s8=s8: e.reciprocal(out=s8[:, 3, :], in_=s8[:, 3, :]), reads=[bst8], writes=[bst8])
            P.op("dve", lambda e, Y3=Y3, s8=s8: e.tensor_tensor(out=Y3, in0=Y3, in1=s8[:, 2, :].unsqueeze(2).broadcast_to([128, 8, 64]),
                                                                op=ALU.subtract), reads=[bYj, bst8], writes=[bYj])
            P.op("dve", lambda e, Y3=Y3, s8=s8: e.tensor_tensor(out=Y3, in0=Y3, in1=s8[:, 3, :].unsqueeze(2).broadcast_to([128, 8, 64]),
                                                                op=ALU.mult), reads=[bYj, bst8], writes=[bYj])
            P.op("pool", lambda e, Y=Y: e.tensor_tensor(out=Y, in0=Y, in1=lnwb[:, 0:512], op=ALU.mult), reads=[bYj, blnwb], writes=[bYj])
            P.op("pool", lambda e, Y=Y: e.tensor_tensor(out=Y, in0=Y, in1=lnwb[:, 512:1024], op=ALU.add), reads=[bYj, blnwb], writes=[bYj])
            bB = nb()
            for a in range(4):
                P.op("pe", lambda e, bB=bB, a=a, j=j: e.matmul(ps[bB][:, a * 2:(a + 1) * 2], rkbin[j][:, a * 128:(a + 1) * 128], halfsel,
                                                               start=True, stop=True), reads=[brkbin[j], bc], writes=[bps[bB]])
            P.op("act", lambda e, bB=bB, s8=s8: e.activation(out=s8[:, 4, :], in_=ps[bB][:, 0:8], func=AF.Copy), reads=[bps[bB]], writes=[bst8])
            P.op("dve", lambda e, s8=s8: e.tensor_tensor(out=ysq.rearrange("p (h i) -> p h i", h=8), in0=vtm.rearrange("p (h i) -> p h i", h=8),
                                                         in1=s8[:, 4, :].unsqueeze(2).broadcast_to([128, 8, 64]), op=ALU.mult),
                 reads=[bvtm, bst8, bysq], writes=[bysq])
            P.op("pool", lambda e, Y=Y: e.tensor_tensor(out=Y, in0=Y, in1=ysq, op=ALU.add), reads=[bYj, bysq], writes=[bYj])
            bO = nb()
            for a in range(4):
                P.op("pe", lambda e, bO=bO, a=a, Y=Y: e.matmul(ps[bO][:, a * 128:(a + 1) * 128], Y[:, a * 128:(a + 1) * 128], ident,
                                                               start=True, stop=True), reads=[bYj, bc], writes=[bps[bO]])
            P.op("dve", lambda e, bO=bO, j=j: e.tensor_tensor(out=oT, in0=ps[bO], in1=gin[j], op=ALU.mult), reads=[bps[bO], bgin[j]], writes=[boT])
            P.dma(X.OT[OT_RWKV:OT_RWKV + 512, cs].rearrange("(a p) l -> p a l", p=128), oT.rearrange("p (a l) -> p a l", a=4), reads=[boT])
        P.barrier()
    A.reset(m0)


def stage_out(X, layer):
    nc, P, A = X.nc, X.P, X.A
    T = X.T
    NTB = T // 512
    so, small, bc = X.so, X.small, X.bconst
    ps, bps = X.ps, X.bps
    ones = X.ones
    m0 = A.mark()
    psi = [0]

    def nb():
        b = psi[0] % 8
        psi[0] += 1
        return b
    x1 = A.alloc(8 * 512)
    bx1 = Buf()
    h2 = A.alloc(8 * 512)
    bh2 = Buf()
    R = A.alloc(32 * 512)
    ot = R[:, 0:16 * 512]
    mg = R[:, 16 * 512:24 * 512]
    xb = R[:, 24 * 512:32 * 512]
    sq = R[:, 0:8 * 512]
    u = R
    bot, bmg, bxb, bsq, bu = Buf(), Buf(), Buf(), Buf(), Buf()
    wb = A.alloc(16 * 128)
    bwb = Buf()
    wo = [A.alloc(8 * 128) for _ in range(2)]
    bwo = [Buf() for _ in range(2)]
    w1 = [A.alloc(8 * 128) for _ in range(3)]
    bw1 = [Buf() for _ in range(3)]
    w2 = A.alloc(32 * 128)
    bw2 = Buf()
    gt = [A.alloc(512) for _ in range(3)]
    bgt = [Buf() for _ in range(3)]
    tmp = A.alloc(512)
    btmp = Buf()
    rstd = A.alloc(512)
    brstd = Buf()
    KT = (4, 8, 4)
    K0 = (0, 4, 12)
    gi_ = 0
    w1i = 0
    for tb in range(NTB):
        t0 = tb * 512
        ts = slice(t0, t0 + 512)
        P.dma(ot.rearrange("p (k t) -> p k t", k=16), X.OT[:, ts].rearrange("(k p) t -> p k t", p=128), writes=[bot])
        P.dma(xb.rearrange("p (k t) -> p k t", k=8), X.xcur[:, ts].rearrange("(k p) t -> p k t", p=128), writes=[bxb])
        for m in range(8):
            P.dma(wb.rearrange("p (k c) -> p k c", k=16), X.w_branch[layer, :, m * 128:(m + 1) * 128].rearrange("(k p) c -> p k c", p=128),
                  writes=[bwb])
            for i in range(3):
                g = gi_ % 3
                gi_ += 1
                r0 = C_GATE + i * 1024 + m * 128
                P.dma(gt[g], X.PT[r0:r0 + 128, ts], writes=[bgt[g]])
                P.op("act", lambda e, g=g: e.activation(out=gt[g], in_=gt[g], func=AF.Sigmoid), reads=[bgt[g]], writes=[bgt[g]])
                b = nb()
                for kk_ in range(KT[i]):
                    k = K0[i] + kk_
                    P.op("pe", lambda e, b=b, k=k, kk_=kk_, i=i: e.matmul(ps[b], wb[:, k * 128:(k + 1) * 128], ot[:, k * 512:(k + 1) * 512],
                                                                         start=(kk_ == 0), stop=(kk_ == KT[i] - 1)), reads=[bwb, bot], writes=[bps[b]])
                if i == 0:
                    P.op("dve", lambda e, b=b, g=g, m=m: e.tensor_tensor(out=mg[:, m * 512:(m + 1) * 512], in0=ps[b], in1=gt[g], op=ALU.mult),
                         reads=[bps[b], bgt[g]], writes=[bmg])
                else:
                    P.op("dve", lambda e, b=b, g=g: e.tensor_tensor(out=tmp, in0=ps[b], in1=gt[g], op=ALU.mult),
                         reads=[bps[b], bgt[g]], writes=[btmp])
                    P.op("pool", lambda e, m=m: e.tensor_tensor(out=mg[:, m * 512:(m + 1) * 512], in0=mg[:, m * 512:(m + 1) * 512], in1=tmp, op=ALU.add),
                         reads=[btmp, bmg], writes=[bmg])
        for n in range(8):
            j = n % 2
            P.dma(wo[j].rearrange("p (k c) -> p k c", k=8), X.w_out[layer, :, n * 128:(n + 1) * 128].rearrange("(k p) c -> p k c", p=128),
                  writes=[bwo[j]])
            b = nb()
            for k in range(8):
                P.op("pe", lambda e, b=b, k=k, j=j: e.matmul(ps[b], wo[j][:, k * 128:(k + 1) * 128], mg[:, k * 512:(k + 1) * 512],
                                                             start=(k == 0), stop=(k == 7)), reads=[bwo[j], bmg], writes=[bps[b]])
            P.op("dve", lambda e, b=b, n=n: e.tensor_tensor(out=x1[:, n * 512:(n + 1) * 512], in0=ps[b], in1=xb[:, n * 512:(n + 1) * 512], op=ALU.add),
                 reads=[bps[b], bxb], writes=[bx1])
        P.op("act", lambda e: e.activation(out=sq, in_=x1, func=AF.Square), reads=[bx1, bot], writes=[bsq, bot])
        b = nb()
        for k in range(8):
            P.op("pe", lambda e, b=b, k=k: e.matmul(ps[b], ones, sq[:, k * 512:(k + 1) * 512], start=(k == 0), stop=(k == 7)),
                 reads=[bsq, bc], writes=[bps[b]])
        P.op("dve", lambda e, b=b: e.tensor_scalar(out=rstd, in0=ps[b], scalar1=1.0 / D, scalar2=EPS, op0=ALU.mult, op1=ALU.add),
             reads=[bps[b]], writes=[brstd])
        P.op("act", lambda e: e.activation(out=rstd, in_=rstd, func=AF.Sqrt), reads=[brstd], writes=[brstd])
        P.op("dve", lambda e: e.reciprocal(out=rstd, in_=rstd), reads=[brstd], writes=[brstd])
        mno = so["mlp_norm"] + layer * 8
        for k in range(8):
            P.op("dve", lambda e, k=k: e.scalar_tensor_tensor(out=h2[:, k * 512:(k + 1) * 512], in0=x1[:, k * 512:(k + 1) * 512],
                                                              scalar=small[:, mno + k:mno + k + 1], in1=rstd, op0=ALU.mult, op1=ALU.mult),
                 reads=[bx1, brstd, bc], writes=[bh2])
        for f in range(32):
            j = w1i % 3
            w1i += 1
            P.dma(w1[j].rearrange("p (k c) -> p k c", k=8), X.w_mlp_in[layer, :, f * 128:(f + 1) * 128].rearrange("(k p) c -> p k c", p=128),
                  writes=[bw1[j]])
            b = nb()
            for k in range(8):
                P.op("pe", lambda e, b=b, k=k, j=j: e.matmul(ps[b], w1[j][:, k * 128:(k + 1) * 128], h2[:, k * 512:(k + 1) * 512],
                                                             start=(k == 0), stop=(k == 7)), reads=[bw1[j], bh2], writes=[bps[b]])
            uf = u[:, f * 512:(f + 1) * 512]
            P.op("act", lambda e, b=b, uf=uf: e.activation(out=uf, in_=ps[b], func=AF.Relu),
                 reads=[bps[b], bot, bmg, bxb, bsq], writes=[bu])
            P.op("pool" if f % 2 else "dve", lambda e, uf=uf: e.tensor_tensor(out=uf, in0=uf, in1=uf, op=ALU.mult), reads=[bu], writes=[bu])
        for n in range(8):
            P.dma(w2.rearrange("p (k c) -> p k c", k=32), X.w_mlp_out[layer, :, n * 128:(n + 1) * 128].rearrange("(k p) c -> p k c", p=128),
                  writes=[bw2])
            b = nb()
            for f in range(32):
                P.op("pe", lambda e, b=b, f=f: e.matmul(ps[b], w2[:, f * 128:(f + 1) * 128], u[:, f * 512:(f + 1) * 512],
                                                        start=(f == 0), stop=(f == 31)), reads=[bw2, bu], writes=[bps[b]])
            P.op("dve", lambda e, b=b, n=n: e.tensor_tensor(out=x1[:, n * 512:(n + 1) * 512], in0=ps[b], in1=x1[:, n * 512:(n + 1) * 512], op=ALU.add),
                 reads=[bps[b], bx1], writes=[bx1])
        P.dma(X.xnext[:, ts].rearrange("(k p) t -> p k t", p=128), x1.rearrange("p (k t) -> p k t", k=8), reads=[bx1],
              writes=[])
        bot.readers.update(bu.readers)
        bxb.readers.update(bu.readers)
        bot.last_w = bu.last_w
        bxb.last_w = bu.last_w
        bmg.last_w = bu.last_w
        bmg.readers.update(bu.readers)
    P.barrier()
    A.reset(m0)


def stage_final(X):
    nc, P, A = X.nc, X.P, X.A
    T = X.T
    NTB = T // 512
    so, small, bc = X.so, X.small, X.bconst
    ps, bps = X.ps, X.bps
    ones = X.ones
    m0 = A.mark()
    xt = [A.alloc(8 * 512) for _ in range(2)]
    bxt = [Buf() for _ in range(2)]
    sq = A.alloc(8 * 512)
    bsq = Buf()
    rstd = A.alloc(512)
    brstd = Buf()
    fo = so["final_norm"]
    for tb in range(NTB):
        j = tb % 2
        ts = slice(tb * 512, (tb + 1) * 512)
        P.dma(xt[j].rearrange("p (k t) -> p k t", k=8), X.xcur[:, ts].rearrange("(k p) t -> p k t", p=128), writes=[bxt[j]])
        P.op("act", lambda e, j=j: e.activation(out=sq, in_=xt[j], func=AF.Square), reads=[bxt[j]], writes=[bsq])
        b = tb % 8
        for k in range(8):
            P.op("pe", lambda e, b=b, k=k: e.matmul(ps[b], ones, sq[:, k * 512:(k + 1) * 512], start=(k == 0), stop=(k == 7)),
                 reads=[bsq, bc], writes=[bps[b]])
        P.op("dve", lambda e, b=b: e.tensor_scalar(out=rstd, in0=ps[b], scalar1=1.0 / D, scalar2=EPS, op0=ALU.mult, op1=ALU.add),
             reads=[bps[b]], writes=[brstd])
        P.op("act", lambda e: e.activation(out=rstd, in_=rstd, func=AF.Sqrt), reads=[brstd], writes=[brstd])
        P.op("dve", lambda e: e.reciprocal(out=rstd, in_=rstd), reads=[brstd], writes=[brstd])
        for k in range(8):
            P.op("dve", lambda e, k=k, j=j: e.scalar_tensor_tensor(out=xt[j][:, k * 512:(k + 1) * 512], in0=xt[j][:, k * 512:(k + 1) * 512],
                                                                   scalar=small[:, fo + k:fo + k + 1], in1=rstd, op0=ALU.mult, op1=ALU.mult),
                 reads=[bxt[j], brstd, bc], writes=[bxt[j]])
        P.dma(X.yT[:, ts].rearrange("(k p) t -> p k t", p=128), xt[j].rearrange("p (k t) -> p k t", k=8), reads=[bxt[j]])
    P.barrier()
    A.reset(m0)


def pack_small(inp):
    so = {}
    cols = []
    off = 0
    so["attn_norm"] = off
    a = inp["attn_norm"].reshape(DEPTH, 8, 128).transpose(2, 0, 1).reshape(128, DEPTH * 8)
    cols.append(a)
    off += a.shape[1]

    def add(name, arr):
        nonlocal off
        arr = np.asarray(arr, np.float32).reshape(128, -1)
        so[name] = off
        cols.append(arr)
        off += arr.shape[1]
    cw = inp["ssm_conv_w"].reshape(DEPTH, 5, 12, 128).transpose(3, 0, 2, 1)
    add("ssm_conv_w", cw)
    add("ssm_conv_b", inp["ssm_conv_b"].reshape(DEPTH, 12, 128).transpose(2, 0, 1))
    dtb = np.zeros((128, DEPTH), np.float32)
    alg = np.zeros((128, DEPTH), np.float32)
    for dr in range(2):
        dtb[dr * 32:dr * 32 + 16] = inp["ssm_dt_bias"][:, dr].T
        alg[dr * 32:dr * 32 + 16] = inp["ssm_a_log"][:, dr].T
    add("ssm_dt_bias", dtb)
    add("ssm_a_log", alg)
    add("ssm_d", np.broadcast_to(inp["ssm_d"].reshape(1, DEPTH * 16), (128, DEPTH * 16)))
    add("ssm_norm_w", inp["ssm_norm_w"].reshape(DEPTH, 8, 128).transpose(2, 0, 1))
    add("mlp_norm", inp["mlp_norm"].reshape(DEPTH, 8, 128).transpose(2, 0, 1))
    add("final_norm", inp["final_norm"].reshape(8, 128).T)
    mu = np.zeros((128, DEPTH, 16), np.float32)
    for L in range(DEPTH):
        m = inp["rwkv_mu"][L]
        for i in range(12):
            mu[:, L, i] = m[i * 128:(i + 1) * 128]
        mu[0:64, L, 12] = m[1536:1600]
        mu[0:64, L, 13] = m[1600:1664]
        mu[0:64, L, 14] = m[1664:1728]
    add("rw_mu", mu)

    def fm4(a):
        return np.asarray(a).reshape(DEPTH, 4, 128).transpose(2, 0, 1)
    add("rw_w0", np.stack([fm4(inp["rwkv_w0"][:, d]) for d in range(2)], axis=2))
    add("rw_a0", np.stack([fm4(inp["rwkv_a0"][:, d]) for d in range(2)], axis=2))
    add("rw_kk", fm4(inp["rwkv_k_k"]))
    add("rw_ka", fm4(inp["rwkv_k_a"]))
    add("rw_rk", fm4(inp["rwkv_r_k"].reshape(DEPTH, 512)))
    return np.ascontiguousarray(np.concatenate(cols, axis=1).astype(np.float32)), so


def build(T, NL, so, nsmall, co, ncst, dbg=(), stages=("in", "ret", "ssm", "rwkv", "out"), L0=0):
    nc = bass.Bass("TRN2", target_bir_lowering=False)
    X = Ctx()
    X.nc = nc
    X.T = T
    X.so = so
    X.co = co
    X.P = Prog(nc)
    X.xT = nc.dram_tensor("xT", [D, T], F32, kind="ExternalInput").ap()
    X.w_in = nc.dram_tensor("w_in", [DEPTH, D, N_IN], F32, kind="ExternalInput").ap()
    small_d = nc.dram_tensor("small", [128, nsmall], F32, kind="ExternalInput").ap()
    cst_d = nc.dram_tensor("cst", [128, ncst], F32, kind="ExternalInput").ap()
    X.rope = nc.dram_tensor("rope", [128, 2, T], F32, kind="ExternalInput").ap()

    def scratch(name, shape):
        kind = "ExternalOutput" if name in dbg else "Internal"
        return nc.dram_tensor(name, shape, F32, kind=kind).ap()
    X.PT = scratch("PT", [N_IN, T])
    X.Vtm = scratch("Vtm", [T, 512])
    X.OT = scratch("OT", [2048, T])
    X.RWQ = scratch("RWQ", [2, 512, T // 128, 4, 128])
    X.RVT = scratch("RVT", [512, T])
    X.RGT = scratch("RGT", [512, T])
    X.RKB = scratch("RKB", [512, T])
    X.RGL = scratch("RGL", [2, 512, T // 128])
    X.RY0 = scratch("RY0", [T, 512])
    X.DBG = scratch("DBG", [128, 8192])
    X.rw_w_up = nc.dram_tensor("rwkv_w_up", [DEPTH, 2, 32, 512], F32, kind="ExternalInput").ap()
    X.rw_a_up = nc.dram_tensor("rwkv_a_up", [DEPTH, 2, 32, 512], F32, kind="ExternalInput").ap()
    X.rw_g_up = nc.dram_tensor("rwkv_g_up", [DEPTH, 64, 512], F32, kind="ExternalInput").ap()
    X.rw_lnwb = nc.dram_tensor("rw_lnwb", [DEPTH, 2, 512], F32, kind="ExternalInput").ap()
    X.w_branch = nc.dram_tensor("w_branch", [DEPTH, 2048, D], F32, kind="ExternalInput").ap()
    X.w_out = nc.dram_tensor("w_out", [DEPTH, D, D], F32, kind="ExternalInput").ap()
    X.w_mlp_in = nc.dram_tensor("w_mlp_in", [DEPTH, D, 4096], F32, kind="ExternalInput").ap()
    X.w_mlp_out = nc.dram_tensor("w_mlp_out", [DEPTH, 4096, D], F32, kind="ExternalInput").ap()
    X.RSF = scratch("RSF", [T // 128, 128, 512])
    X.XA = scratch("XA", [D, T])
    X.XB = scratch("XB", [D, T])
    X.XC = scratch("XC", [1536, T])
    X.Q4 = scratch("Q4", [64, 4, T])
    X.YP = scratch("YP", [T, 1024])
    X.HINF = scratch("HINF", [T // 128, 128, 1024])
    X.HINB = scratch("HINB", [T // 128, 128, 1024])
    X.yT = nc.dram_tensor("yT", [D, T], F32, kind="ExternalOutput").ap()
    X.A = Arena(nc, 50500)
    A = X.A
    psall = nc.alloc_psum_tensor("psall", [128, 4096], F32)
    X.psall = psall
    X.ps = [psall[:, i * 512:(i + 1) * 512] for i in range(8)]
    X.bps = [Buf(excl=True) for _ in range(8)]
    X.bconst = Buf()
    X.cst_d = cst_d
    X.cst = None
    X.small = A.alloc(nsmall)
    X.ones = A.alloc(128)
    X.ident = A.alloc(128)
    P = X.P
    P.dma(X.ones, cst_d[:, co["ones"]:co["ones"] + 128], writes=[X.bconst])
    P.dma(X.ident, cst_d[:, co["ident"]:co["ident"] + 128], writes=[X.bconst])
    P.dma(X.small, small_d, writes=[X.bconst])
    P.barrier()
    X.xcur = X.xT
    for layer in range(L0, L0 + NL):
        X.xnext = X.XA if layer % 2 == 0 else X.XB
        if "in" in stages:
            stage_in(X, layer)
        if "ret" in stages:
            stage_ret(X, layer)
        if "ssm" in stages:
            stage_ssm(X, layer)
        if "rwkv" in stages:
            stage_rwkv(X, layer)
        if "out" in stages:
            stage_out(X, layer)
            X.xcur = X.xnext
    if "out" in stages:
        stage_final(X)
    P.emit()
    return nc, X


def make_inputs(inp, T, b):
    x = np.asarray(inp["x"])[b, :T]
    return {"xT": np.ascontiguousarray(x.T)}


def kernel(**inp):
    inp = {k: np.asarray(v) for k, v in inp.items()}
    B, T = inp["x"].shape[0], inp["x"].shape[1]
    small, so = pack_small(inp)
    cst, co, rope = host_consts(T)
    nc, X = build(T, DEPTH, so, small.shape[1], co, cst.shape[1])
    shared = {"w_in": inp["w_in"], "small": small, "cst": cst, "rope": rope,
              "rwkv_w_up": inp["rwkv_w_up"], "rwkv_a_up": inp["rwkv_a_up"], "rwkv_g_up": inp["rwkv_g_up"],
              "rw_lnwb": np.ascontiguousarray(np.stack([inp["rwkv_ln_w"], inp["rwkv_ln_b"]], axis=1)),
              "w_branch": inp["w_branch"], "w_out": inp["w_out"], "w_mlp_in": inp["w_mlp_in"], "w_mlp_out": inp["w_mlp_out"]}
    in_maps = []
    for b in range(B):
        m = dict(shared)
        m["xT"] = np.ascontiguousarray(inp["x"][b].T)
        in_maps.append(m)
    res = run_bass_kernel_spmd(nc, in_maps, core_ids=list(range(B)))
    out = np.stack([np.ascontiguousarray(r["yT"].T) for r in res.results], axis=0)
    return out.astype(np.float32)
```

```python
import math
import contextlib
import numpy as np
import concourse.bass as bass
import concourse.mybir as mybir
from concourse.bass_utils import run_bass_kernel_spmd

F32 = mybir.dt.float32
AF = mybir.ActivationFunctionType
ALU = mybir.AluOpType
AX = mybir.AxisListType

D = 1024
SEQ = 4096
DEPTH = 4
N_IN = 9440
EPS = 1e-6
C_RWKV, C_SSM, C_RET, C_GATE = 0, 1728, 4320, 6368
EPOCH = 20000
import os
NO_POOL = bool(os.environ.get('NO_POOL'))
OPLOG = bool(os.environ.get('OPLOG'))
MAXOPS = int(os.environ.get('MAXOPS', '100000000'))


class Buf:
    __slots__ = ("last_w", "readers", "excl")

    def __init__(self, excl=False):
        self.last_w = None
        self.readers = {}
        self.excl = excl


class Prog:
    ENGS = ("pe", "act", "dve", "pool", "sp")

    def __init__(self, nc, n_dma_sems=32):
        self.nc = nc
        self.streams = {e: [] for e in self.ENGS}
        self.cnt = {e: 0 for e in self.ENGS}
        self.epoch = {e: 0 for e in self.ENGS}
        self.seen = {e: {} for e in self.ENGS}
        self.nd = n_dma_sems
        self.dma_i = 0
        self.nops = 0
        self.dma_last = {}

    def _wait(self, eng, tok):
        key, val = tok
        if self.seen[eng].get(key, 0) >= val:
            return
        self.seen[eng][key] = val
        self.streams[eng].append(("w", key, val))

    def _deps(self, reads, writes):
        deps = set()
        for b in reads:
            if b.last_w is not None:
                deps.add(b.last_w)
            if b.excl:
                for t in b.readers.values():
                    deps.add(t)
        for b in writes:
            if b.last_w is not None:
                deps.add(b.last_w)
            for t in b.readers.values():
                deps.add(t)
        return deps

    def op(self, eng, fn, reads=(), writes=()):
        if self.nops >= MAXOPS:
            return None
        if eng == "pool" and NO_POOL:
            eng = "dve"
        deps = self._deps(reads, writes)
        for tok in sorted(deps, key=lambda t: (str(t[0]), t[1])):
            if eng == "pe" and tok[0][0] == "pe":
                continue
            self._wait(eng, tok)
        self.cnt[eng] += 1
        if self.cnt[eng] > EPOCH:
            self.epoch[eng] += 1
            self.cnt[eng] = 1
        key = (eng, self.epoch[eng])
        tok = (key, self.cnt[eng])
        self.streams[eng].append(("op", fn, key, 1))
        for b in writes:
            b.last_w = tok
            b.readers = {}
        for b in reads:
            b.readers[eng] = tok
        self.nops += 1
        if OPLOG:
            import inspect
            fr = inspect.stack()[1]
            print("OP", self.nops, eng, fr.lineno, fr.code_context[0].strip()[:90])
        return tok

    def dma(self, out, in_, reads=(), writes=(), eng="sp"):
        if self.nops >= MAXOPS:
            return None
        deps = self._deps(reads, writes)
        i = self.dma_i
        self.dma_i += 1
        slot = i % self.nd
        rnd = i // self.nd
        key = ("d", slot)
        if rnd > 0:
            deps.add((key, 16 * rnd))
        for tok in sorted(deps, key=lambda t: (str(t[0]), t[1])):
            self._wait(eng, tok)
        tok = (key, 16 * (rnd + 1))
        self.streams[eng].append(("op", lambda e, o=out, i_=in_: e.dma_start(out=o, in_=i_), key, 16))
        self.dma_last[key] = tok
        for b in writes:
            b.last_w = tok
            b.readers = {}
        for b in reads:
            b.readers[("dma", slot)] = tok
        self.nops += 1
        if OPLOG:
            import inspect
            fr = inspect.stack()[1]
            print("OP", self.nops, "dma", fr.lineno, fr.code_context[0].strip()[:90])
        return tok

    def barrier(self):
        toks = []
        for e in self.ENGS:
            if self.cnt[e] > 0:
                toks.append(((e, self.epoch[e]), self.cnt[e]))
        toks += list(self.dma_last.values())
        for e in self.ENGS:
            for t in toks:
                if e == "pe" and t[0][0] == "pe":
                    continue
                self._wait(e, t)

    def emit(self):
        nc = self.nc
        final = list(self.dma_last.values())
        keys = set()
        for e in self.ENGS:
            for it in self.streams[e]:
                keys.add(it[1] if it[0] == "w" else it[2])
        keys = sorted(keys, key=str)
        with contextlib.ExitStack() as st:
            semh = {}
            for k in keys:
                semh[k] = st.enter_context(nc.semaphore("s_" + "_".join(str(x) for x in k)))
            block = st.enter_context(nc.Block())
            streams = self.streams

            def run(engname):
                def f(e):
                    for it in streams[engname]:
                        if it[0] == "w":
                            e.wait_ge(semh[it[1]], it[2])
                        else:
                            it[1](e).then_inc(semh[it[2]], it[3])
                    if engname == "sp":
                        for (k, v) in final:
                            e.wait_ge(semh[k], v)
                return f
            block.tensor(run("pe"))
            block.scalar(run("act"))
            block.vector(run("dve"))
            block.gpsimd(run("pool"))
            block.sync(run("sp"))


class Arena:
    def __init__(self, nc, words):
        self.t = nc.alloc_sbuf_tensor("arena", [128, words], F32)
        self.words = words
        self.off = 0

    def mark(self):
        return self.off

    def reset(self, m):
        self.off = m

    def alloc(self, n):
        o = self.off
        self.off += n
        assert self.off <= self.words, "SBUF arena overflow %d > %d" % (self.off, self.words)
        return self.t[:, o:o + n]


class Ctx:
    pass


def load_cst(X, names):
    out = {}
    for (n, w) in names:
        t = X.A.alloc(w)
        X.P.dma(t, X.cst_d[:, X.co[n]:X.co[n] + w], writes=[X.bconst])
        out[n] = t
    X.P.barrier()
    return out


OFFS = {}


RET_SCALE = 128 ** -0.5


def host_consts(T):
    parts = []
    co = {}
    off = [0]

    def add(name, arr):
        arr = np.asarray(arr, np.float64).reshape(128, -1)
        co[name] = off[0]
        parts.append(arr)
        off[0] += arr.shape[1]
    add("ones", np.ones((128, 128)))
    add("ident", np.eye(128))
    sw = np.zeros((128, 128))
    for m in range(128):
        sw[(m + 64) % 128, m] = 1.0
    add("swap", sw)
    lg = np.log(1.0 - 2.0 ** (-5.0 - np.arange(4, dtype=np.float64)))
    pos = np.arange(128, dtype=np.float64)
    mask = np.exp(lg[None, :, None] * np.abs(pos[:, None, None] - pos[None, None, :])) * RET_SCALE
    add("ret_mask", mask)
    add("ret_vf", np.exp(lg[None, :] * (127.0 - pos[:, None])) * RET_SCALE)
    add("ret_vb", np.exp(lg[None, :] * pos[:, None]) * RET_SCALE)
    add("ret_gf", np.broadcast_to(np.exp(lg[None, :, None] * (pos[None, None, :] + 1.0)), (128, 4, 128)))
    add("ret_gb", np.broadcast_to(np.exp(lg[None, :, None] * (128.0 - pos[None, None, :])), (128, 4, 128)))
    selu = np.zeros((128, 32, 128))
    for hd in range(32):
        dr, hh = hd // 16, hd % 16
        selu[dr * 32 + hh, hd, :] = 1.0
    add("ssm_selu", selu)
    sl = pos[:, None] - pos[None, :]
    add("ssm_maskf", np.tile(np.where(sl <= 0, 0.0, -30000.0), (1, 4)))
    add("ssm_maskb", np.tile(np.where(sl >= 0, 0.0, -30000.0), (1, 4)))
    pf = pos[:, None] - pos[None, :]
    low = (pf > 0) * 1.0
    up = (pf < 0) * 1.0
    lowi = (pf >= 0) * 1.0
    upi = (pf <= 0) * 1.0
    add("rw_low4", np.tile(low, (1, 4)))
    add("rw_up4", np.tile(up, (1, 4)))
    add("rw_upupi2", np.tile(np.concatenate([up, upi], axis=1), (1, 2)))
    add("rw_lowlowi2", np.tile(np.concatenate([low, lowi], axis=1), (1, 2)))
    bi = np.arange(128)
    bd32 = (bi[:, None] // 32 == bi[None, :] // 32) * 1.0
    bd64 = (bi[:, None] // 64 == bi[None, :] // 64) * 1.0
    add("rw_bd_low4", np.tile(bd32 * low, (1, 4)))
    add("rw_bd_up4", np.tile(bd32 * up, (1, 4)))
    add("rw_l32_low4", np.tile(bd64 * (1 - bd32) * low, (1, 4)))
    add("rw_l32_up4", np.tile(bd64 * (1 - bd32) * up, (1, 4)))
    add("rw_l64_low4", np.tile((1 - bd64) * low, (1, 4)))
    add("rw_l64_up4", np.tile((1 - bd64) * up, (1, 4)))
    blk = np.zeros((128, 128))
    blk[0:64, 0:64] = 1.0
    blk[64:128, 64:128] = 1.0
    add("rw_blk", blk)
    hs = np.zeros((128, 2))
    hs[0:64, 0] = 1.0
    hs[64:128, 1] = 1.0
    add("rw_halfsel", hs)
    cst = np.ascontiguousarray(np.concatenate(parts, axis=1).astype(np.float32))
    co["ret_gl"] = [float(np.exp(lg[h] * 128.0)) for h in range(4)]
    inv_freq = 1.0 / (10000.0 ** np.linspace(0.0, 1.0, 64))
    ang = np.arange(T, dtype=np.float64)[None, :] * inv_freq[:, None]
    ang = (np.arange(T, dtype=np.float32)[None, :] * inv_freq.astype(np.float32)[:, None]).astype(np.float64)
    cos, sin = np.cos(ang), np.sin(ang)
    rope = np.zeros((128, 2, T), np.float32)
    rope[0:64, 0] = cos
    rope[64:128, 0] = cos
    rope[0:64, 1] = -sin
    rope[64:128, 1] = sin
    return cst, co, rope


def col_tiles():
    tiles = []

    def rng(c0, n):
        o = 0
        while o < n:
            w = min(128, n - o)
            tiles.append((c0 + o, w))
            o += w
    rng(C_RWKV, 1728)
    rng(C_SSM, 1024)
    rng(C_SSM + 1024, 1536)
    rng(C_SSM + 2560, 32)
    rng(C_RET, 2048)
    rng(C_GATE, 3072)
    return tiles


def col_groups():
    groups = []
    cur = []
    for (c0, w) in col_tiles():
        if cur and (cur[-1][0] + cur[-1][1] == c0) and (sum(t[1] for t in cur) + w <= 512):
            cur.append((c0, w))
        else:
            if cur:
                groups.append(cur)
            cur = [(c0, w)]
    groups.append(cur)
    return groups


def stage_in(X, layer):
    nc, P, A = X.nc, X.P, X.A
    T = X.T
    HB = min(T, 2048)
    NH = T // HB
    NTBH = HB // 512
    m0 = A.mark()
    xt = [A.alloc(8 * 512) for _ in range(2)]
    bxt = [Buf() for _ in range(2)]
    sq = A.alloc(8 * 512)
    bsq = Buf()
    rstd = A.alloc(512)
    brstd = Buf()
    hT = A.alloc(8 * HB)
    bh = Buf()
    hTv = hT.rearrange("p (k t) -> p k t", k=8)
    wt = [A.alloc(8 * 512) for _ in range(2)]
    bwt = [Buf() for _ in range(2)]
    stg = [A.alloc(512) for _ in range(3)]
    bstg = [Buf() for _ in range(3)]
    groups = col_groups()
    w_in = X.w_in
    ps = X.ps
    bps = X.bps
    psi = 0
    wi = 0
    si = 0
    xi = 0
    for hf in range(NH):
        for tb in range(NTBH):
            j = xi % 2
            xi += 1
            t0 = hf * HB + tb * 512
            xv = xt[j].rearrange("p (k t) -> p k t", k=8)
            P.dma(xv, X.xcur[:, t0:t0 + 512].rearrange("(k p) t -> p k t", p=128), writes=[bxt[j]])
            P.op("act", lambda e, j=j: e.activation(out=sq, in_=xt[j], func=AF.Square), reads=[bxt[j]], writes=[bsq])
            pb = psi % 8
            psi += 1
            for kt in range(8):
                P.op("pe", lambda e, kt=kt, pb=pb: e.matmul(ps[pb], X.ones, sq[:, kt * 512:(kt + 1) * 512],
                                                             start=(kt == 0), stop=(kt == 7)),
                     reads=[bsq, X.bconst], writes=[bps[pb]])
            P.op("dve", lambda e, pb=pb: e.tensor_scalar(out=rstd, in0=ps[pb], scalar1=1.0 / D, scalar2=EPS,
                                                          op0=ALU.mult, op1=ALU.add), reads=[bps[pb]], writes=[brstd])
            P.op("act", lambda e: e.activation(out=rstd, in_=rstd, func=AF.Sqrt), reads=[brstd], writes=[brstd])
            P.op("dve", lambda e: e.reciprocal(out=rstd, in_=rstd), reads=[brstd], writes=[brstd])
            for kt in range(8):
                P.op("dve" if kt % 2 == 0 else "dve", lambda e, kt=kt, j=j, tb=tb: e.scalar_tensor_tensor(
                    out=hTv[:, kt, tb * 512:(tb + 1) * 512], in0=xt[j][:, kt * 512:(kt + 1) * 512],
                    scalar=X.small[:, X.so["attn_norm"] + layer * 8 + kt: X.so["attn_norm"] + layer * 8 + kt + 1],
                    in1=rstd, op0=ALU.mult, op1=ALU.mult), reads=[bxt[j], brstd, X.bconst], writes=[bh])
        for grp in groups:
            g0 = grp[0][0]
            gw = sum(t[1] for t in grp)
            k = wi % 2
            wi += 1
            wtv = wt[k].rearrange("p (k c) -> p k c", k=8)
            P.dma(wtv[:, :, 0:gw], w_in[layer, :, g0:g0 + gw].rearrange("(k p) c -> p k c", p=128), writes=[bwt[k]])
            for tb in range(NTBH):
                t0 = hf * HB + tb * 512
                for (c0, w) in grp:
                    off = c0 - g0
                    pb = psi % 8
                    psi += 1
                    for kt in range(8):
                        P.op("pe", lambda e, kt=kt, pb=pb, wtv=wtv, w=w, off=off, tb=tb: e.matmul(
                            ps[pb][0:w, :], wtv[:, kt, off:off + w], hTv[:, kt, tb * 512:(tb + 1) * 512], start=(kt == 0), stop=(kt == 7)),
                            reads=[bwt[k], bh], writes=[bps[pb]])
                    s = si % 3
                    si += 1
                    if si % 2 == 0:
                        P.op("act", lambda e, s=s, pb=pb, w=w: e.activation(out=stg[s][0:w, :], in_=ps[pb][0:w, :], func=AF.Copy),
                             reads=[bps[pb]], writes=[bstg[s]])
                    else:
                        P.op("dve", lambda e, s=s, pb=pb, w=w: e.tensor_copy(out=stg[s][0:w, :], in_=ps[pb][0:w, :]),
                             reads=[bps[pb]], writes=[bstg[s]])
                    P.dma(X.PT[c0:c0 + w, t0:t0 + 512], stg[s][0:w, :], reads=[bstg[s]])
        k = wi % 2
        wi += 1
        wvv = wt[k].rearrange("p (k c) -> p k c", k=8)
        P.dma(wvv, w_in[layer, :, C_RET + 1024:C_RET + 1536].rearrange("(k p) c -> p k c", p=128), writes=[bwt[k]])
        for sub in range(HB // 128):
            t0 = hf * HB + sub * 128
            pb = psi % 8
            psi += 1
            for kt in range(8):
                P.op("pe", lambda e, kt=kt, pb=pb, sub=sub, wvv=wvv: e.matmul(
                    ps[pb], hTv[:, kt, sub * 128:(sub + 1) * 128], wvv[:, kt, :],
                    start=(kt == 0), stop=(kt == 7)), reads=[bwt[k], bh], writes=[bps[pb]])
            s = si % 3
            si += 1
            P.op("dve", lambda e, s=s, pb=pb: e.tensor_copy(out=stg[s], in_=ps[pb]), reads=[bps[pb]], writes=[bstg[s]])
            P.dma(X.Vtm[t0:t0 + 128, :], stg[s], reads=[bstg[s]])
    P.barrier()
    A.reset(m0)


OT_RWKV, OT_SSM, OT_RET = 0, 512, 1536


def stage_ret(X, layer):
    nc, P, A = X.nc, X.P, X.A
    T = X.T
    NT = T // 128
    NTB = T // 512
    co = X.co
    m0 = A.mark()
    ps, bps = X.ps, X.bps
    cs_ = load_cst(X, [("ret_mask", 512), ("ret_vf", 4), ("ret_vb", 4), ("ret_gf", 512), ("ret_gb", 512), ("swap", 128)])
    mask, vf, vb, gf, gb, swap = (cs_[k] for k in ("ret_mask", "ret_vf", "ret_vb", "ret_gf", "ret_gb", "swap"))
    ident, ones = X.ident, X.ones
    gl = co["ret_gl"]
    bc = X.bconst
    SB = A.alloc(NT * 512)
    bSB = Buf()
    sfc = [A.alloc(512) for _ in range(2)]
    bsfc = [Buf() for _ in range(2)]
    rope = A.alloc(2 * 512)
    brope = Buf()
    kin = A.alloc(4 * 512)
    bkin = Buf()
    qin = A.alloc(4 * 512)
    bqin = Buf()
    kr = A.alloc(4 * 512)
    bkr = Buf()
    qr = A.alloc(4 * 512)
    bqr = Buf()
    tmp = A.alloc(4 * 512)
    btmp = Buf()
    ktm = A.alloc(512)
    bktm = Buf()
    vt = [A.alloc(512) for _ in range(2)]
    bvt = [Buf() for _ in range(2)]
    vfb = A.alloc(1024)
    bvfb = Buf()
    srun = [A.alloc(512) for _ in range(2)]
    bsrun = [Buf() for _ in range(2)]
    psi = [0]

    def nb():
        b = psi[0] % 8
        psi[0] += 1
        return b

    def rotary(src, bsrc, dst, bdst, row0, t0):
        P.dma(src.rearrange("p (h t) -> p h t", h=4),
              X.PT[row0:row0 + 512, t0:t0 + 512].rearrange("(h p) t -> p h t", p=128), writes=[bsrc])
        P.op("dve", lambda e: e.tensor_tensor(
            out=tmp.rearrange("p (h t) -> p h t", h=4), in0=src.rearrange("p (h t) -> p h t", h=4),
            in1=rope[:, 0:512].unsqueeze(1).broadcast_to([128, 4, 512]), op=ALU.mult),
            reads=[bsrc, brope], writes=[btmp])
        for hp in range(2):
            b0 = nb()
            b1 = nb()
            for hh, b in ((0, b0), (1, b1)):
                h = hp * 2 + hh
                P.op("pe", lambda e, h=h, b=b: e.matmul(ps[b], swap, src[:, h * 512:(h + 1) * 512], start=True, stop=True),
                     reads=[bsrc, bc], writes=[bps[b]])
            for hh, b in ((0, b0), (1, b1)):
                h = hp * 2 + hh
                P.op("dve", lambda e, h=h, b=b: e.tensor_tensor(out=dst[:, h * 512:(h + 1) * 512], in0=ps[b],
                                                               in1=rope[:, 512:1024], op=ALU.mult),
                     reads=[bps[b], brope], writes=[bdst])
        P.op("pool", lambda e: e.tensor_tensor(out=dst, in0=dst, in1=tmp, op=ALU.add), reads=[bdst, btmp], writes=[bdst])

    cur = 0
    P.op("dve", lambda e: e.memset(srun[0], 0.0), writes=[bsrun[0]])
    for tb in range(NTB):
        t0 = tb * 512
        P.dma(rope.rearrange("p (a t) -> p a t", a=2), X.rope[:, :, t0:t0 + 512], writes=[brope])
        rotary(kin, bkin, kr, bkr, C_RET + 512, t0)
        for ci in range(4):
            c = tb * 4 + ci
            j = c % 2
            P.dma(vt[j], X.Vtm[c * 128:(c + 1) * 128, :], writes=[bvt[j]])
            b = nb()
            for h in range(4):
                P.op("pe", lambda e, h=h, b=b, ci=ci: e.matmul(
                    ps[b][:, h * 128:(h + 1) * 128], kr[:, h * 512 + ci * 128: h * 512 + (ci + 1) * 128], ident,
                    start=True, stop=True), reads=[bkr, bc], writes=[bps[b]])
            P.op("act", lambda e, b=b: e.activation(out=ktm, in_=ps[b], func=AF.Copy), reads=[bps[b]], writes=[bktm])
            vfbv = vfb.rearrange("p (h a e) -> p h a e", h=4, a=2)
            P.op("dve", lambda e, j=j, vfbv=vfbv: e.tensor_tensor(
                out=vfbv[:, :, 0, :], in0=vt[j].rearrange("p (h e) -> p h e", h=4),
                in1=vf.unsqueeze(2).broadcast_to([128, 4, 128]), op=ALU.mult), reads=[bvt[j], bc], writes=[bvfb])
            P.op("pool", lambda e, j=j, vfbv=vfbv: e.tensor_tensor(
                out=vfbv[:, :, 1, :], in0=vt[j].rearrange("p (h e) -> p h e", h=4),
                in1=vb.unsqueeze(2).broadcast_to([128, 4, 128]), op=ALU.mult), reads=[bvt[j], bc], writes=[bvfb])
            b0, b1 = nb(), nb()
            for h in range(4):
                b = b0 if h < 2 else b1
                P.op("pe", lambda e, h=h, b=b: e.matmul(ps[b][:, (h % 2) * 256:(h % 2 + 1) * 256], ktm[:, h * 128:(h + 1) * 128],
                                                         vfb[:, h * 256:(h + 1) * 256], start=True, stop=True),
                     reads=[bktm, bvfb], writes=[bps[b]])
            P.dma(X.RSF[c], srun[cur], reads=[bsrun[cur]])
            nxt = 1 - cur
            for h in range(4):
                b = b0 if h < 2 else b1
                P.op("dve", lambda e, h=h, b=b, cur=cur, nxt=nxt: e.scalar_tensor_tensor(
                    out=srun[nxt][:, h * 128:(h + 1) * 128], in0=srun[cur][:, h * 128:(h + 1) * 128], scalar=gl[h],
                    in1=ps[b][:, (h % 2) * 256:(h % 2) * 256 + 128], op0=ALU.mult, op1=ALU.add),
                    reads=[bsrun[cur], bps[b]], writes=[bsrun[nxt]])
                P.op("act", lambda e, h=h, b=b, c=c: e.activation(
                    out=SB[:, c * 512 + h * 128: c * 512 + (h + 1) * 128],
                    in_=ps[b][:, (h % 2) * 256 + 128:(h % 2) * 256 + 256], func=AF.Copy),
                    reads=[bps[b]], writes=[bSB])
            cur = nxt
    if os.environ.get('RET_STOP') == '1':
        P.barrier(); A.reset(m0); return
    P.op("dve", lambda e, cur=cur: e.memset(srun[cur], 0.0), writes=[bsrun[cur]])
    for c in range(NT - 1, -1, -1):
        nxt = 1 - cur
        for h in range(4):
            P.op("dve", lambda e, h=h, c=c, cur=cur, nxt=nxt: e.scalar_tensor_tensor(
                out=srun[nxt][:, h * 128:(h + 1) * 128], in0=srun[cur][:, h * 128:(h + 1) * 128], scalar=gl[h],
                in1=SB[:, c * 512 + h * 128: c * 512 + (h + 1) * 128], op0=ALU.mult, op1=ALU.add),
                reads=[bsrun[cur], bSB], writes=[bsrun[nxt]])
        P.op("pool", lambda e, c=c, cur=cur: e.tensor_copy(out=SB[:, c * 512:(c + 1) * 512], in_=srun[cur]),
             reads=[bsrun[cur], bSB], writes=[bSB])
        cur = nxt
    if os.environ.get('RET_STOP') == '2':
        P.barrier(); A.reset(m0); return
    P.barrier()
    gin = A.alloc(4 * 512)
    bgin = Buf()
    qf = A.alloc(4 * 512)
    bqf = Buf()
    qb = A.alloc(4 * 512)
    bqb = Buf()
    sm = A.alloc(512)
    bsm = Buf()
    sqo = A.alloc(512)
    bsqo = Buf()
    rs = A.alloc(512)
    brs = Buf()
    ot = [A.alloc(512) for _ in range(2)]
    bot = [Buf() for _ in range(2)]
    for tb in range(NTB):
        t0 = tb * 512
        P.dma(rope.rearrange("p (a t) -> p a t", a=2), X.rope[:, :, t0:t0 + 512], writes=[brope])
        rotary(kin, bkin, kr, bkr, C_RET + 512, t0)
        rotary(qin, bqin, qr, bqr, C_RET, t0)
        P.dma(gin.rearrange("p (h t) -> p h t", h=4),
              X.PT[C_RET + 1536:C_RET + 2048, t0:t0 + 512].rearrange("(h p) t -> p h t", p=128), writes=[bgin])
        P.op("act", lambda e: e.activation(out=gin, in_=gin, func=AF.Silu), reads=[bgin], writes=[bgin])
        qr4 = qr.rearrange("p (h c l) -> p h c l", h=4, c=4)
        for ci in range(4):
            P.op("dve", lambda e, ci=ci: e.tensor_tensor(
                out=qf.rearrange("p (h c l) -> p h c l", h=4, c=4)[:, :, ci, :], in0=qr4[:, :, ci, :],
                in1=gf.rearrange("p (h l) -> p h l", h=4), op=ALU.mult), reads=[bqr, bc], writes=[bqf])
            P.op("pool", lambda e, ci=ci: e.tensor_tensor(
                out=qb.rearrange("p (h c l) -> p h c l", h=4, c=4)[:, :, ci, :], in0=qr4[:, :, ci, :],
                in1=gb.rearrange("p (h l) -> p h l", h=4), op=ALU.mult), reads=[bqr, bc], writes=[bqb])
        for ci in range(4):
            c = tb * 4 + ci
            j = c % 2
            P.dma(vt[j], X.Vtm[c * 128:(c + 1) * 128, :], writes=[bvt[j]])
            P.dma(sfc[j], X.RSF[c], writes=[bsfc[j]])
            b = nb()
            for h in range(4):
                sl = slice(h * 512 + ci * 128, h * 512 + (ci + 1) * 128)
                P.op("pe", lambda e, h=h, b=b, sl=sl: e.matmul(ps[b][:, h * 128:(h + 1) * 128], kr[:, sl], qr[:, sl],
                                                               start=True, stop=True), reads=[bkr, bqr], writes=[bps[b]])
            P.op("dve", lambda e, b=b: e.tensor_tensor(out=sm, in0=ps[b], in1=mask, op=ALU.mult),
                 reads=[bps[b], bc], writes=[bsm])
            b = nb()
            for h in range(4):
                sl = slice(h * 512 + ci * 128, h * 512 + (ci + 1) * 128)
                o = ps[b][:, h * 128:(h + 1) * 128]
                P.op("pe", lambda e, h=h, o=o, j=j: e.matmul(o, vt[j][:, h * 128:(h + 1) * 128], sm[:, h * 128:(h + 1) * 128],
                                                             start=True, stop=False), reads=[bvt[j], bsm], writes=[bps[b]])
                P.op("pe", lambda e, h=h, o=o, j=j, sl=sl: e.matmul(o, sfc[j][:, h * 128:(h + 1) * 128], qf[:, sl],
                                                                    start=False, stop=False), reads=[bsfc[j], bqf], writes=[bps[b]])
                P.op("pe", lambda e, h=h, o=o, c=c, sl=sl: e.matmul(o, SB[:, c * 512 + h * 128:c * 512 + (h + 1) * 128], qb[:, sl],
                                                                    start=False, stop=True), reads=[bSB, bqb], writes=[bps[b]])
            P.op("act", lambda e, b=b: e.activation(out=sqo, in_=ps[b], func=AF.Square), reads=[bps[b]], writes=[bsqo])
            b2 = nb()
            P.op("pe", lambda e, b2=b2: e.matmul(ps[b2], ones, sqo, start=True, stop=True), reads=[bsqo, bc], writes=[bps[b2]])
            P.op("dve", lambda e, b2=b2: e.tensor_scalar(out=rs, in0=ps[b2], scalar1=1.0 / 128, scalar2=EPS,
                                                          op0=ALU.mult, op1=ALU.add), reads=[bps[b2]], writes=[brs])
            P.op("act", lambda e: e.activation(out=rs, in_=rs, func=AF.Sqrt), reads=[brs], writes=[brs])
            P.op("dve", lambda e: e.reciprocal(out=rs, in_=rs), reads=[brs], writes=[brs])
            P.op("dve", lambda e, b=b, j=j: e.tensor_tensor(out=ot[j], in0=ps[b], in1=rs, op=ALU.mult),
                 reads=[bps[b], brs], writes=[bot[j]])
            P.op("pool", lambda e, j=j, ci=ci: e.tensor_tensor(
                out=ot[j].rearrange("p (h l) -> p h l", h=4), in0=ot[j].rearrange("p (h l) -> p h l", h=4),
                in1=gin.rearrange("p (h c l) -> p h c l", h=4, c=4)[:, :, ci, :], op=ALU.mult),
                reads=[bot[j], bgin], writes=[bot[j]])
            P.dma(X.OT[OT_RET:OT_RET + 512, c * 128:(c + 1) * 128].rearrange("(h p) t -> p h t", p=128),
                  ot[j].rearrange("p (h l) -> p h l", h=4), reads=[bot[j]])
    P.barrier()
    A.reset(m0)


def stage_ssm(X, layer):
    nc, P, A = X.nc, X.P, X.A
    T = X.T
    NT = T // 128
    co, so = X.co, X.so
    cst, small = X.cst, X.small
    bc = X.bconst
    ps, bps = X.ps, X.bps
    m0 = A.mark()
    psi = [0]

    def nb():
        b = psi[0] % 8
        psi[0] += 1
        return b
    ident, ones = X.ident, X.ones
    cs_ = load_cst(X, [("ssm_selu", 4096), ("ssm_maskf", 512), ("ssm_maskb", 512)])
    selu = cs_["ssm_selu"][0:64, :]
    maskf, maskb = cs_["ssm_maskf"], cs_["ssm_maskb"]
    XS0 = C_SSM + 1024
    m1 = A.mark()
    xp = [A.alloc(T + 4) for _ in range(2)]
    bxp = [Buf() for _ in range(2)]
    acc = [A.alloc(T) for _ in range(2)]
    bacc = [Buf() for _ in range(2)]
    for j in range(2):
        P.op("pool", lambda e, j=j: e.memset(xp[j][:, 0:2], 0.0), writes=[bxp[j]])
        P.op("pool", lambda e, j=j: e.memset(xp[j][:, T + 2:T + 4], 0.0), writes=[bxp[j]])
    for i in range(12):
        j = i % 2
        P.dma(xp[j][:, 2:T + 2], X.PT[XS0 + i * 128:XS0 + (i + 1) * 128, :], writes=[bxp[j]])
        wcol = so["ssm_conv_w"] + (layer * 12 + i) * 5
        bcol = so["ssm_conv_b"] + layer * 12 + i
        P.op("dve", lambda e, j=j, wcol=wcol, bcol=bcol: e.tensor_scalar(
            out=acc[j], in0=xp[j][:, 0:T], scalar1=small[:, wcol:wcol + 1], scalar2=small[:, bcol:bcol + 1],
            op0=ALU.mult, op1=ALU.add), reads=[bxp[j], bc], writes=[bacc[j]])
        for k in range(1, 5):
            P.op("dve", lambda e, j=j, k=k, wcol=wcol: e.scalar_tensor_tensor(
                out=acc[j], in0=xp[j][:, k:k + T], scalar=small[:, wcol + k:wcol + k + 1], in1=acc[j],
                op0=ALU.mult, op1=ALU.add), reads=[bxp[j], bacc[j], bc], writes=[bacc[j]])
        P.op("act", lambda e, j=j: e.activation(out=acc[j], in_=acc[j], func=AF.Silu), reads=[bacc[j]], writes=[bacc[j]])
        P.dma(X.XC[i * 128:(i + 1) * 128, :], acc[j], reads=[bacc[j]])
    P.barrier()
    A.reset(m1)
    q4 = A.alloc(4 * T)
    bq4 = Buf()
    la = A.alloc(T)
    bla = Buf()
    cum = A.alloc(T)
    bcum = Buf()
    rmask = A.alloc(T)
    brm = Buf()
    nA = A.alloc(1)
    bnA = Buf()
    q4v = q4.rearrange("p (q t) -> p q t", q=4)
    dtq = q4v[0:64, 0, :]
    uq = q4v[0:64, 1, :]
    eaq = q4v[0:64, 2, :]
    deq = q4v[0:64, 3, :]
    P.op("pool", lambda e: e.memset(q4[0:64, :], 0.0), writes=[bq4])
    DT0 = C_SSM + 2560
    P.dma(q4v[0:16, 0, :], X.PT[DT0:DT0 + 16, :], writes=[bq4])
    P.dma(q4v[32:48, 0, :], X.PT[DT0 + 16:DT0 + 32, :], writes=[bq4])
    P.op("pool", lambda e: e.memset(rmask[0:64, :], 1.0), writes=[brm])
    P.op("pool", lambda e: e.memset(rmask[0:64, :].rearrange("p (c l) -> p c l", l=128)[:, :, 0:1], 0.0), writes=[brm])
    dbc = so["ssm_dt_bias"] + layer
    alc = so["ssm_a_log"] + layer
    P.op("act", lambda e: e.activation(out=dtq, in_=dtq, func=AF.Exp, bias=small[0:64, dbc:dbc + 1]), reads=[bq4, bc], writes=[bq4])
    P.op("act", lambda e: e.activation(out=dtq, in_=dtq, func=AF.Ln, bias=1.0), reads=[bq4], writes=[bq4])
    P.op("act", lambda e: e.activation(out=nA[0:64, :], in_=small[0:64, alc:alc + 1], func=AF.Exp), reads=[bc], writes=[bnA])
    P.op("dve", lambda e: e.tensor_scalar(out=nA[0:64, :], in0=nA[0:64, :], scalar1=-1.0, scalar2=None, op0=ALU.mult),
         reads=[bnA], writes=[bnA])
    P.op("dve", lambda e: e.tensor_scalar(out=la[0:64, :], in0=dtq, scalar1=nA[0:64, :], scalar2=None, op0=ALU.mult),
         reads=[bq4, bnA], writes=[bla])
    P.op("dve", lambda e: e.tensor_tensor_scan(out=cum[0:64, :], data0=rmask[0:64, :], data1=la[0:64, :], initial=0.0,
                                               op0=ALU.mult, op1=ALU.add), reads=[brm, bla], writes=[bcum])
    cum3 = cum.rearrange("p (c l) -> p c l", l=128)
    atot = cum3[:, :, 127:128].broadcast_to([128, NT, 128])
    P.op("dve", lambda e: e.tensor_copy(out=q4v[0:32, 1, :], in_=cum[0:32, :]), reads=[bcum], writes=[bq4])
    P.op("dve", lambda e: e.tensor_tensor(out=q4v[0:32, 3, :].rearrange("p (c l) -> p c l", l=128), in0=atot[0:32],
                                          in1=cum3[0:32], op=ALU.subtract), reads=[bcum], writes=[bq4])
    P.op("dve", lambda e: e.tensor_tensor(out=q4v[32:64, 3, :], in0=cum[32:64, :], in1=la[32:64, :], op=ALU.subtract),
         reads=[bcum, bla], writes=[bq4])
    P.op("dve", lambda e: e.tensor_tensor(out=q4v[32:64, 1, :].rearrange("p (c l) -> p c l", l=128), in0=atot[32:64],
                                          in1=q4v[32:64, 3, :].rearrange("p (c l) -> p c l", l=128), op=ALU.subtract),
         reads=[bcum, bq4], writes=[bq4])
    P.op("act", lambda e: e.activation(out=eaq, in_=uq, func=AF.Exp), reads=[bq4], writes=[bq4])
    P.op("act", lambda e: e.activation(out=deq, in_=deq, func=AF.Exp), reads=[bq4], writes=[bq4])
    P.dma(X.Q4[:, :, :], q4v[0:64, :, :], reads=[bq4])
    P.barrier()
    A.reset(m1)
    EA = A.alloc(NT * 64)
    CD = A.alloc(NT * 64)
    bEA, bCD = Buf(), Buf()
    xin = [A.alloc(1024) for _ in range(2)]
    bxin = [Buf() for _ in range(2)]
    bcin = [A.alloc(512) for _ in range(2)]
    bbcin = [Buf() for _ in range(2)]
    qin = [A.alloc(512) for _ in range(2)]
    bqin = [Buf() for _ in range(2)]
    xs = A.alloc(1024)
    bxs = Buf()
    btm = A.alloc(256)
    bbtm = Buf()
    tmq = A.alloc(192)
    btmq = Buf()
    gt = A.alloc(256)
    bgt = Buf()
    rhsU = A.alloc(4096)
    brhsU = Buf()
    negu = A.alloc(128)
    bnegu = Buf()
    Eall = A.alloc(4096)
    bE = [Buf() for _ in range(8)]
    xdt = [A.alloc(1024) for _ in range(2)]
    bxdt = [Buf() for _ in range(2)]
    xdd = [A.alloc(1024) for _ in range(2)]
    bxdd = [Buf() for _ in range(2)]
    yp = [A.alloc(1024) for _ in range(2)]
    byp = [Buf() for _ in range(2)]
    hrun = A.alloc(1024)
    bhrun = Buf()
    sbst = [A.alloc(1024) for _ in range(2)]
    bsbst = [Buf() for _ in range(2)]
    dcol = so["ssm_d"] + layer * 16
    P.op("pool", lambda e: e.memset(hrun, 0.0), writes=[bhrun])
    for c in range(NT):
        j = c % 2
        cs = slice(c * 128, (c + 1) * 128)
        P.dma(xin[j].rearrange("p (i t) -> p i t", i=8), X.XC[0:1024, cs].rearrange("(i p) t -> p i t", p=128), writes=[bxin[j]])
        P.dma(bcin[j].rearrange("p (i t) -> p i t", i=4), X.XC[1024:1536, cs].rearrange("(i p) t -> p i t", p=128), writes=[bbcin[j]])
        P.dma(qin[j][0:64, :].rearrange("p (q t) -> p q t", q=4), X.Q4[:, :, cs], writes=[bqin[j]])
        for half in range(2):
            b = nb()
            for ii in range(4):
                i = half * 4 + ii
                P.op("pe", lambda e, b=b, ii=ii, i=i, j=j: e.matmul(ps[b][:, ii * 128:(ii + 1) * 128], xin[j][:, i * 128:(i + 1) * 128],
                                                                   ident, start=True, stop=True), reads=[bxin[j], bc], writes=[bps[b]])
            if half == 0:
                P.op("act", lambda e, b=b: e.activation(out=xs[:, 0:512], in_=ps[b], func=AF.Copy), reads=[bps[b]], writes=[bxs])
            else:
                P.op("dve", lambda e, b=b: e.tensor_copy(out=xs[:, 512:1024], in_=ps[b]), reads=[bps[b]], writes=[bxs])
        b = nb()
        for g in range(2):
            P.op("pe", lambda e, b=b, g=g, j=j: e.matmul(ps[b][:, g * 128:(g + 1) * 128], bcin[j][:, g * 128:(g + 1) * 128], ident,
                                                         start=True, stop=True), reads=[bbcin[j], bc], writes=[bps[b]])
        for qi, q in enumerate((0, 2, 3)):
            P.op("pe", lambda e, b=b, qi=qi, q=q, j=j: e.matmul(ps[b][:, 256 + qi * 64:256 + (qi + 1) * 64],
                                                               qin[j][0:64, q * 128:(q + 1) * 128], ident[0:64, 0:64],
                                                               start=True, stop=True), reads=[bqin[j], bc], writes=[bps[b]])
        P.op("act", lambda e, b=b: e.activation(out=btm, in_=ps[b][:, 0:256], func=AF.Copy), reads=[bps[b]], writes=[bbtm])
        P.op("dve", lambda e, b=b: e.tensor_copy(out=tmq, in_=ps[b][:, 256:448]), reads=[bps[b]], writes=[btmq])
        P.op("pool", lambda e, c=c: e.tensor_copy(out=EA[:, c * 64:(c + 1) * 64], in_=tmq[:, 64:128]), reads=[btmq], writes=[bEA])
        P.op("pool", lambda e, c=c: e.tensor_tensor(out=CD[:, c * 64:(c + 1) * 64], in0=tmq[:, 64:128], in1=tmq[:, 128:192], op=ALU.mult),
             reads=[btmq], writes=[bCD])
        b = nb()
        for g in range(2):
            P.op("pe", lambda e, b=b, g=g, j=j: e.matmul(ps[b][:, g * 128:(g + 1) * 128], bcin[j][:, g * 128:(g + 1) * 128],
                                                         bcin[j][:, 256 + g * 128:256 + (g + 1) * 128], start=True, stop=True),
                 reads=[bbcin[j]], writes=[bps[b]])
        P.op("act", lambda e, b=b: e.activation(out=gt, in_=ps[b][:, 0:256], func=AF.Copy), reads=[bps[b]], writes=[bgt])
        P.op("pool", lambda e, j=j: e.tensor_tensor(
            out=rhsU[0:64, :].rearrange("p (h l) -> p h l", h=32), in0=selu.rearrange("p (h l) -> p h l", h=32),
            in1=qin[j][0:64, 128:256].unsqueeze(1).broadcast_to([64, 32, 128]), op=ALU.mult),
            reads=[bqin[j], bc], writes=[brhsU])
        P.op("dve", lambda e, j=j: e.tensor_scalar(out=negu[0:64, :], in0=qin[j][0:64, 128:256], scalar1=-1.0, scalar2=None, op0=ALU.mult),
             reads=[bqin[j]], writes=[bnegu])
        for k in range(8):
            b = nb()
            sl = slice(k * 512, (k + 1) * 512)
            P.op("pe", lambda e, b=b, sl=sl: e.matmul(ps[b], ones[0:64, :], rhsU[0:64, sl], start=True, stop=False),
                 reads=[brhsU, bc], writes=[bps[b]])
            P.op("pe", lambda e, b=b, sl=sl: e.matmul(ps[b], negu[0:64, :], selu[:, sl], start=False, stop=False),
                 reads=[bnegu, bc], writes=[bps[b]])
            mk = maskf if k < 4 else maskb
            P.op("pe", lambda e, b=b, mk=mk: e.matmul(ps[b], ident, mk, start=False, stop=True), reads=[bc], writes=[bps[b]])
            P.op("act", lambda e, b=b, sl=sl: e.activation(out=Eall[:, sl], in_=ps[b], func=AF.Exp), reads=[bps[b]], writes=[bE[k]])
            g = (k % 4) // 2
            eng = "dve" if k % 2 == 0 else "pool"
            P.op(eng, lambda e, sl=sl, g=g: e.tensor_tensor(
                out=Eall[:, sl].rearrange("p (h l) -> p h l", h=4), in0=Eall[:, sl].rearrange("p (h l) -> p h l", h=4),
                in1=gt[:, g * 128:(g + 1) * 128].unsqueeze(1).broadcast_to([128, 4, 128]), op=ALU.mult),
                reads=[bE[k], bgt], writes=[bE[k]])
        for dr in range(2):
            eng = "dve" if dr == 0 else "pool"
            P.op(eng, lambda e, dr=dr: e.tensor_tensor(
                out=xdt[dr].rearrange("p (h q) -> p h q", h=16), in0=xs.rearrange("p (h q) -> p h q", h=16),
                in1=tmq[:, dr * 32:dr * 32 + 16].unsqueeze(2).broadcast_to([128, 16, 64]), op=ALU.mult),
                reads=[bxs, btmq], writes=[bxdt[dr]])
            P.op(eng, lambda e, dr=dr: e.tensor_tensor(
                out=xdd[dr].rearrange("p (h q) -> p h q", h=16), in0=xdt[dr].rearrange("p (h q) -> p h q", h=16),
                in1=tmq[:, 128 + dr * 32:128 + dr * 32 + 16].unsqueeze(2).broadcast_to([128, 16, 64]), op=ALU.mult),
                reads=[bxdt[dr], btmq], writes=[bxdd[dr]])
        P.op("pool", lambda e, j=j: e.tensor_tensor(
            out=yp[j].rearrange("p (h q) -> p h q", h=16), in0=xs.rearrange("p (h q) -> p h q", h=16),
            in1=small[:, dcol:dcol + 16].unsqueeze(2).broadcast_to([128, 16, 64]), op=ALU.mult),
            reads=[bxs, bc], writes=[byp[j]])
        for g in range(2):
            b = nb()
            for e8 in range(8):
                h = g * 8 + e8
                for dr in range(2):
                    hd = dr * 16 + h
                    P.op("pe", lambda e, b=b, e8=e8, h=h, hd=hd, dr=dr: e.matmul(
                        ps[b][:, e8 * 64:(e8 + 1) * 64], Eall[:, hd * 128:(hd + 1) * 128], xdt[dr][:, h * 64:(h + 1) * 64],
                        start=(dr == 0), stop=(dr == 1)), reads=[bE[hd // 4], bxdt[dr]], writes=[bps[b]])
            P.op("dve", lambda e, b=b, g=g, j=j: e.tensor_tensor(out=yp[j][:, g * 512:(g + 1) * 512], in0=ps[b],
                                                                 in1=yp[j][:, g * 512:(g + 1) * 512], op=ALU.add),
                 reads=[bps[b], byp[j]], writes=[byp[j]])
        P.dma(X.YP[cs, :], yp[j], reads=[byp[j]])
        P.dma(X.HINF[c], hrun, reads=[bhrun])
        for g in range(2):
            b = nb()
            P.op("pe", lambda e, b=b, g=g: e.matmul(ps[b], btm[:, g * 128:(g + 1) * 128], xdd[0][:, g * 512:(g + 1) * 512],
                                                    start=True, stop=True), reads=[bbtm, bxdd[0]], writes=[bps[b]])
            P.op("dve", lambda e, g=g, c=c: e.tensor_tensor(
                out=hrun[:, g * 512:(g + 1) * 512].rearrange("p (h q) -> p h q", h=8),
                in0=hrun[:, g * 512:(g + 1) * 512].rearrange("p (h q) -> p h q", h=8),
                in1=CD[:, c * 64 + g * 8:c * 64 + g * 8 + 8].unsqueeze(2).broadcast_to([128, 8, 64]), op=ALU.mult),
                reads=[bhrun, bCD], writes=[bhrun])
            P.op("dve", lambda e, b=b, g=g: e.tensor_tensor(out=hrun[:, g * 512:(g + 1) * 512], in0=ps[b],
                                                            in1=hrun[:, g * 512:(g + 1) * 512], op=ALU.add),
                 reads=[bps[b], bhrun], writes=[bhrun])
            b = nb()
            P.op("pe", lambda e, b=b, g=g: e.matmul(ps[b], btm[:, g * 128:(g + 1) * 128], xdd[1][:, g * 512:(g + 1) * 512],
                                                    start=True, stop=True), reads=[bbtm, bxdd[1]], writes=[bps[b]])
            P.op("act", lambda e, b=b, g=g, j=j: e.activation(out=sbst[j][:, g * 512:(g + 1) * 512], in_=ps[b], func=AF.Copy),
                 reads=[bps[b]], writes=[bsbst[j]])
        P.dma(X.HINB[c], sbst[j], reads=[bsbst[j]])
    P.barrier()
    P.op("pool", lambda e: e.memset(hrun, 0.0), reads=[bhrun], writes=[bhrun])
    for c in range(NT - 1, -1, -1):
        j = c % 2
        P.dma(sbst[j], X.HINB[c], writes=[bsbst[j]])
        P.dma(X.HINB[c], hrun, reads=[bhrun, bsbst[j]])
        for g in range(2):
            P.op("dve", lambda e, g=g, c=c: e.tensor_tensor(
                out=hrun[:, g * 512:(g + 1) * 512].rearrange("p (h q) -> p h q", h=8),
                in0=hrun[:, g * 512:(g + 1) * 512].rearrange("p (h q) -> p h q", h=8),
                in1=CD[:, c * 64 + 32 + g * 8:c * 64 + 32 + g * 8 + 8].unsqueeze(2).broadcast_to([128, 8, 64]), op=ALU.mult),
                reads=[bhrun, bCD], writes=[bhrun])
        P.op("pool", lambda e, j=j: e.tensor_tensor(out=hrun, in0=hrun, in1=sbst[j], op=ALU.add), reads=[bhrun, bsbst[j]], writes=[bhrun])
    P.barrier()
    zin = [A.alloc(1024) for _ in range(2)]
    bzin = [Buf() for _ in range(2)]
    hf = [A.alloc(1024) for _ in range(2)]
    bhf = [Buf() for _ in range(2)]
    hb = [A.alloc(1024) for _ in range(2)]
    bhb = [Buf() for _ in range(2)]
    tmps = [A.alloc(512) for _ in range(4)]
    btmps = [Buf() for _ in range(4)]
    yzs = [A.alloc(1024) for _ in range(2)]
    byzs = [Buf() for _ in range(2)]
    sqs = [A.alloc(1024) for _ in range(2)]
    bsqs = [Buf() for _ in range(2)]
    rs = A.alloc(256)
    brs = Buf()
    nwc = so["ssm_norm_w"] + layer * 8
    for c in range(NT):
        j = c % 2
        yz, byz, sq, bsq = yzs[j], byzs[j], sqs[j], bsqs[j]
        cs = slice(c * 128, (c + 1) * 128)
        P.dma(yp[j], X.YP[cs, :], writes=[byp[j]])
        P.dma(bcin[j].rearrange("p (i t) -> p i t", i=4), X.XC[1024:1536, cs].rearrange("(i p) t -> p i t", p=128), writes=[bbcin[j]])
        P.dma(hf[j], X.HINF[c], writes=[bhf[j]])
        P.dma(hb[j], X.HINB[c], writes=[bhb[j]])
        P.dma(zin[j].rearrange("p (i t) -> p i t", i=8), X.PT[C_SSM:C_SSM + 1024, cs].rearrange("(i p) t -> p i t", p=128), writes=[bzin[j]])
        P.op("act", lambda e, j=j: e.activation(out=zin[j], in_=zin[j], func=AF.Silu), reads=[bzin[j]], writes=[bzin[j]])
        for dr in range(2):
            hsrc, bh_ = (hf[j], bhf[j]) if dr == 0 else (hb[j], bhb[j])
            for g in range(2):
                b = nb()
                P.op("pe", lambda e, b=b, g=g, j=j, hsrc=hsrc: e.matmul(ps[b], bcin[j][:, 256 + g * 128:256 + (g + 1) * 128],
                                                                        hsrc[:, g * 512:(g + 1) * 512], start=True, stop=True),
                     reads=[bbcin[j], bh_], writes=[bps[b]])
                tmp, btmp = tmps[dr * 2 + g], btmps[dr * 2 + g]
                P.op("dve", lambda e, b=b, g=g, dr=dr, c=c, tmp=tmp: e.tensor_tensor(
                    out=tmp.rearrange("p (h q) -> p h q", h=8), in0=ps[b].rearrange("p (h q) -> p h q", h=8),
                    in1=EA[:, c * 64 + dr * 32 + g * 8:c * 64 + dr * 32 + g * 8 + 8].unsqueeze(2).broadcast_to([128, 8, 64]),
                    op=ALU.mult), reads=[bps[b], bEA], writes=[btmp])
                P.op("pool", lambda e, g=g, j=j, tmp=tmp: e.tensor_tensor(out=yp[j][:, g * 512:(g + 1) * 512], in0=yp[j][:, g * 512:(g + 1) * 512],
                                                                  in1=tmp, op=ALU.add), reads=[btmp, byp[j]], writes=[byp[j]])
        for half in range(2):
            b = nb()
            for ii in range(4):
                i = half * 4 + ii
                P.op("pe", lambda e, b=b, ii=ii, i=i, j=j: e.matmul(ps[b][:, ii * 128:(ii + 1) * 128], yp[j][:, i * 128:(i + 1) * 128],
                                                                   ident, start=True, stop=True), reads=[byp[j], bc], writes=[bps[b]])
            P.op("dve", lambda e, b=b, half=half, j=j, yz=yz: e.tensor_tensor(out=yz[:, half * 512:(half + 1) * 512], in0=ps[b],
                                                                       in1=zin[j][:, half * 512:(half + 1) * 512], op=ALU.mult),
                 reads=[bps[b], bzin[j]], writes=[byz])
        P.op("act", lambda e, sq=sq, yz=yz: e.activation(out=sq, in_=yz, func=AF.Square), reads=[byz], writes=[bsq])
        b = nb()
        for g in range(2):
            for ii in range(4):
                i = g * 4 + ii
                P.op("pe", lambda e, b=b, g=g, ii=ii, i=i, sq=sq: e.matmul(ps[b][:, g * 128:(g + 1) * 128], ones, sq[:, i * 128:(i + 1) * 128],
                                                                   start=(ii == 0), stop=(ii == 3)), reads=[bsq, bc], writes=[bps[b]])
        P.op("dve", lambda e, b=b: e.tensor_scalar(out=rs, in0=ps[b][:, 0:256], scalar1=1.0 / 512, scalar2=EPS,
                                                    op0=ALU.mult, op1=ALU.add), reads=[bps[b]], writes=[brs])
        P.op("act", lambda e: e.activation(out=rs, in_=rs, func=AF.Sqrt), reads=[brs], writes=[brs])
        P.op("dve", lambda e: e.reciprocal(out=rs, in_=rs), reads=[brs], writes=[brs])
        for g in range(2):
            P.op("dve", lambda e, g=g, yz=yz: e.tensor_tensor(
                out=yz[:, g * 512:(g + 1) * 512].rearrange("p (i t) -> p i t", i=4),
                in0=yz[:, g * 512:(g + 1) * 512].rearrange("p (i t) -> p i t", i=4),
                in1=rs[:, g * 128:(g + 1) * 128].unsqueeze(1).broadcast_to([128, 4, 128]), op=ALU.mult),
                reads=[byz, brs], writes=[byz])
        P.op("pool", lambda e, yz=yz: e.tensor_tensor(
            out=yz.rearrange("p (i t) -> p i t", i=8), in0=yz.rearrange("p (i t) -> p i t", i=8),
            in1=small[:, nwc:nwc + 8].unsqueeze(2).broadcast_to([128, 8, 128]), op=ALU.mult), reads=[byz, bc], writes=[byz])
        P.dma(X.OT[OT_SSM:OT_SSM + 1024, cs].rearrange("(i p) t -> p i t", p=128), yz.rearrange("p (i t) -> p i t", i=8), reads=[byz])
    P.barrier()
    A.reset(m0)


NEG_EXP_HALF = -math.exp(-0.5)
GN_EPS = 64e-5


def stage_rwkv(X, layer):
    nc, P, A = X.nc, X.P, X.A
    T = X.T
    NT = T // 128
    TB = 512
    NB = T // TB
    CB = TB // 128
    co, so = X.co, X.so
    cst, small = X.cst, X.small
    bc = X.bconst
    ps, bps = X.ps, X.bps
    m0 = A.mark()
    psi = [0]

    def nb():
        b = psi[0] % 8
        psi[0] += 1
        return b

    def nb2():
        if psi[0] % 2 == 1:
            psi[0] += 1
        b = psi[0] % 8
        psi[0] += 2
        return b
    ident = X.ident
    cs_ = load_cst(X, [("rw_blk", 128), ("rw_halfsel", 2), ("rw_low4", 512), ("rw_up4", 512), ("rw_upupi2", 512), ("rw_lowlowi2", 512),
                       ("rw_bd_low4", 512), ("rw_bd_up4", 512), ("rw_l32_low4", 512), ("rw_l32_up4", 512), ("rw_l64_low4", 512), ("rw_l64_up4", 512)])
    blk, halfsel, low4, up4, upupi2, lowlowi2 = (cs_[k] for k in ("rw_blk", "rw_halfsel", "rw_low4", "rw_up4", "rw_upupi2", "rw_lowlowi2"))

    def scol(name, idx):
        o = so[name] + idx
        return small[:, o:o + 1]
    m1 = A.mark()
    wup = A.alloc(512)
    aup = A.alloc(512)
    gup = A.alloc(512)
    bw = Buf()
    P.dma(wup[0:64, :], X.rw_w_up[layer].rearrange("d k c -> (d k) c"), writes=[bw])
    P.dma(aup[0:64, :], X.rw_a_up[layer].rearrange("d k c -> (d k) c"), writes=[bw])
    P.dma(gup[0:64, :], X.rw_g_up[layer], writes=[bw])
    P.barrier()
    hmu = A.alloc(16)
    omu = A.alloc(16)
    bmu = Buf()
    muo = so["rw_mu"] + layer * 16
    P.op("dve", lambda e: e.tensor_scalar(out=hmu, in0=small[:, muo:muo + 16], scalar1=0.5, scalar2=None, op0=ALU.mult),
         reads=[bc], writes=[bmu])
    P.op("dve", lambda e: e.tensor_scalar(out=omu, in0=small[:, muo:muo + 16], scalar1=-1.0, scalar2=1.0, op0=ALU.mult, op1=ALU.add),
         reads=[bc], writes=[bmu])
    oka = A.alloc(4)
    hrk = A.alloc(4)
    kao = so["rw_ka"] + layer * 4
    rko = so["rw_rk"] + layer * 4
    P.op("dve", lambda e: e.tensor_scalar(out=oka, in0=small[:, kao:kao + 4], scalar1=-1.0, scalar2=1.0, op0=ALU.mult, op1=ALU.add),
         reads=[bc], writes=[bmu])
    P.op("dve", lambda e: e.tensor_scalar(out=hrk, in0=small[:, rko:rko + 4], scalar1=0.5, scalar2=None, op0=ALU.mult),
         reads=[bc], writes=[bmu])
    rmask = A.alloc(TB)
    brm = Buf()
    P.op("pool", lambda e: e.memset(rmask, 1.0), writes=[brm])
    P.op("pool", lambda e: e.memset(rmask.rearrange("p (c l) -> p c l", l=128)[:, :, 0:1], 0.0), writes=[brm])
    xp = [A.alloc(TB + 2) for _ in range(2)]
    bxp = [Buf() for _ in range(2)]
    xpi = [0]
    t1 = A.alloc(TB)
    bt1 = Buf()

    def shifted(row0, nrows, mucol, t0, out, bout, func=None):
        j = xpi[0] % 2
        xpi[0] += 1
        lo = t0 - 1
        hi = t0 + TB + 1
        dlo, dhi = 0, TB + 2
        if lo < 0:
            P.op("pool", lambda e, j=j: e.memset(xp[j][0:nrows, 0:1], 0.0), writes=[bxp[j]])
            lo, dlo = 0, 1
        if hi > T:
            P.op("pool", lambda e, j=j: e.memset(xp[j][0:nrows, TB + 1:TB + 2], 0.0), writes=[bxp[j]])
            hi, dhi = T, TB + 1
        P.dma(xp[j][0:nrows, dlo:dhi], X.PT[row0:row0 + nrows, lo:hi], writes=[bxp[j]])
        P.op("pool", lambda e, j=j: e.tensor_tensor(out=t1[0:nrows, :], in0=xp[j][0:nrows, 0:TB], in1=xp[j][0:nrows, 2:TB + 2], op=ALU.add),
             reads=[bxp[j]], writes=[bt1])
        P.op("dve", lambda e: e.tensor_scalar(out=t1[0:nrows, :], in0=t1[0:nrows, :], scalar1=hmu[0:nrows, mucol:mucol + 1], scalar2=None,
                                              op0=ALU.mult), reads=[bt1, bmu], writes=[bt1])
        P.op("dve", lambda e, j=j: e.scalar_tensor_tensor(out=out[0:nrows, :], in0=xp[j][0:nrows, 1:TB + 1],
                                                          scalar=omu[0:nrows, mucol:mucol + 1], in1=t1[0:nrows, :],
                                                          op0=ALU.mult, op1=ALU.add), reads=[bxp[j], bt1, bmu], writes=[bout])
        if func is not None:
            P.op("act", lambda e: e.activation(out=out[0:nrows, :], in_=out[0:nrows, :], func=func), reads=[bout], writes=[bout])

    def tile(n=TB):
        return A.alloc(n), Buf()
    tw, btw = tile()
    adt, badt = tile()
    sg, bsg = tile()
    ur, bur = tile()
    uk, buk = tile()
    uv, buv = tile()
    gti, bgti = tile()
    kk, bkk = tile()
    sq, bsq = tile()
    sgm, bsgm = tile()
    ad, bad = tile()
    cum, bcum = tile()
    et, bet = tile()
    ct, bct = tile()
    E1, bE1 = tile()
    E2, bE2 = tile()
    E3, bE3 = tile()
    be_, bbe = tile()
    kd, bkd = tile()
    kbs, bkbs = tile()
    glt, bglt = tile(CB)
    R0 = C_RWKV
    for blkk in range(NB):
        t0 = blkk * TB
        shifted(R0 + 1536, 64, 12, t0, tw, btw, AF.Tanh)
        shifted(R0 + 1600, 64, 13, t0, adt, badt, None)
        shifted(R0 + 1664, 64, 14, t0, sg, bsg, AF.Sigmoid)
        for p in range(4):
            pc = slice(p * 128, (p + 1) * 128)
            shifted(R0 + p * 128, 128, p, t0, ur, bur)
            shifted(R0 + 512 + p * 128, 128, 4 + p, t0, uk, buk)
            shifted(R0 + 1024 + p * 128, 128, 8 + p, t0, uv, buv)
            P.dma(X.RVT[pc, t0:t0 + TB], uv, reads=[buv])
            b = nb()
            P.op("pe", lambda e, b=b, pc=pc: e.matmul(ps[b], gup[0:64, pc], sg[0:64, :], start=True, stop=True), reads=[bw, bsg], writes=[bps[b]])
            P.op("act", lambda e, b=b: e.activation(out=gti, in_=ps[b], func=AF.Copy), reads=[bps[b]], writes=[bgti])
            P.dma(X.RGT[pc, t0:t0 + TB], gti, reads=[bgti])
            P.op("dve", lambda e, p=p: e.tensor_scalar(out=kk, in0=uk, scalar1=scol("rw_kk", layer * 4 + p), scalar2=None, op0=ALU.mult),
                 reads=[buk, bc], writes=[bkk])
            P.op("act", lambda e: e.activation(out=sq, in_=kk, func=AF.Square), reads=[bkk], writes=[bsq])
            b = nb()
            P.op("pe", lambda e, b=b: e.matmul(ps[b], blk, sq, start=True, stop=True), reads=[bsq, bc], writes=[bps[b]])
            P.op("act", lambda e, b=b: e.activation(out=sq, in_=ps[b], func=AF.Sqrt), reads=[bps[b]], writes=[bsq])
            P.op("dve", lambda e: e.tensor_scalar(out=sq, in0=sq, scalar1=1e-12, scalar2=None, op0=ALU.max), reads=[bsq], writes=[bsq])
            P.op("dve", lambda e: e.reciprocal(out=sq, in_=sq), reads=[bsq], writes=[bsq])
            P.op("dve", lambda e: e.tensor_tensor(out=kk, in0=kk, in1=sq, op=ALU.mult), reads=[bkk, bsq], writes=[bkk])
            for d in range(2):
                ds_ = slice(d * 32, (d + 1) * 32)
                b = nb()
                P.op("pe", lambda e, b=b, pc=pc, ds_=ds_: e.matmul(ps[b], wup[ds_, pc], tw[ds_, :], start=True, stop=True),
                     reads=[bw, btw], writes=[bps[b]])
                P.op("act", lambda e, b=b, d=d, p=p: e.activation(out=sgm, in_=ps[b], func=AF.Sigmoid,
                                                                 bias=scol("rw_w0", (layer * 2 + d) * 4 + p)),
                     reads=[bps[b], bc], writes=[bsgm])
                b = nb()
                P.op("pe", lambda e, b=b, pc=pc, ds_=ds_: e.matmul(ps[b], aup[ds_, pc], adt[ds_, :], start=True, stop=True),
                     reads=[bw, badt], writes=[bps[b]])
                P.op("act", lambda e, b=b, d=d, p=p: e.activation(out=ad, in_=ps[b], func=AF.Sigmoid,
                                                                 bias=scol("rw_a0", (layer * 2 + d) * 4 + p)),
                     reads=[bps[b], bc], writes=[bad])
                P.op("dve", lambda e: e.tensor_scalar(out=sgm, in0=sgm, scalar1=NEG_EXP_HALF, scalar2=None, op0=ALU.mult),
                     reads=[bsgm], writes=[bsgm])
                P.op("dve", lambda e: e.tensor_tensor_scan(out=cum, data0=rmask, data1=sgm, initial=0.0, op0=ALU.mult, op1=ALU.add),
                     reads=[brm, bsgm], writes=[bcum])
                cum3 = cum.rearrange("p (c l) -> p c l", l=128)
                ltot = cum3[:, :, 127:128]
                P.op("act", lambda e, ltot=ltot: e.activation(out=glt.unsqueeze(2), in_=ltot, func=AF.Exp), reads=[bcum], writes=[bglt])
                P.dma(X.RGL[d, pc, blkk * CB:(blkk + 1) * CB], glt, reads=[bglt])
                if d == 0:
                    P.op("pool", lambda e: e.tensor_tensor(out=et, in0=cum, in1=sgm, op=ALU.subtract), reads=[bcum, bsgm], writes=[bet])
                    cc, bcc = cum, bcum
                else:
                    P.op("pool", lambda e, ltot=ltot, cum3=cum3: e.tensor_tensor(
                        out=et.rearrange("p (c l) -> p c l", l=128), in0=ltot.broadcast_to([128, CB, 128]), in1=cum3, op=ALU.subtract),
                        reads=[bcum], writes=[bet])
                    P.op("pool", lambda e: e.tensor_tensor(out=ct, in0=et, in1=sgm, op=ALU.add), reads=[bet, bsgm], writes=[bct])
                    cc, bcc = ct, bct
                P.op("act", lambda e: e.activation(out=E1, in_=et, func=AF.Exp), reads=[bet], writes=[bE1])
                P.op("act", lambda e, cc=cc: e.activation(out=E2, in_=cc, func=AF.Exp, scale=-1.0), reads=[bcc], writes=[bE2])
                P.op("act", lambda e, cc=cc: e.activation(out=E3, in_=cc, func=AF.Exp), reads=[bcc], writes=[bE3])
                P.op("dve", lambda e: e.scalar_tensor_tensor(out=E1, in0=kk, scalar=-1.0, in1=E1, op0=ALU.mult, op1=ALU.mult),
                     reads=[bkk, bE1], writes=[bE1])
                P.op("pool", lambda e: e.tensor_tensor(out=E3, in0=ur, in1=E3, op=ALU.mult), reads=[bur, bE3], writes=[bE3])
                P.op("dve", lambda e: e.tensor_tensor(out=be_, in0=kk, in1=ad, op=ALU.mult), reads=[bkk, bad], writes=[bbe])
                P.op("pool", lambda e: e.tensor_tensor(out=be_, in0=be_, in1=E2, op=ALU.mult), reads=[bbe, bE2], writes=[bbe])
                P.op("dve", lambda e, p=p: e.tensor_scalar(out=ad, in0=ad, scalar1=scol("rw_ka", layer * 4 + p), scalar2=oka[:, p:p + 1],
                                                           op0=ALU.mult, op1=ALU.add), reads=[bad, bc, bmu], writes=[bad])
                P.op("dve", lambda e: e.tensor_tensor(out=kd, in0=uk, in1=ad, op=ALU.mult), reads=[buk, bad], writes=[bkd])
                if d == 0:
                    P.op("pool", lambda e: e.tensor_copy(out=kbs, in_=kd), reads=[bkd], writes=[bkbs])
                else:
                    P.op("pool", lambda e: e.tensor_tensor(out=kbs, in0=kbs, in1=kd, op=ALU.add), reads=[bkd, bkbs], writes=[bkbs])
                P.op("dve", lambda e: e.tensor_tensor(out=kd, in0=kd, in1=E2, op=ALU.mult), reads=[bkd, bE2], writes=[bkd])
                for q, (src, bsrc) in enumerate(((E1, bE1), (E3, bE3), (be_, bbe), (kd, bkd))):
                    P.dma(X.RWQ[d, pc, blkk * CB:(blkk + 1) * CB, q, :], src.rearrange("p (c l) -> p c l", l=128), reads=[bsrc])
            P.op("dve", lambda e, p=p: e.scalar_tensor_tensor(out=kbs, in0=kbs, scalar=hrk[:, p:p + 1], in1=ur, op0=ALU.mult, op1=ALU.mult),
                 reads=[bkbs, bur, bmu], writes=[bkbs])
            P.dma(X.RKB[pc, t0:t0 + TB], kbs, reads=[bkbs])
    P.barrier()
    A.reset(m1)
    if os.environ.get("RW_STOP") == "A":
        A.reset(m0)
        return
    GL = A.alloc(4 * NT)
    bGL = Buf()
    qt = [A.alloc(2048) for _ in range(2)]
    bqt = [Buf() for _ in range(2)]
    vin = [A.alloc(512) for _ in range(2)]
    bvin = [Buf() for _ in range(2)]
    OFFS["Pb"] = A.off
    Pb = A.alloc(1024)
    bPb = [Buf() for _ in range(2)]
    OFFS["QT"] = A.off
    QT = A.alloc(2048)
    bQ = [Buf() for _ in range(2)]
    bT = [Buf() for _ in range(2)]
    ARB = A.alloc(1024)
    bARB = Buf()
    Qf = A.alloc(1024)
    bQf = [Buf() for _ in range(2)]
    PD = A.alloc(2048)
    bPk = [Buf() for _ in range(2)]
    bD = [Buf() for _ in range(2)]
    Ws = [A.alloc(1024) for _ in range(2)]
    bWs = [[Buf() for _ in range(2)] for _ in range(2)]
    Ls = [A.alloc(1024) for _ in range(2)]
    bLs = [[Buf() for _ in range(2)] for _ in range(2)]
    AK = A.alloc(2048)
    bAK = Buf()
    vtm = A.alloc(512)
    bvtm = Buf()
    btmk = A.alloc(1024)
    bbtmk = Buf()
    Xs = A.alloc(512)
    bXs = Buf()
    Us = A.alloc(512)
    bUs = Buf()
    Hst = [A.alloc(512) for _ in range(2)]
    qm = A.alloc(2048)
    bqm = Buf()
    bH = [Buf() for _ in range(2)]
    Ysb = [A.alloc(512) for _ in range(2)]
    bY = [Buf() for _ in range(2)]
    y0 = [A.alloc(512) for _ in range(2)]
    by0 = [Buf() for _ in range(2)]
    Pv = Pb.rearrange("p (h l) -> p h l", h=8)
    QTv = QT.rearrange("p (h a l) -> p h a l", h=8, a=2)
    AKv = AK.rearrange("p (h a l) -> p h a l", h=8, a=2)
    ARBv = ARB.rearrange("p (h l) -> p h l", h=8)
    Qfv = Qf.rearrange("p (h l) -> p h l", h=8)
    PDv = PD.rearrange("p (h a l) -> p h a l", h=8, a=2)
    lnwb = A.alloc(1024)
    blnwb = Buf()
    P.dma(lnwb.rearrange("p (a c) -> p a c", a=2), X.rw_lnwb[layer:layer + 1].broadcast_to([128, 2, 512]), writes=[blnwb])
    rkbin = [A.alloc(512) for _ in range(2)]
    brkbin = [Buf() for _ in range(2)]
    gin = [A.alloc(512) for _ in range(2)]
    bgin = [Buf() for _ in range(2)]
    st8 = A.alloc(64)
    bst8 = Buf()
    ysq = A.alloc(512)
    bysq = Buf()
    oT = A.alloc(512)
    boT = Buf()
    for d in range(2):
        maskP = low4 if d == 0 else up4
        maskQ2 = upupi2 if d == 0 else lowlowi2
        P.dma(GL.rearrange("p (a c) -> p a c", a=4), X.RGL[d].rearrange("(a p) c -> p a c", p=128), writes=[bGL])
        cur = 0
        P.op("pool", lambda e: e.memset(Hst[0], 0.0), writes=[bH[0]])
        P.op("pool", lambda e: e.memset(Hst[1], 0.0), writes=[bH[1]])
        order = range(NT) if d == 0 else range(NT - 1, -1, -1)
        for ci_, c in enumerate(order):
            if d * NT + ci_ >= int(os.environ.get("RW_NCH", "1000")):
                break
            j = c % 2
            cs = slice(c * 128, (c + 1) * 128)
            P.dma(qt[j].rearrange("p (a q l) -> p a q l", a=4, q=4), X.RWQ[d, :, c, :, :].rearrange("(a p) q l -> p a q l", p=128), writes=[bqt[j]])
            P.dma(vin[j].rearrange("p (a l) -> p a l", a=4), X.RVT[:, cs].rearrange("(a p) l -> p a l", p=128), writes=[bvin[j]])
            q4 = qt[j].rearrange("p (a q l) -> p a q l", a=4, q=4)

            def hq(h, q0, q1=None, q4=q4):
                a, half = h // 2, h % 2
                rows = slice(half * 64, half * 64 + 64)
                if q1 is None:
                    return q4[rows, a, q0, :]
                return q4[rows, a, q0:q1, :].rearrange("p q l -> p (q l)")
            qmv = qm.rearrange("p (x a q l) -> p x a q l", x=2, a=4, q=2)
            for half in range(2):
                P.op("dve" if half == 0 else "pool", lambda e, half=half, q4=q4, qmv=qmv: e.tensor_scalar(
                    out=qmv[:, half], in0=q4[:, :, 2:4, :], scalar1=halfsel[:, half:half + 1], scalar2=None, op0=ALU.mult),
                    reads=[bqt[j], bc], writes=[bqm])
            b = nb()
            for a in range(4):
                P.op("pe", lambda e, b=b, a=a, j=j: e.matmul(ps[b][:, a * 128:(a + 1) * 128], vin[j][:, a * 128:(a + 1) * 128], ident,
                                                             start=True, stop=True), reads=[bvin[j], bc], writes=[bps[b]])
            P.op("act", lambda e, b=b: e.activation(out=vtm, in_=ps[b], func=AF.Copy), reads=[bps[b]], writes=[bvtm])
            for qi, q in enumerate((2, 3)):
                b = nb()
                for a in range(4):
                    P.op("pe", lambda e, b=b, a=a, q=q, q4=q4: e.matmul(ps[b][:, a * 128:(a + 1) * 128], q4[:, a, q, :], ident,
                                                                       start=True, stop=True), reads=[bqt[j], bc], writes=[bps[b]])
                eng = "dve" if qi == 0 else "act"
                if eng == "dve":
                    P.op("dve", lambda e, b=b, qi=qi: e.tensor_copy(out=btmk[:, qi * 512:(qi + 1) * 512], in_=ps[b]), reads=[bps[b]], writes=[bbtmk])
                else:
                    P.op("act", lambda e, b=b, qi=qi: e.activation(out=btmk[:, qi * 512:(qi + 1) * 512], in_=ps[b], func=AF.Copy),
                         reads=[bps[b]], writes=[bbtmk])
            for gi in range(2):
                hs = range(gi * 4, gi * 4 + 4)
                b = nb()
                for h in hs:
                    P.op("pe", lambda e, b=b, h=h, q4=q4, qmv=qmv: e.matmul(ps[b][:, (h % 4) * 128:(h % 4 + 1) * 128], q4[:, h // 2, 0, :],
                                                                            qmv[:, h % 2, h // 2, 0, :], start=True, stop=True),
                         reads=[bqt[j], bqm], writes=[bps[b]])
                P.op("dve", lambda e, b=b, gi=gi, maskP=maskP: e.tensor_tensor(out=Pb[:, gi * 512:(gi + 1) * 512], in0=ps[b], in1=maskP, op=ALU.mult),
                     reads=[bps[b], bc], writes=[bPb[gi]])
                for hp in range(2):
                    h0 = gi * 4 + hp * 2
                    b = nb()
                    for hh in range(2):
                        h = h0 + hh
                        P.op("pe", lambda e, b=b, h=h, hh=hh, q4=q4, qmv=qmv: e.matmul(
                            ps[b][:, hh * 256:(hh + 1) * 256], qmv[:, h % 2, h // 2, 0, :], q4[:, h // 2, 0:2, :].rearrange("p q l -> p (q l)"),
                            start=True, stop=True), reads=[bqt[j], bqm], writes=[bps[b]])
                    pv = ps[b].rearrange("p (h a l) -> p h a l", h=2, a=2)
                    mv = maskQ2.rearrange("p (h a l) -> p h a l", h=2, a=2)
                    P.op("dve", lambda e, pv=pv, mv=mv, h0=h0: e.tensor_tensor(out=Qfv[:, h0:h0 + 2, :], in0=pv[:, :, 0, :], in1=mv[:, :, 0, :],
                                                                               op=ALU.mult), reads=[bps[b], bc], writes=[bQf[gi]])
                    P.op("dve", lambda e, pv=pv, mv=mv, h0=h0: e.tensor_tensor(out=ARBv[:, h0:h0 + 2, :], in0=pv[:, :, 1, :], in1=mv[:, :, 1, :],
                                                                               op=ALU.mult), reads=[bps[b], bc], writes=[bARB])
                    b = nb()
                    for hh in range(2):
                        h = h0 + hh
                        P.op("pe", lambda e, b=b, h=h, hh=hh, q4=q4, qmv=qmv: e.matmul(
                            ps[b][:, hh * 256:(hh + 1) * 256], qmv[:, h % 2, h // 2, 1, :], q4[:, h // 2, 0:2, :].rearrange("p q l -> p (q l)"),
                            start=True, stop=True), reads=[bqt[j], bqm], writes=[bps[b]])
                    P.op("dve", lambda e, b=b, h0=h0, maskQ2=maskQ2: e.tensor_tensor(out=AK[:, h0 * 256:(h0 + 2) * 256], in0=ps[b], in1=maskQ2, op=ALU.mult),
                         reads=[bps[b], bc], writes=[bAK])
            if os.environ.get("RW_DBG") == "1":
                P.dma(X.DBG[:, 0:1024], Pb, reads=bPb)
                P.dma(X.DBG[:, 1024:3072], QT, reads=bQ + bT)
                P.dma(X.DBG[:, 3072:5120], AK, reads=[bAK])
                P.dma(X.DBG[:, 5120:6144], ARB, reads=[bARB])
                P.barrier(); A.reset(m0); return
            lo_ = d == 0
            bdP = cs_["rw_bd_low4"] if lo_ else cs_["rw_bd_up4"]
            bdQ = cs_["rw_bd_up4"] if lo_ else cs_["rw_bd_low4"]
            lvP = [cs_["rw_l32_low4"] if lo_ else cs_["rw_l32_up4"], cs_["rw_l64_low4"] if lo_ else cs_["rw_l64_up4"]]
            lvQ = [cs_["rw_l32_up4"] if lo_ else cs_["rw_l32_low4"], None]
            id4 = ident.unsqueeze(1).broadcast_to([128, 4, 128])
            for gi in range(2):
                g4 = slice(gi * 4, gi * 4 + 4)
                gs = slice(gi * 512, (gi + 1) * 512)
                P.op("dve", lambda e, g4=g4, gs=gs, bdP=bdP: e.tensor_tensor(out=PDv[:, g4, 0, :], in0=Pb[:, gs].rearrange("p (h l) -> p h l", h=4),
                                                                             in1=bdP.rearrange("p (h l) -> p h l", h=4), op=ALU.mult),
                     reads=[bPb[gi], bc], writes=[bPk[gi]])
                P.op("pool", lambda e, g4=g4, gs=gs, bdQ=bdQ: e.tensor_tensor(out=QTv[:, g4, 0, :], in0=Qf[:, gs].rearrange("p (h l) -> p h l", h=4),
                                                                              in1=bdQ.rearrange("p (h l) -> p h l", h=4), op=ALU.mult),
                     reads=[bQf[gi], bc], writes=[bQ[gi]])
                P.op("pool", lambda e, g4=g4: e.tensor_tensor(out=PDv[:, g4, 1, :], in0=PDv[:, g4, 0, :], in1=id4, op=ALU.add),
                     reads=[bPk[gi], bc], writes=[bD[gi]])
                P.op("dve", lambda e, g4=g4: e.tensor_tensor(out=QTv[:, g4, 1, :], in0=QTv[:, g4, 0, :], in1=id4, op=ALU.add),
                     reads=[bQ[gi], bc], writes=[bT[gi]])
            for step in range(1, 6):
                for gi in range(2):
                    g4 = slice(gi * 4, gi * 4 + 4)
                    if step in (1, 5):
                        bA, bB = nb(), nb()
                        for h in range(gi * 4, gi * 4 + 4):
                            hl = h % 4
                            oa = ps[bA][:, hl * 128:(hl + 1) * 128]
                            ob = ps[bB][:, hl * 128:(hl + 1) * 128]
                            ra = QTv[:, h, 0, :] if step == 1 else QTv[:, h, 1, :]
                            rb = PDv[:, h, 0, :] if step == 1 else PDv[:, h, 1, :]
                            P.op("pe", lambda e, oa=oa, h=h, ra=ra: e.matmul(oa, PDv[:, h, 0, :], ra, start=True, stop=True),
                                 reads=[bPk[gi], bQ[gi], bT[gi]], writes=[bps[bA]])
                            P.op("pe", lambda e, ob=ob, h=h, rb=rb: e.matmul(ob, QTv[:, h, 0, :], rb, start=True, stop=True),
                                 reads=[bQ[gi], bPk[gi], bD[gi]], writes=[bps[bB]])
                        pa = ps[bA].rearrange("p (h l) -> p h l", h=4)
                        pb_ = ps[bB].rearrange("p (h l) -> p h l", h=4)
                        if step == 1:
                            P.op("dve", lambda e, pa=pa, g4=g4: e.tensor_copy(out=QTv[:, g4, 0, :], in_=pa), reads=[bps[bA]], writes=[bQ[gi]])
                            P.op("act", lambda e, pb_=pb_, g4=g4: e.activation(out=PDv[:, g4, 0, :], in_=pb_, func=AF.Copy), reads=[bps[bB]], writes=[bPk[gi]])
                        else:
                            P.op("dve", lambda e, pa=pa, g4=g4: e.tensor_tensor(out=QTv[:, g4, 1, :], in0=pa, in1=QTv[:, g4, 1, :], op=ALU.add),
                                 reads=[bps[bA], bT[gi]], writes=[bT[gi]])
                            P.op("dve", lambda e, pb_=pb_, g4=g4: e.tensor_tensor(out=PDv[:, g4, 1, :], in0=pb_, in1=PDv[:, g4, 1, :], op=ALU.add),
                                 reads=[bps[bB], bD[gi]], writes=[bD[gi]])
                    else:
                        bQD = nb2()
                        bPD_ = nb2()
                        for h in range(gi * 4, gi * 4 + 4):
                            hl = h % 4
                            oq = ps[bQD + hl // 2][:, (hl % 2) * 256:(hl % 2 + 1) * 256]
                            op_ = ps[bPD_ + hl // 2][:, (hl % 2) * 256:(hl % 2 + 1) * 256]
                            P.op("pe", lambda e, oq=oq, h=h: e.matmul(oq, PDv[:, h, 0, :], QT[:, h * 256:(h + 1) * 256], start=True, stop=True),
                                 reads=[bPk[gi], bQ[gi], bT[gi]], writes=[bps[bQD + hl // 2]])
                            P.op("pe", lambda e, op_=op_, h=h: e.matmul(op_, QTv[:, h, 0, :], PD[:, h * 256:(h + 1) * 256], start=True, stop=True),
                                 reads=[bQ[gi], bPk[gi], bD[gi]], writes=[bps[bPD_ + hl // 2]])
                        for k2 in range(2):
                            h0 = gi * 4 + k2 * 2
                            pq = ps[bQD + k2].rearrange("p (h a l) -> p h a l", h=2, a=2)
                            pp = ps[bPD_ + k2].rearrange("p (h a l) -> p h a l", h=2, a=2)
                            P.op("dve", lambda e, pq=pq, h0=h0: e.tensor_copy(out=QTv[:, h0:h0 + 2, 0, :], in_=pq[:, :, 0, :]),
                                 reads=[bps[bQD + k2]], writes=[bQ[gi]])
                            P.op("dve", lambda e, pq=pq, h0=h0: e.tensor_tensor(out=QTv[:, h0:h0 + 2, 1, :], in0=pq[:, :, 1, :], in1=QTv[:, h0:h0 + 2, 1, :],
                                                                                op=ALU.add), reads=[bps[bQD + k2], bT[gi]], writes=[bT[gi]])
                            P.op("act", lambda e, pp=pp, h0=h0: e.activation(out=PDv[:, h0:h0 + 2, 0, :], in_=pp[:, :, 0, :], func=AF.Copy),
                                 reads=[bps[bPD_ + k2]], writes=[bPk[gi]])
                            P.op("dve", lambda e, pp=pp, h0=h0: e.tensor_tensor(out=PDv[:, h0:h0 + 2, 1, :], in0=pp[:, :, 1, :], in1=PDv[:, h0:h0 + 2, 1, :],
                                                                                op=ALU.add), reads=[bps[bPD_ + k2], bD[gi]], writes=[bD[gi]])
            for lv in range(2):
                for gi in range(2):
                    g4 = slice(gi * 4, gi * 4 + 4)
                    gs = slice(gi * 512, (gi + 1) * 512)
                    mP, mQ = lvP[lv], lvQ[lv]
                    P.op("pool", lambda e, gs=gs, mP=mP: e.tensor_tensor(out=Ls[0][:, gs], in0=Pb[:, gs], in1=mP, op=ALU.mult),
                         reads=[bPb[gi], bc], writes=[bLs[0][gi]])
                    bA = nb()
                    for h in range(gi * 4, gi * 4 + 4):
                        hl = h % 4
                        P.op("pe", lambda e, bA=bA, h=h, hl=hl: e.matmul(ps[bA][:, hl * 128:(hl + 1) * 128], Ls[0][:, h * 128:(h + 1) * 128], QTv[:, h, 1, :],
                                                                       start=True, stop=True), reads=[bLs[0][gi], bT[gi]], writes=[bps[bA]])
                    P.op("act", lambda e, bA=bA, gs=gs: e.activation(out=Ws[0][:, gs], in_=ps[bA], func=AF.Copy), reads=[bps[bA]], writes=[bWs[0][gi]])
                    if lv == 0:
                        P.op("pool", lambda e, gs=gs, mQ=mQ: e.tensor_tensor(out=Ls[1][:, gs], in0=Qf[:, gs], in1=mQ, op=ALU.mult),
                             reads=[bQf[gi], bc], writes=[bLs[1][gi]])
                        bB = nb()
                        for h in range(gi * 4, gi * 4 + 4):
                            hl = h % 4
                            P.op("pe", lambda e, bB=bB, h=h, hl=hl: e.matmul(ps[bB][:, hl * 128:(hl + 1) * 128], Ls[1][:, h * 128:(h + 1) * 128], PDv[:, h, 1, :],
                                                                           start=True, stop=True), reads=[bLs[1][gi], bD[gi]], writes=[bps[bB]])
                        P.op("dve", lambda e, bB=bB, gs=gs: e.tensor_copy(out=Ws[1][:, gs], in_=ps[bB]), reads=[bps[bB]], writes=[bWs[1][gi]])
                    bA2 = nb()
                    for h in range(gi * 4, gi * 4 + 4):
                        hl = h % 4
                        P.op("pe", lambda e, bA2=bA2, h=h, hl=hl: e.matmul(ps[bA2][:, hl * 128:(hl + 1) * 128], PDv[:, h, 1, :], Ws[0][:, h * 128:(h + 1) * 128],
                                                                         start=True, stop=True), reads=[bD[gi], bWs[0][gi]], writes=[bps[bA2]])
                    if lv == 0:
                        bB2 = nb()
                        for h in range(gi * 4, gi * 4 + 4):
                            hl = h % 4
                            P.op("pe", lambda e, bB2=bB2, h=h, hl=hl: e.matmul(ps[bB2][:, hl * 128:(hl + 1) * 128], QTv[:, h, 1, :], Ws[1][:, h * 128:(h + 1) * 128],
                                                                             start=True, stop=True), reads=[bT[gi], bWs[1][gi]], writes=[bps[bB2]])
                    P.op("dve", lambda e, bA2=bA2, g4=g4: e.tensor_tensor(out=QTv[:, g4, 1, :], in0=ps[bA2].rearrange("p (h l) -> p h l", h=4),
                                                                          in1=QTv[:, g4, 1, :], op=ALU.add), reads=[bps[bA2], bT[gi]], writes=[bT[gi]])
                    if lv == 0:
                        P.op("dve", lambda e, bB2=bB2, g4=g4: e.tensor_tensor(out=PDv[:, g4, 1, :], in0=ps[bB2].rearrange("p (h l) -> p h l", h=4),
                                                                              in1=PDv[:, g4, 1, :], op=ALU.add), reads=[bps[bB2], bD[gi]], writes=[bD[gi]])
            if os.environ.get("RW_DBG") == "3":
                P.dma(X.DBG[:, 0:1024], Pb, reads=bPb)
                P.dma(X.DBG[:, 1024:3072], QT, reads=bQ + bT)
                P.barrier(); A.reset(m0); return
            Hc, bHc = Hst[cur], bH[cur]
            Hn, bHn = Hst[1 - cur], bH[1 - cur]
            bX = nb()
            for a in range(4):
                P.op("pe", lambda e, a=a, q4=q4, Hc=Hc, bX=bX: e.matmul(ps[bX][:, a * 128:(a + 1) * 128], q4[:, a, 0, :], Hc[:, a * 128:(a + 1) * 128],
                                                                      start=True, stop=False), reads=[bqt[j], bHc], writes=[bps[bX]])
                for half in range(2):
                    h = a * 2 + half
                    P.op("pe", lambda e, h=h, half=half, bX=bX: e.matmul(ps[bX][:, h * 64:(h + 1) * 64], AKv[:, h, 0, :], vtm[:, h * 64:(h + 1) * 64],
                                                                       start=False, stop=(half == 1)), reads=[bAK, bvtm], writes=[bps[bX]])
            P.op("act", lambda e, bX=bX: e.activation(out=Xs, in_=ps[bX], func=AF.Copy), reads=[bps[bX]], writes=[bXs])
            if os.environ.get('RW_B3') == '1':
                P.barrier(); A.reset(m0); return
            bU = nb()
            for h in range(8):
                P.op("pe", lambda e, bU=bU, h=h: e.matmul(ps[bU][:, h * 64:(h + 1) * 64], QTv[:, h, 1, :], Xs[:, h * 64:(h + 1) * 64],
                                                          start=True, stop=True), reads=[bT[h // 4], bXs], writes=[bps[bU]])
            P.op("dve", lambda e, bU=bU: e.tensor_copy(out=Us, in_=ps[bU]), reads=[bps[bU]], writes=[bUs])
            if os.environ.get('RW_B3') == '2':
                P.barrier(); A.reset(m0); return
            bYp = nb()
            for a in range(4):
                P.op("pe", lambda e, a=a, q4=q4, Hc=Hc, bYp=bYp: e.matmul(ps[bYp][:, a * 128:(a + 1) * 128], q4[:, a, 1, :], Hc[:, a * 128:(a + 1) * 128],
                                                                        start=True, stop=False), reads=[bqt[j], bHc], writes=[bps[bYp]])
                for half in range(2):
                    h = a * 2 + half
                    o = ps[bYp][:, h * 64:(h + 1) * 64]
                    P.op("pe", lambda e, o=o, h=h: e.matmul(o, ARBv[:, h, :], Us[:, h * 64:(h + 1) * 64], start=False, stop=False),
                         reads=[bARB, bUs], writes=[bps[bYp]])
                    P.op("pe", lambda e, o=o, h=h, half=half: e.matmul(o, AKv[:, h, 1, :], vtm[:, h * 64:(h + 1) * 64], start=False, stop=(half == 1)),
                         reads=[bAK, bvtm], writes=[bps[bYp]])
            if os.environ.get('RW_B3') == '3':
                P.barrier(); A.reset(m0); return
            bHp = nb()
            for a in range(4):
                o = ps[bHp][:, a * 128:(a + 1) * 128]
                P.op("pe", lambda e, o=o, a=a: e.matmul(o, btmk[:, a * 128:(a + 1) * 128], Us[:, a * 128:(a + 1) * 128], start=True, stop=False),
                     reads=[bbtmk, bUs], writes=[bps[bHp]])
                P.op("pe", lambda e, o=o, a=a: e.matmul(o, btmk[:, 512 + a * 128:512 + (a + 1) * 128], vtm[:, a * 128:(a + 1) * 128],
                                                        start=False, stop=True), reads=[bbtmk, bvtm], writes=[bps[bHp]])
            for half in range(2):
                rows = slice(half * 64, half * 64 + 64)
                pin = ps[bHp].rearrange("p (a x i) -> p a x i", a=4, x=2)[rows, :, half, :]
                P.op("dve", lambda e, pin=pin, rows=rows, half=half, Hn=Hn, Hc=Hc: e.tensor_tensor(
                    out=Hn.rearrange("p (a x i) -> p a x i", a=4, x=2)[rows, :, half, :], in0=pin,
                    in1=Hc.rearrange("p (a x i) -> p a x i", a=4, x=2)[rows, :, half, :], op=ALU.add),
                    reads=[bps[bHp], bHc], writes=[bHn])
            P.op("dve", lambda e, Hn=Hn, c=c: e.tensor_tensor(
                out=Hn.rearrange("p (a i) -> p a i", a=4), in0=Hn.rearrange("p (a i) -> p a i", a=4),
                in1=GL.rearrange("p (a c) -> p a c", a=4)[:, :, c:c + 1].broadcast_to([128, 4, 128]), op=ALU.mult),
                reads=[bHn, bGL], writes=[bHn])
            cur = 1 - cur
            if d == 0:
                P.op("act", lambda e, bYp=bYp, j=j: e.activation(out=Ysb[j], in_=ps[bYp], func=AF.Copy), reads=[bps[bYp]], writes=[bY[j]])
                P.dma(X.RY0[cs, :], Ysb[j], reads=[bY[j]])
                continue
            P.dma(y0[j], X.RY0[cs, :], writes=[by0[j]])
            P.dma(rkbin[j].rearrange("p (a l) -> p a l", a=4), X.RKB[:, cs].rearrange("(a p) l -> p a l", p=128), writes=[brkbin[j]])
            P.dma(gin[j].rearrange("p (a l) -> p a l", a=4), X.RGT[:, cs].rearrange("(a p) l -> p a l", p=128), writes=[bgin[j]])
            Y = Ysb[j]
            bYj = bY[j]
            P.op("dve", lambda e, bYp=bYp, Y=Y, j=j: e.tensor_tensor(out=Y, in0=ps[bYp], in1=y0[j], op=ALU.add),
                 reads=[bps[bYp], by0[j]], writes=[bYj])
            Y3 = Y.rearrange("p (h i) -> p h i", h=8)
            s8 = st8.rearrange("p (k h) -> p k h", k=8)
            P.op("dve", lambda e, Y3=Y3, s8=s8: e.tensor_reduce(out=s8[:, 0, :], in_=Y3, axis=AX.X, op=ALU.add), reads=[bYj], writes=[bst8])
            P.op("act", lambda e, Y=Y: e.activation(out=ysq, in_=Y, func=AF.Square), reads=[bYj], writes=[bysq])
            P.op("dve", lambda e, s8=s8: e.tensor_reduce(out=s8[:, 1, :], in_=ysq.rearrange("p (h i) -> p h i", h=8), axis=AX.X, op=ALU.add),
                 reads=[bysq], writes=[bst8])
            P.op("dve", lambda e, s8=s8: e.tensor_scalar(out=s8[:, 2, :], in0=s8[:, 0, :], scalar1=1.0 / 64, scalar2=None, op0=ALU.mult),
                 reads=[bst8], writes=[bst8])
            P.op("dve", lambda e, s8=s8: e.tensor_tensor(out=s8[:, 5, :], in0=s8[:, 2, :], in1=s8[:, 2, :], op=ALU.mult), reads=[bst8], writes=[bst8])
            P.op("dve", lambda e, s8=s8: e.scalar_tensor_tensor(out=s8[:, 3, :], in0=s8[:, 1, :], scalar=1.0 / 64, in1=s8[:, 5, :],
                                                                op0=ALU.mult, op1=ALU.subtract), reads=[bst8], writes=[bst8])
            P.op("dve", lambda e, s8=s8: e.tensor_scalar(out=s8[:, 3, :], in0=s8[:, 3, :], scalar1=GN_EPS, scalar2=None, op0=ALU.add),
                 reads=[bst8], writes=[bst8])
            P.op("act", lambda e, s8=s8: e.activation(out=s8[:, 3, :], in_=s8[:, 3, :], func=AF.Sqrt), reads=[bst8], writes=[bst8])
            P.op("dve", lambda e, s8=s8: e.reciprocal(out=s8[:, 3, :], in_=s8[:, 3, :]), reads=[bst8], writes=[bst8])
            P.op("dve", lambda e, Y3=Y3, s8=s8: e.tensor_tensor(out=Y3, in0=Y3, in1=s8[:, 2, :].unsqueeze(2).broadcast_to([128, 8, 64]),
                                                                op=ALU.subtract), reads=[bYj, bst8], writes=[bYj])
            P.op("dve", lambda e, Y3=Y3, s8=s8: e.tensor_tensor(out=Y3, in0=Y3, in1=s8[:, 3, :].unsqueeze(2).broadcast_to([128, 8, 64]),
                                                                op=ALU.mult), reads=[bYj, bst8], writes=[bYj])
            P.op("pool", lambda e, Y=Y: e.tensor_tensor(out=Y, in0=Y, in1=lnwb[:, 0:512], op=ALU.mult), reads=[bYj, blnwb], writes=[bYj])
            P.op("pool", lambda e, Y=Y: e.tensor_tensor(out=Y, in0=Y, in1=lnwb[:, 512:1024], op=ALU.add), reads=[bYj, blnwb], writes=[bYj])
            bB = nb()
            for a in range(4):
                P.op("pe", lambda e, bB=bB, a=a, j=j: e.matmul(ps[bB][:, a * 2:(a + 1) * 2], rkbin[j][:, a * 128:(a + 1) * 128], halfsel,
                                                               start=True, stop=True), reads=[brkbin[j], bc], writes=[bps[bB]])
            P.op("act", lambda e, bB=bB, s8=s8: e.activation(out=s8[:, 4, :], in_=ps[bB][:, 0:8], func=AF.Copy), reads=[bps[bB]], writes=[bst8])
            P.op("dve", lambda e, s8=s8: e.tensor_tensor(out=ysq.rearrange("p (h i) -> p h i", h=8), in0=vtm.rearrange("p (h i) -> p h i", h=8),
                                                         in1=s8[:, 4, :].unsqueeze(2).broadcast_to([128, 8, 64]), op=ALU.mult),
                 reads=[bvtm, bst8, bysq], writes=[bysq])
            P.op("pool", lambda e, Y=Y: e.tensor_tensor(out=Y, in0=Y, in1=ysq, op=ALU.add), reads=[bYj, bysq], writes=[bYj])
            bO = nb()
            for a in range(4):
                P.op("pe", lambda e, bO=bO, a=a, Y=Y: e.matmul(ps[bO][:, a * 128:(a + 1) * 128], Y[:, a * 128:(a + 1) * 128], ident,
                                                               start=True, stop=True), reads=[bYj, bc], writes=[bps[bO]])
            P.op("dve", lambda e, bO=bO, j=j: e.tensor_tensor(out=oT, in0=ps[bO], in1=gin[j], op=ALU.mult), reads=[bps[bO], bgin[j]], writes=[boT])
            P.dma(X.OT[OT_RWKV:OT_RWKV + 512, cs].rearrange("(a p) l -> p a l", p=128), oT.rearrange("p (a l) -> p a l", a=4), reads=[boT])
        P.barrier()
    A.reset(m0)


def stage_out(X, layer):
    nc, P, A = X.nc, X.P, X.A
    T = X.T
    NTB = T // 512
    so, small, bc = X.so, X.small, X.bconst
    ps, bps = X.ps, X.bps
    ones = X.ones
    m0 = A.mark()
    psi = [0]

    def nb():
        b = psi[0] % 8
        psi[0] += 1
        return b
    x1 = A.alloc(8 * 512)
    bx1 = Buf()
    h2 = A.alloc(8 * 512)
    bh2 = Buf()
    R = A.alloc(32 * 512)
    ot = R[:, 0:16 * 512]
    mg = R[:, 16 * 512:24 * 512]
    xb = R[:, 24 * 512:32 * 512]
    sq = R[:, 0:8 * 512]
    u = R
    bot, bmg, bxb, bsq, bu = Buf(), Buf(), Buf(), Buf(), Buf()
    wbs = [A.alloc(16 * 128) for _ in range(2)]
    bwbs = [Buf() for _ in range(2)]
    wo = [A.alloc(8 * 128) for _ in range(2)]
    bwo = [Buf() for _ in range(2)]
    w1 = [A.alloc(8 * 128) for _ in range(3)]
    bw1 = [Buf() for _ in range(3)]
    w2s = [A.alloc(32 * 128) for _ in range(2)]
    bw2s = [Buf() for _ in range(2)]
    gt = [A.alloc(512) for _ in range(3)]
    bgt = [Buf() for _ in range(3)]
    tmp = A.alloc(512)
    btmp = Buf()
    rstd = A.alloc(512)
    brstd = Buf()
    KT = (4, 8, 4)
    K0 = (0, 4, 12)
    gi_ = 0
    w1i = 0
    for tb in range(NTB):
        t0 = tb * 512
        ts = slice(t0, t0 + 512)
        P.dma(ot.rearrange("p (k t) -> p k t", k=16), X.OT[:, ts].rearrange("(k p) t -> p k t", p=128), writes=[bot])
        P.dma(xb.rearrange("p (k t) -> p k t", k=8), X.xcur[:, ts].rearrange("(k p) t -> p k t", p=128), writes=[bxb])
        for m in range(8):
            wb, bwb = wbs[m % 2], bwbs[m % 2]
            P.dma(wb.rearrange("p (k c) -> p k c", k=16), X.w_branch[layer, :, m * 128:(m + 1) * 128].rearrange("(k p) c -> p k c", p=128),
                  writes=[bwb])
            for i in range(3):
                g = gi_ % 3
                gi_ += 1
                r0 = C_GATE + i * 1024 + m * 128
                P.dma(gt[g], X.PT[r0:r0 + 128, ts], writes=[bgt[g]])
                P.op("act", lambda e, g=g: e.activation(out=gt[g], in_=gt[g], func=AF.Sigmoid), reads=[bgt[g]], writes=[bgt[g]])
                b = nb()
                for kk_ in range(KT[i]):
                    k = K0[i] + kk_
                    P.op("pe", lambda e, b=b, k=k, kk_=kk_, i=i, wb=wb: e.matmul(ps[b], wb[:, k * 128:(k + 1) * 128], ot[:, k * 512:(k + 1) * 512],
                                                                         start=(kk_ == 0), stop=(kk_ == KT[i] - 1)), reads=[bwb, bot], writes=[bps[b]])
                if i == 0:
                    P.op("dve", lambda e, b=b, g=g, m=m: e.tensor_tensor(out=mg[:, m * 512:(m + 1) * 512], in0=ps[b], in1=gt[g], op=ALU.mult),
                         reads=[bps[b], bgt[g]], writes=[bmg])
                else:
                    P.op("dve", lambda e, b=b, g=g: e.tensor_tensor(out=tmp, in0=ps[b], in1=gt[g], op=ALU.mult),
                         reads=[bps[b], bgt[g]], writes=[btmp])
                    P.op("pool", lambda e, m=m: e.tensor_tensor(out=mg[:, m * 512:(m + 1) * 512], in0=mg[:, m * 512:(m + 1) * 512], in1=tmp, op=ALU.add),
                         reads=[btmp, bmg], writes=[bmg])
        for n in range(8):
            j = n % 2
            P.dma(wo[j].rearrange("p (k c) -> p k c", k=8), X.w_out[layer, :, n * 128:(n + 1) * 128].rearrange("(k p) c -> p k c", p=128),
                  writes=[bwo[j]])
            b = nb()
            for k in range(8):
                P.op("pe", lambda e, b=b, k=k, j=j: e.matmul(ps[b], wo[j][:, k * 128:(k + 1) * 128], mg[:, k * 512:(k + 1) * 512],
                                                             start=(k == 0), stop=(k == 7)), reads=[bwo[j], bmg], writes=[bps[b]])
            P.op("dve", lambda e, b=b, n=n: e.tensor_tensor(out=x1[:, n * 512:(n + 1) * 512], in0=ps[b], in1=xb[:, n * 512:(n + 1) * 512], op=ALU.add),
                 reads=[bps[b], bxb], writes=[bx1])
        P.op("act", lambda e: e.activation(out=sq, in_=x1, func=AF.Square), reads=[bx1, bot], writes=[bsq, bot])
        b = nb()
        for k in range(8):
            P.op("pe", lambda e, b=b, k=k: e.matmul(ps[b], ones, sq[:, k * 512:(k + 1) * 512], start=(k == 0), stop=(k == 7)),
                 reads=[bsq, bc], writes=[bps[b]])
        P.op("dve", lambda e, b=b: e.tensor_scalar(out=rstd, in0=ps[b], scalar1=1.0 / D, scalar2=EPS, op0=ALU.mult, op1=ALU.add),
             reads=[bps[b]], writes=[brstd])
        P.op("act", lambda e: e.activation(out=rstd, in_=rstd, func=AF.Sqrt), reads=[brstd], writes=[brstd])
        P.op("dve", lambda e: e.reciprocal(out=rstd, in_=rstd), reads=[brstd], writes=[brstd])
        mno = so["mlp_norm"] + layer * 8
        for k in range(8):
            P.op("dve", lambda e, k=k: e.scalar_tensor_tensor(out=h2[:, k * 512:(k + 1) * 512], in0=x1[:, k * 512:(k + 1) * 512],
                                                              scalar=small[:, mno + k:mno + k + 1], in1=rstd, op0=ALU.mult, op1=ALU.mult),
                 reads=[bx1, brstd, bc], writes=[bh2])
        for f in range(32):
            j = w1i % 3
            w1i += 1
            P.dma(w1[j].rearrange("p (k c) -> p k c", k=8), X.w_mlp_in[layer, :, f * 128:(f + 1) * 128].rearrange("(k p) c -> p k c", p=128),
                  writes=[bw1[j]])
            b = nb()
            for k in range(8):
                P.op("pe", lambda e, b=b, k=k, j=j: e.matmul(ps[b], w1[j][:, k * 128:(k + 1) * 128], h2[:, k * 512:(k + 1) * 512],
                                                             start=(k == 0), stop=(k == 7)), reads=[bw1[j], bh2], writes=[bps[b]])
            uf = u[:, f * 512:(f + 1) * 512]
            P.op("act", lambda e, b=b, uf=uf: e.activation(out=uf, in_=ps[b], func=AF.Relu),
                 reads=[bps[b], bot, bmg, bxb, bsq], writes=[bu])
            P.op("pool" if f % 2 else "dve", lambda e, uf=uf: e.tensor_tensor(out=uf, in0=uf, in1=uf, op=ALU.mult), reads=[bu], writes=[bu])
        for n in range(8):
            w2, bw2 = w2s[n % 2], bw2s[n % 2]
            P.dma(w2.rearrange("p (k c) -> p k c", k=32), X.w_mlp_out[layer, :, n * 128:(n + 1) * 128].rearrange("(k p) c -> p k c", p=128),
                  writes=[bw2])
            b = nb()
            for f in range(32):
                P.op("pe", lambda e, b=b, f=f, w2=w2: e.matmul(ps[b], w2[:, f * 128:(f + 1) * 128], u[:, f * 512:(f + 1) * 512],
                                                        start=(f == 0), stop=(f == 31)), reads=[bw2, bu], writes=[bps[b]])
            P.op("dve", lambda e, b=b, n=n: e.tensor_tensor(out=x1[:, n * 512:(n + 1) * 512], in0=ps[b], in1=x1[:, n * 512:(n + 1) * 512], op=ALU.add),
                 reads=[bps[b], bx1], writes=[bx1])
        P.dma(X.xnext[:, ts].rearrange("(k p) t -> p k t", p=128), x1.rearrange("p (k t) -> p k t", k=8), reads=[bx1],
              writes=[])
        bot.readers.update(bu.readers)
        bxb.readers.update(bu.readers)
        bot.last_w = bu.last_w
        bxb.last_w = bu.last_w
        bmg.last_w = bu.last_w
        bmg.readers.update(bu.readers)
    P.barrier()
    A.reset(m0)


def stage_final(X):
    nc, P, A = X.nc, X.P, X.A
    T = X.T
    NTB = T // 512
    so, small, bc = X.so, X.small, X.bconst
    ps, bps = X.ps, X.bps
    ones = X.ones
    m0 = A.mark()
    xt = [A.alloc(8 * 512) for _ in range(2)]
    bxt = [Buf() for _ in range(2)]
    sq = A.alloc(8 * 512)
    bsq = Buf()
    rstd = A.alloc(512)
    brstd = Buf()
    fo = so["final_norm"]
    for tb in range(NTB):
        j = tb % 2
        ts = slice(tb * 512, (tb + 1) * 512)
        P.dma(xt[j].rearrange("p (k t) -> p k t", k=8), X.xcur[:, ts].rearrange("(k p) t -> p k t", p=128), writes=[bxt[j]])
        P.op("act", lambda e, j=j: e.activation(out=sq, in_=xt[j], func=AF.Square), reads=[bxt[j]], writes=[bsq])
        b = tb % 8
        for k in range(8):
            P.op("pe", lambda e, b=b, k=k: e.matmul(ps[b], ones, sq[:, k * 512:(k + 1) * 512], start=(k == 0), stop=(k == 7)),
                 reads=[bsq, bc], writes=[bps[b]])
        P.op("dve", lambda e, b=b: e.tensor_scalar(out=rstd, in0=ps[b], scalar1=1.0 / D, scalar2=EPS, op0=ALU.mult, op1=ALU.add),
             reads=[bps[b]], writes=[brstd])
        P.op("act", lambda e: e.activation(out=rstd, in_=rstd, func=AF.Sqrt), reads=[brstd], writes=[brstd])
        P.op("dve", lambda e: e.reciprocal(out=rstd, in_=rstd), reads=[brstd], writes=[brstd])
        for k in range(8):
            P.op("dve", lambda e, k=k, j=j: e.scalar_tensor_tensor(out=xt[j][:, k * 512:(k + 1) * 512], in0=xt[j][:, k * 512:(k + 1) * 512],
                                                                   scalar=small[:, fo + k:fo + k + 1], in1=rstd, op0=ALU.mult, op1=ALU.mult),
                 reads=[bxt[j], brstd, bc], writes=[bxt[j]])
        P.dma(X.yT[:, ts].rearrange("(k p) t -> p k t", p=128), xt[j].rearrange("p (k t) -> p k t", k=8), reads=[bxt[j]])
    P.barrier()
    A.reset(m0)


def pack_small(inp):
    so = {}
    cols = []
    off = 0
    so["attn_norm"] = off
    a = inp["attn_norm"].reshape(DEPTH, 8, 128).transpose(2, 0, 1).reshape(128, DEPTH * 8)
    cols.append(a)
    off += a.shape[1]

    def add(name, arr):
        nonlocal off
        arr = np.asarray(arr, np.float32).reshape(128, -1)
        so[name] = off
        cols.append(arr)
        off += arr.shape[1]
    cw = inp["ssm_conv_w"].reshape(DEPTH, 5, 12, 128).transpose(3, 0, 2, 1)
    add("ssm_conv_w", cw)
    add("ssm_conv_b", inp["ssm_conv_b"].reshape(DEPTH, 12, 128).transpose(2, 0, 1))
    dtb = np.zeros((128, DEPTH), np.float32)
    alg = np.zeros((128, DEPTH), np.float32)
    for dr in range(2):
        dtb[dr * 32:dr * 32 + 16] = inp["ssm_dt_bias"][:, dr].T
        alg[dr * 32:dr * 32 + 16] = inp["ssm_a_log"][:, dr].T
    add("ssm_dt_bias", dtb)
    add("ssm_a_log", alg)
    add("ssm_d", np.broadcast_to(inp["ssm_d"].reshape(1, DEPTH * 16), (128, DEPTH * 16)))
    add("ssm_norm_w", inp["ssm_norm_w"].reshape(DEPTH, 8, 128).transpose(2, 0, 1))
    add("mlp_norm", inp["mlp_norm"].reshape(DEPTH, 8, 128).transpose(2, 0, 1))
    add("final_norm", inp["final_norm"].reshape(8, 128).T)
    mu = np.zeros((128, DEPTH, 16), np.float32)
    for L in range(DEPTH):
        m = inp["rwkv_mu"][L]
        for i in range(12):
            mu[:, L, i] = m[i * 128:(i + 1) * 128]
        mu[0:64, L, 12] = m[1536:1600]
        mu[0:64, L, 13] = m[1600:1664]
        mu[0:64, L, 14] = m[1664:1728]
    add("rw_mu", mu)

    def fm4(a):
        return np.asarray(a).reshape(DEPTH, 4, 128).transpose(2, 0, 1)
    add("rw_w0", np.stack([fm4(inp["rwkv_w0"][:, d]) for d in range(2)], axis=2))
    add("rw_a0", np.stack([fm4(inp["rwkv_a0"][:, d]) for d in range(2)], axis=2))
    add("rw_kk", fm4(inp["rwkv_k_k"]))
    add("rw_ka", fm4(inp["rwkv_k_a"]))
    add("rw_rk", fm4(inp["rwkv_r_k"].reshape(DEPTH, 512)))
    return np.ascontiguousarray(np.concatenate(cols, axis=1).astype(np.float32)), so


def build(T, NL, so, nsmall, co, ncst, dbg=(), stages=("in", "ret", "ssm", "rwkv", "out"), L0=0):
    nc = bass.Bass("TRN2", target_bir_lowering=False)
    X = Ctx()
    X.nc = nc
    X.T = T
    X.so = so
    X.co = co
    X.P = Prog(nc)
    X.xT = nc.dram_tensor("xT", [D, T], F32, kind="ExternalInput").ap()
    X.w_in = nc.dram_tensor("w_in", [DEPTH, D, N_IN], F32, kind="ExternalInput").ap()
    small_d = nc.dram_tensor("small", [128, nsmall], F32, kind="ExternalInput").ap()
    cst_d = nc.dram_tensor("cst", [128, ncst], F32, kind="ExternalInput").ap()
    X.rope = nc.dram_tensor("rope", [128, 2, T], F32, kind="ExternalInput").ap()

    def scratch(name, shape):
        kind = "ExternalOutput" if name in dbg else "Internal"
        return nc.dram_tensor(name, shape, F32, kind=kind).ap()
    X.PT = scratch("PT", [N_IN, T])
    X.Vtm = scratch("Vtm", [T, 512])
    X.OT = scratch("OT", [2048, T])
    X.RWQ = scratch("RWQ", [2, 512, T // 128, 4, 128])
    X.RVT = scratch("RVT", [512, T])
    X.RGT = scratch("RGT", [512, T])
    X.RKB = scratch("RKB", [512, T])
    X.RGL = scratch("RGL", [2, 512, T // 128])
    X.RY0 = scratch("RY0", [T, 512])
    X.DBG = scratch("DBG", [128, 8192])
    X.rw_w_up = nc.dram_tensor("rwkv_w_up", [DEPTH, 2, 32, 512], F32, kind="ExternalInput").ap()
    X.rw_a_up = nc.dram_tensor("rwkv_a_up", [DEPTH, 2, 32, 512], F32, kind="ExternalInput").ap()
    X.rw_g_up = nc.dram_tensor("rwkv_g_up", [DEPTH, 64, 512], F32, kind="ExternalInput").ap()
    X.rw_lnwb = nc.dram_tensor("rw_lnwb", [DEPTH, 2, 512], F32, kind="ExternalInput").ap()
    X.w_branch = nc.dram_tensor("w_branch", [DEPTH, 2048, D], F32, kind="ExternalInput").ap()
    X.w_out = nc.dram_tensor("w_out", [DEPTH, D, D], F32, kind="ExternalInput").ap()
    X.w_mlp_in = nc.dram_tensor("w_mlp_in", [DEPTH, D, 4096], F32, kind="ExternalInput").ap()
    X.w_mlp_out = nc.dram_tensor("w_mlp_out", [DEPTH, 4096, D], F32, kind="ExternalInput").ap()
    X.RSF = scratch("RSF", [T // 128, 128, 512])
    X.XA = scratch("XA", [D, T])
    X.XB = scratch("XB", [D, T])
    X.XC = scratch("XC", [1536, T])
    X.Q4 = scratch("Q4", [64, 4, T])
    X.YP = scratch("YP", [T, 1024])
    X.HINF = scratch("HINF", [T // 128, 128, 1024])
    X.HINB = scratch("HINB", [T // 128, 128, 1024])
    X.yT = nc.dram_tensor("yT", [D, T], F32, kind="ExternalOutput").ap()
    X.A = Arena(nc, 50500)
    A = X.A
    psall = nc.alloc_psum_tensor("psall", [128, 4096], F32)
    X.psall = psall
    X.ps = [psall[:, i * 512:(i + 1) * 512] for i in range(8)]
    X.bps = [Buf(excl=True) for _ in range(8)]
    X.bconst = Buf()
    X.cst_d = cst_d
    X.cst = None
    X.small = A.alloc(nsmall)
    X.ones = A.alloc(128)
    X.ident = A.alloc(128)
    P = X.P
    P.dma(X.ones, cst_d[:, co["ones"]:co["ones"] + 128], writes=[X.bconst])
    P.dma(X.ident, cst_d[:, co["ident"]:co["ident"] + 128], writes=[X.bconst])
    P.dma(X.small, small_d, writes=[X.bconst])
    P.barrier()
    X.xcur = X.xT
    for layer in range(L0, L0 + NL):
        X.xnext = X.XA if layer % 2 == 0 else X.XB
        if "in" in stages:
            stage_in(X, layer)
        if "ret" in stages:
            stage_ret(X, layer)
        if "ssm" in stages:
            stage_ssm(X, layer)
        if "rwkv" in stages:
            stage_rwkv(X, layer)
        if "out" in stages:
            stage_out(X, layer)
            X.xcur = X.xnext
    if "out" in stages:
        stage_final(X)
    P.emit()
    return nc, X


def make_inputs(inp, T, b):
    x = np.asarray(inp["x"])[b, :T]
    return {"xT": np.ascontiguousarray(x.T)}


def kernel(**inp):
    inp = {k: np.asarray(v) for k, v in inp.items()}
    B, T = inp["x"].shape[0], inp["x"].shape[1]
    small, so = pack_small(inp)
    cst, co, rope = host_consts(T)
    nc, X = build(T, DEPTH, so, small.shape[1], co, cst.shape[1])
    shared = {"w_in": inp["w_in"], "small": small, "cst": cst, "rope": rope,
              "rwkv_w_up": inp["rwkv_w_up"], "rwkv_a_up": inp["rwkv_a_up"], "rwkv_g_up": inp["rwkv_g_up"],
              "rw_lnwb": np.ascontiguousarray(np.stack([inp["rwkv_ln_w"], inp["rwkv_ln_b"]], axis=1)),
              "w_branch": inp["w_branch"], "w_out": inp["w_out"], "w_mlp_in": inp["w_mlp_in"], "w_mlp_out": inp["w_mlp_out"]}
    in_maps = []
    for b in range(B):
        m = dict(shared)
        m["xT"] = np.ascontiguousarray(inp["x"][b].T)
        in_maps.append(m)
    res = run_bass_kernel_spmd(nc, in_maps, core_ids=list(range(B)))
    out = np.stack([np.ascontiguousarray(r["yT"].T) for r in res.results], axis=0)
    return out.astype(np.float32)
```

```python
import math
import contextlib
import numpy as np
import concourse.bass as bass
import concourse.mybir as mybir
from concourse.bass_utils import run_bass_kernel_spmd

F32 = mybir.dt.float32
AF = mybir.ActivationFunctionType
ALU = mybir.AluOpType
AX = mybir.AxisListType

D = 1024
SEQ = 4096
DEPTH = 4
N_IN = 9440
EPS = 1e-6
C_RWKV, C_SSM, C_RET, C_GATE = 0, 1728, 4320, 6368
EPOCH = 20000
import os
NO_POOL = bool(os.environ.get('NO_POOL'))
OPLOG = bool(os.environ.get('OPLOG'))
MAXOPS = int(os.environ.get('MAXOPS', '100000000'))


class Buf:
    __slots__ = ("last_w", "readers", "excl")

    def __init__(self, excl=False):
        self.last_w = None
        self.readers = {}
        self.excl = excl


class Prog:
    ENGS = ("pe", "act", "dve", "pool", "sp")

    def __init__(self, nc, n_dma_sems=32):
        self.nc = nc
        self.streams = {e: [] for e in self.ENGS}
        self.cnt = {e: 0 for e in self.ENGS}
        self.epoch = {e: 0 for e in self.ENGS}
        self.seen = {e: {} for e in self.ENGS}
        self.nd = n_dma_sems
        self.dma_i = 0
        self.nops = 0
        self.dma_last = {}

    def _wait(self, eng, tok):
        key, val = tok
        if self.seen[eng].get(key, 0) >= val:
            return
        self.seen[eng][key] = val
        self.streams[eng].append(("w", key, val))

    def _deps(self, reads, writes):
        deps = set()
        for b in reads:
            if b.last_w is not None:
                deps.add(b.last_w)
            if b.excl:
                for t in b.readers.values():
                    deps.add(t)
        for b in writes:
            if b.last_w is not None:
                deps.add(b.last_w)
            for t in b.readers.values():
                deps.add(t)
        return deps

    def op(self, eng, fn, reads=(), writes=()):
        if self.nops >= MAXOPS:
            return None
        if eng == "pool" and NO_POOL:
            eng = "dve"
        deps = self._deps(reads, writes)
        for tok in sorted(deps, key=lambda t: (str(t[0]), t[1])):
            if eng == "pe" and tok[0][0] == "pe":
                continue
            self._wait(eng, tok)
        self.cnt[eng] += 1
        if self.cnt[eng] > EPOCH:
            self.epoch[eng] += 1
            self.cnt[eng] = 1
        key = (eng, self.epoch[eng])
        tok = (key, self.cnt[eng])
        self.streams[eng].append(("op", fn, key, 1))
        for b in writes:
            b.last_w = tok
            b.readers = {}
        for b in reads:
            b.readers[eng] = tok
        self.nops += 1
        if OPLOG:
            import inspect
            fr = inspect.stack()[1]
            print("OP", self.nops, eng, fr.lineno, fr.code_context[0].strip()[:90])
        return tok

    def dma(self, out, in_, reads=(), writes=(), eng="sp"):
        if self.nops >= MAXOPS:
            return None
        deps = self._deps(reads, writes)
        i = self.dma_i
        self.dma_i += 1
        slot = i % self.nd
        rnd = i // self.nd
        key = ("d", slot)
        if rnd > 0:
            deps.add((key, 16 * rnd))
        for tok in sorted(deps, key=lambda t: (str(t[0]), t[1])):
            self._wait(eng, tok)
        tok = (key, 16 * (rnd + 1))
        self.streams[eng].append(("op", lambda e, o=out, i_=in_: e.dma_start(out=o, in_=i_), key, 16))
        self.dma_last[key] = tok
        for b in writes:
            b.last_w = tok
            b.readers = {}
        for b in reads:
            b.readers[("dma", slot)] = tok
        self.nops += 1
        if OPLOG:
            import inspect
            fr = inspect.stack()[1]
            print("OP", self.nops, "dma", fr.lineno, fr.code_context[0].strip()[:90])
        return tok

    def barrier(self):
        toks = []
        for e in self.ENGS:
            if self.cnt[e] > 0:
                toks.append(((e, self.epoch[e]), self.cnt[e]))
        toks += list(self.dma_last.values())
        for e in self.ENGS:
            for t in toks:
                if e == "pe" and t[0][0] == "pe":
                    continue
                self._wait(e, t)

    def emit(self):
        nc = self.nc
        final = list(self.dma_last.values())
        keys = set()
        for e in self.ENGS:
            for it in self.streams[e]:
                keys.add(it[1] if it[0] == "w" else it[2])
        keys = sorted(keys, key=str)
        with contextlib.ExitStack() as st:
            semh = {}
            for k in keys:
                semh[k] = st.enter_context(nc.semaphore("s_" + "_".join(str(x) for x in k)))
            block = st.enter_context(nc.Block())
            streams = self.streams

            def run(engname):
                def f(e):
                    for it in streams[engname]:
                        if it[0] == "w":
                            e.wait_ge(semh[it[1]], it[2])
                        else:
                            it[1](e).then_inc(semh[it[2]], it[3])
                    if engname == "sp":
                        for (k, v) in final:
                            e.wait_ge(semh[k], v)
                return f
            block.tensor(run("pe"))
            block.scalar(run("act"))
            block.vector(run("dve"))
            block.gpsimd(run("pool"))
            block.sync(run("sp"))


class Arena:
    def __init__(self, nc, words):
        self.t = nc.alloc_sbuf_tensor("arena", [128, words], F32)
        self.words = words
        self.off = 0

    def mark(self):
        return self.off

    def reset(self, m):
        self.off = m

    def alloc(self, n):
        o = self.off
        self.off += n
        assert self.off <= self.words, "SBUF arena overflow %d > %d" % (self.off, self.words)
        return self.t[:, o:o + n]


class Ctx:
    pass


def load_cst(X, names):
    out = {}
    for (n, w) in names:
        t = X.A.alloc(w)
        X.P.dma(t, X.cst_d[:, X.co[n]:X.co[n] + w], writes=[X.bconst])
        out[n] = t
    X.P.barrier()
    return out


OFFS = {}


RET_SCALE = 128 ** -0.5


def host_consts(T):
    parts = []
    co = {}
    off = [0]

    def add(name, arr):
        arr = np.asarray(arr, np.float64).reshape(128, -1)
        co[name] = off[0]
        parts.append(arr)
        off[0] += arr.shape[1]
    add("ones", np.ones((128, 128)))
    add("ident", np.eye(128))
    sw = np.zeros((128, 128))
    for m in range(128):
        sw[(m + 64) % 128, m] = 1.0
    add("swap", sw)
    lg = np.log(1.0 - 2.0 ** (-5.0 - np.arange(4, dtype=np.float64)))
    pos = np.arange(128, dtype=np.float64)
    mask = np.exp(lg[None, :, None] * np.abs(pos[:, None, None] - pos[None, None, :])) * RET_SCALE
    add("ret_mask", mask)
    add("ret_vf", np.exp(lg[None, :] * (127.0 - pos[:, None])) * RET_SCALE)
    add("ret_vb", np.exp(lg[None, :] * pos[:, None]) * RET_SCALE)
    add("ret_gf", np.broadcast_to(np.exp(lg[None, :, None] * (pos[None, None, :] + 1.0)), (128, 4, 128)))
    add("ret_gb", np.broadcast_to(np.exp(lg[None, :, None] * (128.0 - pos[None, None, :])), (128, 4, 128)))
    selu = np.zeros((128, 32, 128))
    for hd in range(32):
        dr, hh = hd // 16, hd % 16
        selu[dr * 32 + hh, hd, :] = 1.0
    add("ssm_selu", selu)
    sl = pos[:, None] - pos[None, :]
    add("ssm_maskf", np.tile(np.where(sl <= 0, 0.0, -30000.0), (1, 4)))
    add("ssm_maskb", np.tile(np.where(sl >= 0, 0.0, -30000.0), (1, 4)))
    pf = pos[:, None] - pos[None, :]
    low = (pf > 0) * 1.0
    up = (pf < 0) * 1.0
    lowi = (pf >= 0) * 1.0
    upi = (pf <= 0) * 1.0
    add("rw_low4", np.tile(low, (1, 4)))
    add("rw_up4", np.tile(up, (1, 4)))
    add("rw_upupi2", np.tile(np.concatenate([up, upi], axis=1), (1, 2)))
    add("rw_lowlowi2", np.tile(np.concatenate([low, lowi], axis=1), (1, 2)))
    bi = np.arange(128)
    bd32 = (bi[:, None] // 32 == bi[None, :] // 32) * 1.0
    bd64 = (bi[:, None] // 64 == bi[None, :] // 64) * 1.0
    add("rw_bd_low4", np.tile(bd32 * low, (1, 4)))
    add("rw_bd_up4", np.tile(bd32 * up, (1, 4)))
    add("rw_l32_low4", np.tile(bd64 * (1 - bd32) * low, (1, 4)))
    add("rw_l32_up4", np.tile(bd64 * (1 - bd32) * up, (1, 4)))
    add("rw_l64_low4", np.tile((1 - bd64) * low, (1, 4)))
    add("rw_l64_up4", np.tile((1 - bd64) * up, (1, 4)))
    blk = np.zeros((128, 128))
    blk[0:64, 0:64] = 1.0
    blk[64:128, 64:128] = 1.0
    add("rw_blk", blk)
    hs = np.zeros((128, 2))
    hs[0:64, 0] = 1.0
    hs[64:128, 1] = 1.0
    add("rw_halfsel", hs)
    cst = np.ascontiguousarray(np.concatenate(parts, axis=1).astype(np.float32))
    co["ret_gl"] = [float(np.exp(lg[h] * 128.0)) for h in range(4)]
    inv_freq = 1.0 / (10000.0 ** np.linspace(0.0, 1.0, 64))
    ang = np.arange(T, dtype=np.float64)[None, :] * inv_freq[:, None]
    ang = (np.arange(T, dtype=np.float32)[None, :] * inv_freq.astype(np.float32)[:, None]).astype(np.float64)
    cos, sin = np.cos(ang), np.sin(ang)
    rope = np.zeros((128, 2, T), np.float32)
    rope[0:64, 0] = cos
    rope[64:128, 0] = cos
    rope[0:64, 1] = -sin
    rope[64:128, 1] = sin
    return cst, co, rope


def col_tiles():
    tiles = []

    def rng(c0, n):
        o = 0
        while o < n:
            w = min(128, n - o)
            tiles.append((c0 + o, w))
            o += w
    rng(C_RWKV, 1728)
    rng(C_SSM, 1024)
    rng(C_SSM + 1024, 1536)
    rng(C_SSM + 2560, 32)
    rng(C_RET, 2048)
    rng(C_GATE, 3072)
    return tiles


def col_groups():
    groups = []
    cur = []
    for (c0, w) in col_tiles():
        if cur and (cur[-1][0] + cur[-1][1] == c0) and (sum(t[1] for t in cur) + w <= 512):
            cur.append((c0, w))
        else:
            if cur:
                groups.append(cur)
            cur = [(c0, w)]
    groups.append(cur)
    return groups


def stage_in(X, layer):
    nc, P, A = X.nc, X.P, X.A
    T = X.T
    HB = min(T, 2048)
    NH = T // HB
    NTBH = HB // 512
    m0 = A.mark()
    xt = [A.alloc(8 * 512) for _ in range(2)]
    bxt = [Buf() for _ in range(2)]
    sq = A.alloc(8 * 512)
    bsq = Buf()
    rstd = A.alloc(512)
    brstd = Buf()
    hT = A.alloc(8 * HB)
    bh = Buf()
    hTv = hT.rearrange("p (k t) -> p k t", k=8)
    wt = [A.alloc(8 * 512) for _ in range(2)]
    bwt = [Buf() for _ in range(2)]
    stg = [A.alloc(512) for _ in range(3)]
    bstg = [Buf() for _ in range(3)]
    groups = col_groups()
    w_in = X.w_in
    ps = X.ps
    bps = X.bps
    psi = 0
    wi = 0
    si = 0
    xi = 0
    for hf in range(NH):
        for tb in range(NTBH):
            j = xi % 2
            xi += 1
            t0 = hf * HB + tb * 512
            xv = xt[j].rearrange("p (k t) -> p k t", k=8)
            P.dma(xv, X.xcur[:, t0:t0 + 512].rearrange("(k p) t -> p k t", p=128), writes=[bxt[j]])
            P.op("act", lambda e, j=j: e.activation(out=sq, in_=xt[j], func=AF.Square), reads=[bxt[j]], writes=[bsq])
            pb = psi % 8
            psi += 1
            for kt in range(8):
                P.op("pe", lambda e, kt=kt, pb=pb: e.matmul(ps[pb], X.ones, sq[:, kt * 512:(kt + 1) * 512],
                                                             start=(kt == 0), stop=(kt == 7)),
                     reads=[bsq, X.bconst], writes=[bps[pb]])
            P.op("dve", lambda e, pb=pb: e.tensor_scalar(out=rstd, in0=ps[pb], scalar1=1.0 / D, scalar2=EPS,
                                                          op0=ALU.mult, op1=ALU.add), reads=[bps[pb]], writes=[brstd])
            P.op("act", lambda e: e.activation(out=rstd, in_=rstd, func=AF.Sqrt), reads=[brstd], writes=[brstd])
            P.op("dve", lambda e: e.reciprocal(out=rstd, in_=rstd), reads=[brstd], writes=[brstd])
            for kt in range(8):
                P.op("dve" if kt % 2 == 0 else "dve", lambda e, kt=kt, j=j, tb=tb: e.scalar_tensor_tensor(
                    out=hTv[:, kt, tb * 512:(tb + 1) * 512], in0=xt[j][:, kt * 512:(kt + 1) * 512],
                    scalar=X.small[:, X.so["attn_norm"] + layer * 8 + kt: X.so["attn_norm"] + layer * 8 + kt + 1],
                    in1=rstd, op0=ALU.mult, op1=ALU.mult), reads=[bxt[j], brstd, X.bconst], writes=[bh])
        for grp in groups:
            g0 = grp[0][0]
            gw = sum(t[1] for t in grp)
            k = wi % 2
            wi += 1
            wtv = wt[k].rearrange("p (k c) -> p k c", k=8)
            P.dma(wtv[:, :, 0:gw], w_in[layer, :, g0:g0 + gw].rearrange("(k p) c -> p k c", p=128), writes=[bwt[k]])
            for tb in range(NTBH):
                t0 = hf * HB + tb * 512
                for (c0, w) in grp:
                    if C_RET + 1024 <= c0 < C_RET + 1536:
                        continue
                    off = c0 - g0
                    pb = psi % 8
                    psi += 1
                    for kt in range(8):
                        P.op("pe", lambda e, kt=kt, pb=pb, wtv=wtv, w=w, off=off, tb=tb: e.matmul(
                            ps[pb][0:w, :], wtv[:, kt, off:off + w], hTv[:, kt, tb * 512:(tb + 1) * 512], start=(kt == 0), stop=(kt == 7)),
                            reads=[bwt[k], bh], writes=[bps[pb]])
                    s = si % 3
                    si += 1
                    if si % 2 == 0:
                        P.op("act", lambda e, s=s, pb=pb, w=w: e.activation(out=stg[s][0:w, :], in_=ps[pb][0:w, :], func=AF.Copy),
                             reads=[bps[pb]], writes=[bstg[s]])
                    else:
                        P.op("dve", lambda e, s=s, pb=pb, w=w: e.tensor_copy(out=stg[s][0:w, :], in_=ps[pb][0:w, :]),
                             reads=[bps[pb]], writes=[bstg[s]])
                    P.dma(X.PT[c0:c0 + w, t0:t0 + 512], stg[s][0:w, :], reads=[bstg[s]])
        k = wi % 2
        wi += 1
        wvv = wt[k].rearrange("p (k c) -> p k c", k=8)
        P.dma(wvv, w_in[layer, :, C_RET + 1024:C_RET + 1536].rearrange("(k p) c -> p k c", p=128), writes=[bwt[k]])
        for sub in range(HB // 128):
            t0 = hf * HB + sub * 128
            pb = psi % 8
            psi += 1
            for kt in range(8):
                P.op("pe", lambda e, kt=kt, pb=pb, sub=sub, wvv=wvv: e.matmul(
                    ps[pb], hTv[:, kt, sub * 128:(sub + 1) * 128], wvv[:, kt, :],
                    start=(kt == 0), stop=(kt == 7)), reads=[bwt[k], bh], writes=[bps[pb]])
            s = si % 3
            si += 1
            P.op("dve", lambda e, s=s, pb=pb: e.tensor_copy(out=stg[s], in_=ps[pb]), reads=[bps[pb]], writes=[bstg[s]])
            P.dma(X.Vtm[t0:t0 + 128, :], stg[s], reads=[bstg[s]])
    P.barrier()
    A.reset(m0)


OT_RWKV, OT_SSM, OT_RET = 0, 512, 1536


def stage_ret(X, layer):
    nc, P, A = X.nc, X.P, X.A
    T = X.T
    NT = T // 128
    NTB = T // 512
    co = X.co
    m0 = A.mark()
    ps, bps = X.ps, X.bps
    cs_ = load_cst(X, [("ret_mask", 512), ("ret_vf", 4), ("ret_vb", 4), ("ret_gf", 512), ("ret_gb", 512), ("swap", 128)])
    mask, vf, vb, gf, gb, swap = (cs_[k] for k in ("ret_mask", "ret_vf", "ret_vb", "ret_gf", "ret_gb", "swap"))
    ident, ones = X.ident, X.ones
    gl = co["ret_gl"]
    bc = X.bconst
    SB = A.alloc(NT * 512)
    bSB = Buf()
    sfc = [A.alloc(512) for _ in range(2)]
    bsfc = [Buf() for _ in range(2)]
    rope = A.alloc(2 * 512)
    brope = Buf()
    kin = A.alloc(4 * 512)
    bkin = Buf()
    qin = A.alloc(4 * 512)
    bqin = Buf()
    kr = A.alloc(4 * 512)
    bkr = Buf()
    qr = A.alloc(4 * 512)
    bqr = Buf()
    tmp = A.alloc(4 * 512)
    btmp = Buf()
    ktm = A.alloc(512)
    bktm = Buf()
    vt = [A.alloc(512) for _ in range(2)]
    bvt = [Buf() for _ in range(2)]
    vfb = A.alloc(1024)
    bvfb = Buf()
    srun = [A.alloc(512) for _ in range(2)]
    bsrun = [Buf() for _ in range(2)]
    psi = [0]

    def nb():
        b = psi[0] % 8
        psi[0] += 1
        return b

    def rotary(src, bsrc, dst, bdst, row0, t0):
        P.dma(src.rearrange("p (h t) -> p h t", h=4),
              X.PT[row0:row0 + 512, t0:t0 + 512].rearrange("(h p) t -> p h t", p=128), writes=[bsrc])
        P.op("dve", lambda e: e.tensor_tensor(
            out=tmp.rearrange("p (h t) -> p h t", h=4), in0=src.rearrange("p (h t) -> p h t", h=4),
            in1=rope[:, 0:512].unsqueeze(1).broadcast_to([128, 4, 512]), op=ALU.mult),
            reads=[bsrc, brope], writes=[btmp])
        for hp in range(2):
            b0 = nb()
            b1 = nb()
            for hh, b in ((0, b0), (1, b1)):
                h = hp * 2 + hh
                P.op("pe", lambda e, h=h, b=b: e.matmul(ps[b], swap, src[:, h * 512:(h + 1) * 512], start=True, stop=True),
                     reads=[bsrc, bc], writes=[bps[b]])
            for hh, b in ((0, b0), (1, b1)):
                h = hp * 2 + hh
                P.op("dve", lambda e, h=h, b=b: e.tensor_tensor(out=dst[:, h * 512:(h + 1) * 512], in0=ps[b],
                                                               in1=rope[:, 512:1024], op=ALU.mult),
                     reads=[bps[b], brope], writes=[bdst])
        P.op("pool", lambda e: e.tensor_tensor(out=dst, in0=dst, in1=tmp, op=ALU.add), reads=[bdst, btmp], writes=[bdst])

    cur = 0
    P.op("dve", lambda e: e.memset(srun[0], 0.0), writes=[bsrun[0]])
    for tb in range(NTB):
        t0 = tb * 512
        P.dma(rope.rearrange("p (a t) -> p a t", a=2), X.rope[:, :, t0:t0 + 512], writes=[brope])
        rotary(kin, bkin, kr, bkr, C_RET + 512, t0)
        for ci in range(4):
            c = tb * 4 + ci
            j = c % 2
            P.dma(vt[j], X.Vtm[c * 128:(c + 1) * 128, :], writes=[bvt[j]])
            b = nb()
            for h in range(4):
                P.op("pe", lambda e, h=h, b=b, ci=ci: e.matmul(
                    ps[b][:, h * 128:(h + 1) * 128], kr[:, h * 512 + ci * 128: h * 512 + (ci + 1) * 128], ident,
                    start=True, stop=True), reads=[bkr, bc], writes=[bps[b]])
            P.op("act", lambda e, b=b: e.activation(out=ktm, in_=ps[b], func=AF.Copy), reads=[bps[b]], writes=[bktm])
            vfbv = vfb.rearrange("p (h a e) -> p h a e", h=4, a=2)
            P.op("dve", lambda e, j=j, vfbv=vfbv: e.tensor_tensor(
                out=vfbv[:, :, 0, :], in0=vt[j].rearrange("p (h e) -> p h e", h=4),
                in1=vf.unsqueeze(2).broadcast_to([128, 4, 128]), op=ALU.mult), reads=[bvt[j], bc], writes=[bvfb])
            P.op("pool", lambda e, j=j, vfbv=vfbv: e.tensor_tensor(
                out=vfbv[:, :, 1, :], in0=vt[j].rearrange("p (h e) -> p h e", h=4),
                in1=vb.unsqueeze(2).broadcast_to([128, 4, 128]), op=ALU.mult), reads=[bvt[j], bc], writes=[bvfb])
            b0, b1 = nb(), nb()
            for h in range(4):
                b = b0 if h < 2 else b1
                P.op("pe", lambda e, h=h, b=b: e.matmul(ps[b][:, (h % 2) * 256:(h % 2 + 1) * 256], ktm[:, h * 128:(h + 1) * 128],
                                                         vfb[:, h * 256:(h + 1) * 256], start=True, stop=True),
                     reads=[bktm, bvfb], writes=[bps[b]])
            P.dma(X.RSF[c], srun[cur], reads=[bsrun[cur]])
            nxt = 1 - cur
            for h in range(4):
                b = b0 if h < 2 else b1
                P.op("dve", lambda e, h=h, b=b, cur=cur, nxt=nxt: e.scalar_tensor_tensor(
                    out=srun[nxt][:, h * 128:(h + 1) * 128], in0=srun[cur][:, h * 128:(h + 1) * 128], scalar=gl[h],
                    in1=ps[b][:, (h % 2) * 256:(h % 2) * 256 + 128], op0=ALU.mult, op1=ALU.add),
                    reads=[bsrun[cur], bps[b]], writes=[bsrun[nxt]])
                P.op("act", lambda e, h=h, b=b, c=c: e.activation(
                    out=SB[:, c * 512 + h * 128: c * 512 + (h + 1) * 128],
                    in_=ps[b][:, (h % 2) * 256 + 128:(h % 2) * 256 + 256], func=AF.Copy),
                    reads=[bps[b]], writes=[bSB])
            cur = nxt
    if os.environ.get('RET_STOP') == '1':
        P.barrier(); A.reset(m0); return
    P.op("dve", lambda e, cur=cur: e.memset(srun[cur], 0.0), writes=[bsrun[cur]])
    for c in range(NT - 1, -1, -1):
        nxt = 1 - cur
        for h in range(4):
            P.op("dve", lambda e, h=h, c=c, cur=cur, nxt=nxt: e.scalar_tensor_tensor(
                out=srun[nxt][:, h * 128:(h + 1) * 128], in0=srun[cur][:, h * 128:(h + 1) * 128], scalar=gl[h],
                in1=SB[:, c * 512 + h * 128: c * 512 + (h + 1) * 128], op0=ALU.mult, op1=ALU.add),
                reads=[bsrun[cur], bSB], writes=[bsrun[nxt]])
        P.op("pool", lambda e, c=c, cur=cur: e.tensor_copy(out=SB[:, c * 512:(c + 1) * 512], in_=srun[cur]),
             reads=[bsrun[cur], bSB], writes=[bSB])
        cur = nxt
    if os.environ.get('RET_STOP') == '2':
        P.barrier(); A.reset(m0); return
    P.barrier()
    gin = A.alloc(4 * 512)
    bgin = Buf()
    qf = A.alloc(4 * 512)
    bqf = Buf()
    qb = A.alloc(4 * 512)
    bqb = Buf()
    sm = A.alloc(512)
    bsm = Buf()
    sqo = A.alloc(512)
    bsqo = Buf()
    rs = A.alloc(512)
    brs = Buf()
    ot = [A.alloc(512) for _ in range(2)]
    bot = [Buf() for _ in range(2)]
    for tb in range(NTB):
        t0 = tb * 512
        P.dma(rope.rearrange("p (a t) -> p a t", a=2), X.rope[:, :, t0:t0 + 512], writes=[brope])
        rotary(kin, bkin, kr, bkr, C_RET + 512, t0)
        rotary(qin, bqin, qr, bqr, C_RET, t0)
        P.dma(gin.rearrange("p (h t) -> p h t", h=4),
              X.PT[C_RET + 1536:C_RET + 2048, t0:t0 + 512].rearrange("(h p) t -> p h t", p=128), writes=[bgin])
        P.op("act", lambda e: e.activation(out=gin, in_=gin, func=AF.Silu), reads=[bgin], writes=[bgin])
        qr4 = qr.rearrange("p (h c l) -> p h c l", h=4, c=4)
        for ci in range(4):
            P.op("dve", lambda e, ci=ci: e.tensor_tensor(
                out=qf.rearrange("p (h c l) -> p h c l", h=4, c=4)[:, :, ci, :], in0=qr4[:, :, ci, :],
                in1=gf.rearrange("p (h l) -> p h l", h=4), op=ALU.mult), reads=[bqr, bc], writes=[bqf])
            P.op("pool", lambda e, ci=ci: e.tensor_tensor(
                out=qb.rearrange("p (h c l) -> p h c l", h=4, c=4)[:, :, ci, :], in0=qr4[:, :, ci, :],
                in1=gb.rearrange("p (h l) -> p h l", h=4), op=ALU.mult), reads=[bqr, bc], writes=[bqb])
        for ci in range(4):
            c = tb * 4 + ci
            j = c % 2
            P.dma(vt[j], X.Vtm[c * 128:(c + 1) * 128, :], writes=[bvt[j]])
            P.dma(sfc[j], X.RSF[c], writes=[bsfc[j]])
            b = nb()
            for h in range(4):
                sl = slice(h * 512 + ci * 128, h * 512 + (ci + 1) * 128)
                P.op("pe", lambda e, h=h, b=b, sl=sl: e.matmul(ps[b][:, h * 128:(h + 1) * 128], kr[:, sl], qr[:, sl],
                                                               start=True, stop=True), reads=[bkr, bqr], writes=[bps[b]])
            P.op("dve", lambda e, b=b: e.tensor_tensor(out=sm, in0=ps[b], in1=mask, op=ALU.mult),
                 reads=[bps[b], bc], writes=[bsm])
            b = nb()
            for h in range(4):
                sl = slice(h * 512 + ci * 128, h * 512 + (ci + 1) * 128)
                o = ps[b][:, h * 128:(h + 1) * 128]
                P.op("pe", lambda e, h=h, o=o, j=j: e.matmul(o, vt[j][:, h * 128:(h + 1) * 128], sm[:, h * 128:(h + 1) * 128],
                                                             start=True, stop=False), reads=[bvt[j], bsm], writes=[bps[b]])
                P.op("pe", lambda e, h=h, o=o, j=j, sl=sl: e.matmul(o, sfc[j][:, h * 128:(h + 1) * 128], qf[:, sl],
                                                                    start=False, stop=False), reads=[bsfc[j], bqf], writes=[bps[b]])
                P.op("pe", lambda e, h=h, o=o, c=c, sl=sl: e.matmul(o, SB[:, c * 512 + h * 128:c * 512 + (h + 1) * 128], qb[:, sl],
                                                                    start=False, stop=True), reads=[bSB, bqb], writes=[bps[b]])
            P.op("act", lambda e, b=b: e.activation(out=sqo, in_=ps[b], func=AF.Square), reads=[bps[b]], writes=[bsqo])
            b2 = nb()
            P.op("pe", lambda e, b2=b2: e.matmul(ps[b2], ones, sqo, start=True, stop=True), reads=[bsqo, bc], writes=[bps[b2]])
            P.op("dve", lambda e, b2=b2: e.tensor_scalar(out=rs, in0=ps[b2], scalar1=1.0 / 128, scalar2=EPS,
                                                          op0=ALU.mult, op1=ALU.add), reads=[bps[b2]], writes=[brs])
            P.op("act", lambda e: e.activation(out=rs, in_=rs, func=AF.Sqrt), reads=[brs], writes=[brs])
            P.op("dve", lambda e: e.reciprocal(out=rs, in_=rs), reads=[brs], writes=[brs])
            P.op("dve", lambda e, b=b, j=j: e.tensor_tensor(out=ot[j], in0=ps[b], in1=rs, op=ALU.mult),
                 reads=[bps[b], brs], writes=[bot[j]])
            P.op("pool", lambda e, j=j, ci=ci: e.tensor_tensor(
                out=ot[j].rearrange("p (h l) -> p h l", h=4), in0=ot[j].rearrange("p (h l) -> p h l", h=4),
                in1=gin.rearrange("p (h c l) -> p h c l", h=4, c=4)[:, :, ci, :], op=ALU.mult),
                reads=[bot[j], bgin], writes=[bot[j]])
            P.dma(X.OT[OT_RET:OT_RET + 512, c * 128:(c + 1) * 128].rearrange("(h p) t -> p h t", p=128),
                  ot[j].rearrange("p (h l) -> p h l", h=4), reads=[bot[j]])
    P.barrier()
    A.reset(m0)


def stage_ssm(X, layer):
    nc, P, A = X.nc, X.P, X.A
    T = X.T
    NT = T // 128
    co, so = X.co, X.so
    cst, small = X.cst, X.small
    bc = X.bconst
    ps, bps = X.ps, X.bps
    m0 = A.mark()
    psi = [0]

    def nb():
        b = psi[0] % 8
        psi[0] += 1
        return b
    ident, ones = X.ident, X.ones
    cs_ = load_cst(X, [("ssm_selu", 4096), ("ssm_maskf", 512), ("ssm_maskb", 512)])
    selu = cs_["ssm_selu"][0:64, :]
    maskf, maskb = cs_["ssm_maskf"], cs_["ssm_maskb"]
    XS0 = C_SSM + 1024
    m1 = A.mark()
    xp = [A.alloc(T + 4) for _ in range(2)]
    bxp = [Buf() for _ in range(2)]
    acc = [A.alloc(T) for _ in range(2)]
    bacc = [Buf() for _ in range(2)]
    for j in range(2):
        P.op("pool", lambda e, j=j: e.memset(xp[j][:, 0:2], 0.0), writes=[bxp[j]])
        P.op("pool", lambda e, j=j: e.memset(xp[j][:, T + 2:T + 4], 0.0), writes=[bxp[j]])
    for i in range(12):
        j = i % 2
        P.dma(xp[j][:, 2:T + 2], X.PT[XS0 + i * 128:XS0 + (i + 1) * 128, :], writes=[bxp[j]])
        wcol = so["ssm_conv_w"] + (layer * 12 + i) * 5
        bcol = so["ssm_conv_b"] + layer * 12 + i
        P.op("dve", lambda e, j=j, wcol=wcol, bcol=bcol: e.tensor_scalar(
            out=acc[j], in0=xp[j][:, 0:T], scalar1=small[:, wcol:wcol + 1], scalar2=small[:, bcol:bcol + 1],
            op0=ALU.mult, op1=ALU.add), reads=[bxp[j], bc], writes=[bacc[j]])
        for k in range(1, 5):
            P.op("dve", lambda e, j=j, k=k, wcol=wcol: e.scalar_tensor_tensor(
                out=acc[j], in0=xp[j][:, k:k + T], scalar=small[:, wcol + k:wcol + k + 1], in1=acc[j],
                op0=ALU.mult, op1=ALU.add), reads=[bxp[j], bacc[j], bc], writes=[bacc[j]])
        P.op("act", lambda e, j=j: e.activation(out=acc[j], in_=acc[j], func=AF.Silu), reads=[bacc[j]], writes=[bacc[j]])
        P.dma(X.XC[i * 128:(i + 1) * 128, :], acc[j], reads=[bacc[j]])
    P.barrier()
    A.reset(m1)
    q4 = A.alloc(4 * T)
    bq4 = Buf()
    la = A.alloc(T)
    bla = Buf()
    cum = A.alloc(T)
    bcum = Buf()
    rmask = A.alloc(T)
    brm = Buf()
    nA = A.alloc(1)
    bnA = Buf()
    q4v = q4.rearrange("p (q t) -> p q t", q=4)
    dtq = q4v[0:64, 0, :]
    uq = q4v[0:64, 1, :]
    eaq = q4v[0:64, 2, :]
    deq = q4v[0:64, 3, :]
    P.op("pool", lambda e: e.memset(q4[0:64, :], 0.0), writes=[bq4])
    DT0 = C_SSM + 2560
    P.dma(q4v[0:16, 0, :], X.PT[DT0:DT0 + 16, :], writes=[bq4])
    P.dma(q4v[32:48, 0, :], X.PT[DT0 + 16:DT0 + 32, :], writes=[bq4])
    P.op("pool", lambda e: e.memset(rmask[0:64, :], 1.0), writes=[brm])
    P.op("pool", lambda e: e.memset(rmask[0:64, :].rearrange("p (c l) -> p c l", l=128)[:, :, 0:1], 0.0), writes=[brm])
    dbc = so["ssm_dt_bias"] + layer
    alc = so["ssm_a_log"] + layer
    P.op("act", lambda e: e.activation(out=dtq, in_=dtq, func=AF.Exp, bias=small[0:64, dbc:dbc + 1]), reads=[bq4, bc], writes=[bq4])
    P.op("act", lambda e: e.activation(out=dtq, in_=dtq, func=AF.Ln, bias=1.0), reads=[bq4], writes=[bq4])
    P.op("act", lambda e: e.activation(out=nA[0:64, :], in_=small[0:64, alc:alc + 1], func=AF.Exp), reads=[bc], writes=[bnA])
    P.op("dve", lambda e: e.tensor_scalar(out=nA[0:64, :], in0=nA[0:64, :], scalar1=-1.0, scalar2=None, op0=ALU.mult),
         reads=[bnA], writes=[bnA])
    P.op("dve", lambda e: e.tensor_scalar(out=la[0:64, :], in0=dtq, scalar1=nA[0:64, :], scalar2=None, op0=ALU.mult),
         reads=[bq4, bnA], writes=[bla])
    P.op("dve", lambda e: e.tensor_tensor_scan(out=cum[0:64, :], data0=rmask[0:64, :], data1=la[0:64, :], initial=0.0,
                                               op0=ALU.mult, op1=ALU.add), reads=[brm, bla], writes=[bcum])
    cum3 = cum.rearrange("p (c l) -> p c l", l=128)
    atot = cum3[:, :, 127:128].broadcast_to([128, NT, 128])
    P.op("dve", lambda e: e.tensor_copy(out=q4v[0:32, 1, :], in_=cum[0:32, :]), reads=[bcum], writes=[bq4])
    P.op("dve", lambda e: e.tensor_tensor(out=q4v[0:32, 3, :].rearrange("p (c l) -> p c l", l=128), in0=atot[0:32],
                                          in1=cum3[0:32], op=ALU.subtract), reads=[bcum], writes=[bq4])
    P.op("dve", lambda e: e.tensor_tensor(out=q4v[32:64, 3, :], in0=cum[32:64, :], in1=la[32:64, :], op=ALU.subtract),
         reads=[bcum, bla], writes=[bq4])
    P.op("dve", lambda e: e.tensor_tensor(out=q4v[32:64, 1, :].rearrange("p (c l) -> p c l", l=128), in0=atot[32:64],
                                          in1=q4v[32:64, 3, :].rearrange("p (c l) -> p c l", l=128), op=ALU.subtract),
         reads=[bcum, bq4], writes=[bq4])
    P.op("act", lambda e: e.activation(out=eaq, in_=uq, func=AF.Exp), reads=[bq4], writes=[bq4])
    P.op("act", lambda e: e.activation(out=deq, in_=deq, func=AF.Exp), reads=[bq4], writes=[bq4])
    P.dma(X.Q4[:, :, :], q4v[0:64, :, :], reads=[bq4])
    P.barrier()
    A.reset(m1)
    EA = A.alloc(NT * 64)
    CD = A.alloc(NT * 64)
    bEA, bCD = Buf(), Buf()
    xin = [A.alloc(1024) for _ in range(2)]
    bxin = [Buf() for _ in range(2)]
    bcin = [A.alloc(512) for _ in range(2)]
    bbcin = [Buf() for _ in range(2)]
    qin = [A.alloc(512) for _ in range(2)]
    bqin = [Buf() for _ in range(2)]
    xs = A.alloc(1024)
    bxs = Buf()
    btm = A.alloc(256)
    bbtm = Buf()
    tmq = A.alloc(192)
    btmq = Buf()
    gt = A.alloc(256)
    bgt = Buf()
    rhsU = A.alloc(4096)
    brhsU = Buf()
    negu = A.alloc(128)
    bnegu = Buf()
    Eall = A.alloc(4096)
    bE = [Buf() for _ in range(8)]
    xdt = [A.alloc(1024) for _ in range(2)]
    bxdt = [Buf() for _ in range(2)]
    xdd = [A.alloc(1024) for _ in range(2)]
    bxdd = [Buf() for _ in range(2)]
    yp = [A.alloc(1024) for _ in range(2)]
    byp = [Buf() for _ in range(2)]
    hrun = A.alloc(1024)
    bhrun = Buf()
    sbst = [A.alloc(1024) for _ in range(2)]
    bsbst = [Buf() for _ in range(2)]
    dcol = so["ssm_d"] + layer * 16
    P.op("pool", lambda e: e.memset(hrun, 0.0), writes=[bhrun])
    for c in range(NT):
        j = c % 2
        cs = slice(c * 128, (c + 1) * 128)
        P.dma(xin[j].rearrange("p (i t) -> p i t", i=8), X.XC[0:1024, cs].rearrange("(i p) t -> p i t", p=128), writes=[bxin[j]])
        P.dma(bcin[j].rearrange("p (i t) -> p i t", i=4), X.XC[1024:1536, cs].rearrange("(i p) t -> p i t", p=128), writes=[bbcin[j]])
        P.dma(qin[j][0:64, :].rearrange("p (q t) -> p q t", q=4), X.Q4[:, :, cs], writes=[bqin[j]])
        for half in range(2):
            b = nb()
            for ii in range(4):
                i = half * 4 + ii
                P.op("pe", lambda e, b=b, ii=ii, i=i, j=j: e.matmul(ps[b][:, ii * 128:(ii + 1) * 128], xin[j][:, i * 128:(i + 1) * 128],
                                                                   ident, start=True, stop=True), reads=[bxin[j], bc], writes=[bps[b]])
            if half == 0:
                P.op("act", lambda e, b=b: e.activation(out=xs[:, 0:512], in_=ps[b], func=AF.Copy), reads=[bps[b]], writes=[bxs])
            else:
                P.op("dve", lambda e, b=b: e.tensor_copy(out=xs[:, 512:1024], in_=ps[b]), reads=[bps[b]], writes=[bxs])
        b = nb()
        for g in range(2):
            P.op("pe", lambda e, b=b, g=g, j=j: e.matmul(ps[b][:, g * 128:(g + 1) * 128], bcin[j][:, g * 128:(g + 1) * 128], ident,
                                                         start=True, stop=True), reads=[bbcin[j], bc], writes=[bps[b]])
        for qi, q in enumerate((0, 2, 3)):
            P.op("pe", lambda e, b=b, qi=qi, q=q, j=j: e.matmul(ps[b][:, 256 + qi * 64:256 + (qi + 1) * 64],
                                                               qin[j][0:64, q * 128:(q + 1) * 128], ident[0:64, 0:64],
                                                               start=True, stop=True), reads=[bqin[j], bc], writes=[bps[b]])
        P.op("act", lambda e, b=b: e.activation(out=btm, in_=ps[b][:, 0:256], func=AF.Copy), reads=[bps[b]], writes=[bbtm])
        P.op("dve", lambda e, b=b: e.tensor_copy(out=tmq, in_=ps[b][:, 256:448]), reads=[bps[b]], writes=[btmq])
        P.op("pool", lambda e, c=c: e.tensor_copy(out=EA[:, c * 64:(c + 1) * 64], in_=tmq[:, 64:128]), reads=[btmq], writes=[bEA])
        P.op("pool", lambda e, c=c: e.tensor_tensor(out=CD[:, c * 64:(c + 1) * 64], in0=tmq[:, 64:128], in1=tmq[:, 128:192], op=ALU.mult),
             reads=[btmq], writes=[bCD])
        b = nb()
        for g in range(2):
            P.op("pe", lambda e, b=b, g=g, j=j: e.matmul(ps[b][:, g * 128:(g + 1) * 128], bcin[j][:, g * 128:(g + 1) * 128],
                                                         bcin[j][:, 256 + g * 128:256 + (g + 1) * 128], start=True, stop=True),
                 reads=[bbcin[j]], writes=[bps[b]])
        P.op("act", lambda e, b=b: e.activation(out=gt, in_=ps[b][:, 0:256], func=AF.Copy), reads=[bps[b]], writes=[bgt])
        P.op("pool", lambda e, j=j: e.tensor_tensor(
            out=rhsU[0:64, :].rearrange("p (h l) -> p h l", h=32), in0=selu.rearrange("p (h l) -> p h l", h=32),
            in1=qin[j][0:64, 128:256].unsqueeze(1).broadcast_to([64, 32, 128]), op=ALU.mult),
            reads=[bqin[j], bc], writes=[brhsU])
        P.op("dve", lambda e, j=j: e.tensor_scalar(out=negu[0:64, :], in0=qin[j][0:64, 128:256], scalar1=-1.0, scalar2=None, op0=ALU.mult),
             reads=[bqin[j]], writes=[bnegu])
        for k in range(8):
            b = nb()
            sl = slice(k * 512, (k + 1) * 512)
            P.op("pe", lambda e, b=b, sl=sl: e.matmul(ps[b], ones[0:64, :], rhsU[0:64, sl], start=True, stop=False),
                 reads=[brhsU, bc], writes=[bps[b]])
            P.op("pe", lambda e, b=b, sl=sl: e.matmul(ps[b], negu[0:64, :], selu[:, sl], start=False, stop=False),
                 reads=[bnegu, bc], writes=[bps[b]])
            mk = maskf if k < 4 else maskb
            P.op("pe", lambda e, b=b, mk=mk: e.matmul(ps[b], ident, mk, start=False, stop=True), reads=[bc], writes=[bps[b]])
            P.op("act", lambda e, b=b, sl=sl: e.activation(out=Eall[:, sl], in_=ps[b], func=AF.Exp), reads=[bps[b]], writes=[bE[k]])
            g = (k % 4) // 2
            eng = "dve" if k % 2 == 0 else "pool"
            P.op(eng, lambda e, sl=sl, g=g: e.tensor_tensor(
                out=Eall[:, sl].rearrange("p (h l) -> p h l", h=4), in0=Eall[:, sl].rearrange("p (h l) -> p h l", h=4),
                in1=gt[:, g * 128:(g + 1) * 128].unsqueeze(1).broadcast_to([128, 4, 128]), op=ALU.mult),
                reads=[bE[k], bgt], writes=[bE[k]])
        for dr in range(2):
            eng = "dve" if dr == 0 else "pool"
            P.op(eng, lambda e, dr=dr: e.tensor_tensor(
                out=xdt[dr].rearrange("p (h q) -> p h q", h=16), in0=xs.rearrange("p (h q) -> p h q", h=16),
                in1=tmq[:, dr * 32:dr * 32 + 16].unsqueeze(2).broadcast_to([128, 16, 64]), op=ALU.mult),
                reads=[bxs, btmq], writes=[bxdt[dr]])
            P.op(eng, lambda e, dr=dr: e.tensor_tensor(
                out=xdd[dr].rearrange("p (h q) -> p h q", h=16), in0=xdt[dr].rearrange("p (h q) -> p h q", h=16),
                in1=tmq[:, 128 + dr * 32:128 + dr * 32 + 16].unsqueeze(2).broadcast_to([128, 16, 64]), op=ALU.mult),
                reads=[bxdt[dr], btmq], writes=[bxdd[dr]])
        P.op("pool", lambda e, j=j: e.tensor_tensor(
            out=yp[j].rearrange("p (h q) -> p h q", h=16), in0=xs.rearrange("p (h q) -> p h q", h=16),
            in1=small[:, dcol:dcol + 16].unsqueeze(2).broadcast_to([128, 16, 64]), op=ALU.mult),
            reads=[bxs, bc], writes=[byp[j]])
        for g in range(2):
            b = nb()
            for e8 in range(8):
                h = g * 8 + e8
                for dr in range(2):
                    hd = dr * 16 + h
                    P.op("pe", lambda e, b=b, e8=e8, h=h, hd=hd, dr=dr: e.matmul(
                        ps[b][:, e8 * 64:(e8 + 1) * 64], Eall[:, hd * 128:(hd + 1) * 128], xdt[dr][:, h * 64:(h + 1) * 64],
                        start=(dr == 0), stop=(dr == 1)), reads=[bE[hd // 4], bxdt[dr]], writes=[bps[b]])
            P.op("dve", lambda e, b=b, g=g, j=j: e.tensor_tensor(out=yp[j][:, g * 512:(g + 1) * 512], in0=ps[b],
                                                                 in1=yp[j][:, g * 512:(g + 1) * 512], op=ALU.add),
                 reads=[bps[b], byp[j]], writes=[byp[j]])
        P.dma(X.YP[cs, :], yp[j], reads=[byp[j]])
        P.dma(X.HINF[c], hrun, reads=[bhrun])
        for g in range(2):
            b = nb()
            P.op("pe", lambda e, b=b, g=g: e.matmul(ps[b], btm[:, g * 128:(g + 1) * 128], xdd[0][:, g * 512:(g + 1) * 512],
                                                    start=True, stop=True), reads=[bbtm, bxdd[0]], writes=[bps[b]])
            P.op("dve", lambda e, g=g, c=c: e.tensor_tensor(
                out=hrun[:, g * 512:(g + 1) * 512].rearrange("p (h q) -> p h q", h=8),
                in0=hrun[:, g * 512:(g + 1) * 512].rearrange("p (h q) -> p h q", h=8),
                in1=CD[:, c * 64 + g * 8:c * 64 + g * 8 + 8].unsqueeze(2).broadcast_to([128, 8, 64]), op=ALU.mult),
                reads=[bhrun, bCD], writes=[bhrun])
            P.op("dve", lambda e, b=b, g=g: e.tensor_tensor(out=hrun[:, g * 512:(g + 1) * 512], in0=ps[b],
                                                            in1=hrun[:, g * 512:(g + 1) * 512], op=ALU.add),
                 reads=[bps[b], bhrun], writes=[bhrun])
            b = nb()
            P.op("pe", lambda e, b=b, g=g: e.matmul(ps[b], btm[:, g * 128:(g + 1) * 128], xdd[1][:, g * 512:(g + 1) * 512],
                                                    start=True, stop=True), reads=[bbtm, bxdd[1]], writes=[bps[b]])
            P.op("act", lambda e, b=b, g=g, j=j: e.activation(out=sbst[j][:, g * 512:(g + 1) * 512], in_=ps[b], func=AF.Copy),
                 reads=[bps[b]], writes=[bsbst[j]])
        P.dma(X.HINB[c], sbst[j], reads=[bsbst[j]])
    P.barrier()
    P.op("pool", lambda e: e.memset(hrun, 0.0), reads=[bhrun], writes=[bhrun])
    for c in range(NT - 1, -1, -1):
        j = c % 2
        P.dma(sbst[j], X.HINB[c], writes=[bsbst[j]])
        P.dma(X.HINB[c], hrun, reads=[bhrun, bsbst[j]])
        for g in range(2):
            P.op("dve", lambda e, g=g, c=c: e.tensor_tensor(
                out=hrun[:, g * 512:(g + 1) * 512].rearrange("p (h q) -> p h q", h=8),
                in0=hrun[:, g * 512:(g + 1) * 512].rearrange("p (h q) -> p h q", h=8),
                in1=CD[:, c * 64 + 32 + g * 8:c * 64 + 32 + g * 8 + 8].unsqueeze(2).broadcast_to([128, 8, 64]), op=ALU.mult),
                reads=[bhrun, bCD], writes=[bhrun])
        P.op("pool", lambda e, j=j: e.tensor_tensor(out=hrun, in0=hrun, in1=sbst[j], op=ALU.add), reads=[bhrun, bsbst[j]], writes=[bhrun])
    P.barrier()
    zin = [A.alloc(1024) for _ in range(2)]
    bzin = [Buf() for _ in range(2)]
    hf = [A.alloc(1024) for _ in range(2)]
    bhf = [Buf() for _ in range(2)]
    hb = [A.alloc(1024) for _ in range(2)]
    bhb = [Buf() for _ in range(2)]
    tmps2 = [[A.alloc(512) for _ in range(4)] for _ in range(2)]
    btmps2 = [[Buf() for _ in range(4)] for _ in range(2)]
    yzs = [A.alloc(1024) for _ in range(2)]
    byzs = [Buf() for _ in range(2)]
    sqs = [A.alloc(1024) for _ in range(2)]
    bsqs = [Buf() for _ in range(2)]
    rss = [A.alloc(256) for _ in range(2)]
    brss = [Buf() for _ in range(2)]
    nwc = so["ssm_norm_w"] + layer * 8
    class _Rec:
        def __init__(self):
            self.items = []

        def op(self, *a, **k):
            self.items.append((0, a, k))

        def dma(self, *a, **k):
            self.items.append((1, a, k))

    def _sw2(c, PQ):
            j = c % 2
            yz, byz, sq, bsq = yzs[j], byzs[j], sqs[j], bsqs[j]
            tmps, btmps, rs, brs = tmps2[j], btmps2[j], rss[j], brss[j]
            kb = [0]

            def nb():
                b = j * 4 + kb[0] % 4
                kb[0] += 1
                return b
            cs = slice(c * 128, (c + 1) * 128)
            PQ.dma(yp[j], X.YP[cs, :], writes=[byp[j]])
            PQ.dma(bcin[j].rearrange("p (i t) -> p i t", i=4), X.XC[1024:1536, cs].rearrange("(i p) t -> p i t", p=128), writes=[bbcin[j]])
            PQ.dma(hf[j], X.HINF[c], writes=[bhf[j]])
            PQ.dma(hb[j], X.HINB[c], writes=[bhb[j]])
            PQ.dma(zin[j].rearrange("p (i t) -> p i t", i=8), X.PT[C_SSM:C_SSM + 1024, cs].rearrange("(i p) t -> p i t", p=128), writes=[bzin[j]])
            PQ.op("act", lambda e, j=j: e.activation(out=zin[j], in_=zin[j], func=AF.Silu), reads=[bzin[j]], writes=[bzin[j]])
            for dr in range(2):
                hsrc, bh_ = (hf[j], bhf[j]) if dr == 0 else (hb[j], bhb[j])
                for g in range(2):
                    b = nb()
                    PQ.op("pe", lambda e, b=b, g=g, j=j, hsrc=hsrc: e.matmul(ps[b], bcin[j][:, 256 + g * 128:256 + (g + 1) * 128],
                                                                            hsrc[:, g * 512:(g + 1) * 512], start=True, stop=True),
                         reads=[bbcin[j], bh_], writes=[bps[b]])
                    tmp, btmp = tmps[dr * 2 + g], btmps[dr * 2 + g]
                    PQ.op("dve", lambda e, b=b, g=g, dr=dr, c=c, tmp=tmp: e.tensor_tensor(
                        out=tmp.rearrange("p (h q) -> p h q", h=8), in0=ps[b].rearrange("p (h q) -> p h q", h=8),
                        in1=EA[:, c * 64 + dr * 32 + g * 8:c * 64 + dr * 32 + g * 8 + 8].unsqueeze(2).broadcast_to([128, 8, 64]),
                        op=ALU.mult), reads=[bps[b], bEA], writes=[btmp])
                    PQ.op("pool", lambda e, g=g, j=j, tmp=tmp: e.tensor_tensor(out=yp[j][:, g * 512:(g + 1) * 512], in0=yp[j][:, g * 512:(g + 1) * 512],
                                                                      in1=tmp, op=ALU.add), reads=[btmp, byp[j]], writes=[byp[j]])
            for half in range(2):
                b = nb()
                for ii in range(4):
                    i = half * 4 + ii
                    PQ.op("pe", lambda e, b=b, ii=ii, i=i, j=j: e.matmul(ps[b][:, ii * 128:(ii + 1) * 128], yp[j][:, i * 128:(i + 1) * 128],
                                                                       ident, start=True, stop=True), reads=[byp[j], bc], writes=[bps[b]])
                PQ.op("dve", lambda e, b=b, half=half, j=j, yz=yz: e.tensor_tensor(out=yz[:, half * 512:(half + 1) * 512], in0=ps[b],
                                                                           in1=zin[j][:, half * 512:(half + 1) * 512], op=ALU.mult),
                     reads=[bps[b], bzin[j]], writes=[byz])
            PQ.op("act", lambda e, sq=sq, yz=yz: e.activation(out=sq, in_=yz, func=AF.Square), reads=[byz], writes=[bsq])
            b = nb()
            for g in range(2):
                for ii in range(4):
                    i = g * 4 + ii
                    PQ.op("pe", lambda e, b=b, g=g, ii=ii, i=i, sq=sq: e.matmul(ps[b][:, g * 128:(g + 1) * 128], ones, sq[:, i * 128:(i + 1) * 128],
                                                                       start=(ii == 0), stop=(ii == 3)), reads=[bsq, bc], writes=[bps[b]])
            PQ.op("dve", lambda e, b=b, rs=rs: e.tensor_scalar(out=rs, in0=ps[b][:, 0:256], scalar1=1.0 / 512, scalar2=EPS,
                                                        op0=ALU.mult, op1=ALU.add), reads=[bps[b]], writes=[brs])
            PQ.op("act", lambda e, rs=rs: e.activation(out=rs, in_=rs, func=AF.Sqrt), reads=[brs], writes=[brs])
            PQ.op("dve", lambda e, rs=rs: e.reciprocal(out=rs, in_=rs), reads=[brs], writes=[brs])
            for g in range(2):
                PQ.op("dve", lambda e, g=g, yz=yz, rs=rs: e.tensor_tensor(
                    out=yz[:, g * 512:(g + 1) * 512].rearrange("p (i t) -> p i t", i=4),
                    in0=yz[:, g * 512:(g + 1) * 512].rearrange("p (i t) -> p i t", i=4),
                    in1=rs[:, g * 128:(g + 1) * 128].unsqueeze(1).broadcast_to([128, 4, 128]), op=ALU.mult),
                    reads=[byz, brs], writes=[byz])
            PQ.op("pool", lambda e, yz=yz: e.tensor_tensor(
                out=yz.rearrange("p (i t) -> p i t", i=8), in0=yz.rearrange("p (i t) -> p i t", i=8),
                in1=small[:, nwc:nwc + 8].unsqueeze(2).broadcast_to([128, 8, 128]), op=ALU.mult), reads=[byz, bc], writes=[byz])
            PQ.dma(X.OT[OT_SSM:OT_SSM + 1024, cs].rearrange("(i p) t -> p i t", p=128), yz.rearrange("p (i t) -> p i t", i=8), reads=[byz])

    for c0 in range(0, NT, 2):
        recs = []
        for c in (c0, c0 + 1):
            if c < NT:
                r_ = _Rec()
                _sw2(c, r_)
                recs.append(r_.items)
        for i in range(max(len(x) for x in recs)):
            for items in recs:
                if i < len(items):
                    kind, a, k = items[i]
                    (P.dma if kind else P.op)(*a, **k)
    P.barrier()
    A.reset(m0)


NEG_EXP_HALF = -math.exp(-0.5)
GN_EPS = 64e-5


def stage_rwkv(X, layer):
    nc, P, A = X.nc, X.P, X.A
    T = X.T
    NT = T // 128
    TB = 512
    NB = T // TB
    CB = TB // 128
    co, so = X.co, X.so
    cst, small = X.cst, X.small
    bc = X.bconst
    ps, bps = X.ps, X.bps
    m0 = A.mark()
    psi = [0]

    def nb():
        b = psi[0] % 8
        psi[0] += 1
        return b

    def nb2():
        if psi[0] % 2 == 1:
            psi[0] += 1
        b = psi[0] % 8
        psi[0] += 2
        return b
    ident = X.ident
    cs_ = load_cst(X, [("rw_blk", 128), ("rw_halfsel", 2), ("rw_low4", 512), ("rw_up4", 512), ("rw_upupi2", 512), ("rw_lowlowi2", 512),
                       ("rw_bd_low4", 512), ("rw_bd_up4", 512), ("rw_l32_low4", 512), ("rw_l32_up4", 512), ("rw_l64_low4", 512), ("rw_l64_up4", 512)])
    blk, halfsel, low4, up4, upupi2, lowlowi2 = (cs_[k] for k in ("rw_blk", "rw_halfsel", "rw_low4", "rw_up4", "rw_upupi2", "rw_lowlowi2"))

    def scol(name, idx):
        o = so[name] + idx
        return small[:, o:o + 1]
    m1 = A.mark()
    wup = A.alloc(512)
    aup = A.alloc(512)
    gup = A.alloc(512)
    bw = Buf()
    P.dma(wup[0:64, :], X.rw_w_up[layer].rearrange("d k c -> (d k) c"), writes=[bw])
    P.dma(aup[0:64, :], X.rw_a_up[layer].rearrange("d k c -> (d k) c"), writes=[bw])
    P.dma(gup[0:64, :], X.rw_g_up[layer], writes=[bw])
    P.barrier()
    hmu = A.alloc(16)
    omu = A.alloc(16)
    bmu = Buf()
    muo = so["rw_mu"] + layer * 16
    P.op("dve", lambda e: e.tensor_scalar(out=hmu, in0=small[:, muo:muo + 16], scalar1=0.5, scalar2=None, op0=ALU.mult),
         reads=[bc], writes=[bmu])
    P.op("dve", lambda e: e.tensor_scalar(out=omu, in0=small[:, muo:muo + 16], scalar1=-1.0, scalar2=1.0, op0=ALU.mult, op1=ALU.add),
         reads=[bc], writes=[bmu])
    oka = A.alloc(4)
    hrk = A.alloc(4)
    kao = so["rw_ka"] + layer * 4
    rko = so["rw_rk"] + layer * 4
    P.op("dve", lambda e: e.tensor_scalar(out=oka, in0=small[:, kao:kao + 4], scalar1=-1.0, scalar2=1.0, op0=ALU.mult, op1=ALU.add),
         reads=[bc], writes=[bmu])
    P.op("dve", lambda e: e.tensor_scalar(out=hrk, in0=small[:, rko:rko + 4], scalar1=0.5, scalar2=None, op0=ALU.mult),
         reads=[bc], writes=[bmu])
    rmask = A.alloc(TB)
    brm = Buf()
    P.op("pool", lambda e: e.memset(rmask, 1.0), writes=[brm])
    P.op("pool", lambda e: e.memset(rmask.rearrange("p (c l) -> p c l", l=128)[:, :, 0:1], 0.0), writes=[brm])
    xp = [A.alloc(TB + 2) for _ in range(2)]
    bxp = [Buf() for _ in range(2)]
    xpi = [0]
    t1 = A.alloc(TB)
    bt1 = Buf()

    def shifted(row0, nrows, mucol, t0, out, bout, func=None):
        j = xpi[0] % 2
        xpi[0] += 1
        lo = t0 - 1
        hi = t0 + TB + 1
        dlo, dhi = 0, TB + 2
        if lo < 0:
            P.op("pool", lambda e, j=j: e.memset(xp[j][0:nrows, 0:1], 0.0), writes=[bxp[j]])
            lo, dlo = 0, 1
        if hi > T:
            P.op("pool", lambda e, j=j: e.memset(xp[j][0:nrows, TB + 1:TB + 2], 0.0), writes=[bxp[j]])
            hi, dhi = T, TB + 1
        P.dma(xp[j][0:nrows, dlo:dhi], X.PT[row0:row0 + nrows, lo:hi], writes=[bxp[j]])
        P.op("pool", lambda e, j=j: e.tensor_tensor(out=t1[0:nrows, :], in0=xp[j][0:nrows, 0:TB], in1=xp[j][0:nrows, 2:TB + 2], op=ALU.add),
             reads=[bxp[j]], writes=[bt1])
        P.op("dve", lambda e: e.tensor_scalar(out=t1[0:nrows, :], in0=t1[0:nrows, :], scalar1=hmu[0:nrows, mucol:mucol + 1], scalar2=None,
                                              op0=ALU.mult), reads=[bt1, bmu], writes=[bt1])
        P.op("dve", lambda e, j=j: e.scalar_tensor_tensor(out=out[0:nrows, :], in0=xp[j][0:nrows, 1:TB + 1],
                                                          scalar=omu[0:nrows, mucol:mucol + 1], in1=t1[0:nrows, :],
                                                          op0=ALU.mult, op1=ALU.add), reads=[bxp[j], bt1, bmu], writes=[bout])
        if func is not None:
            P.op("act", lambda e: e.activation(out=out[0:nrows, :], in_=out[0:nrows, :], func=func), reads=[bout], writes=[bout])

    def tile(n=TB):
        return A.alloc(n), Buf()
    tw, btw = tile()
    adt, badt = tile()
    sg, bsg = tile()
    ur, bur = tile()
    uk, buk = tile()
    uv, buv = tile()
    gti, bgti = tile()
    kk, bkk = tile()
    sq, bsq = tile()
    sgm, bsgm = tile()
    ad, bad = tile()
    cum, bcum = tile()
    et, bet = tile()
    ct, bct = tile()
    E1, bE1 = tile()
    E2, bE2 = tile()
    E3, bE3 = tile()
    be_, bbe = tile()
    kd, bkd = tile()
    kbs, bkbs = tile()
    glt, bglt = tile(CB)
    R0 = C_RWKV
    for blkk in range(NB):
        t0 = blkk * TB
        shifted(R0 + 1536, 64, 12, t0, tw, btw, AF.Tanh)
        shifted(R0 + 1600, 64, 13, t0, adt, badt, None)
        shifted(R0 + 1664, 64, 14, t0, sg, bsg, AF.Sigmoid)
        for p in range(4):
            pc = slice(p * 128, (p + 1) * 128)
            shifted(R0 + p * 128, 128, p, t0, ur, bur)
            shifted(R0 + 512 + p * 128, 128, 4 + p, t0, uk, buk)
            shifted(R0 + 1024 + p * 128, 128, 8 + p, t0, uv, buv)
            P.dma(X.RVT[pc, t0:t0 + TB], uv, reads=[buv])
            b = nb()
            P.op("pe", lambda e, b=b, pc=pc: e.matmul(ps[b], gup[0:64, pc], sg[0:64, :], start=True, stop=True), reads=[bw, bsg], writes=[bps[b]])
            P.op("act", lambda e, b=b: e.activation(out=gti, in_=ps[b], func=AF.Copy), reads=[bps[b]], writes=[bgti])
            P.dma(X.RGT[pc, t0:t0 + TB], gti, reads=[bgti])
            P.op("dve", lambda e, p=p: e.tensor_scalar(out=kk, in0=uk, scalar1=scol("rw_kk", layer * 4 + p), scalar2=None, op0=ALU.mult),
                 reads=[buk, bc], writes=[bkk])
            P.op("act", lambda e: e.activation(out=sq, in_=kk, func=AF.Square), reads=[bkk], writes=[bsq])
            b = nb()
            P.op("pe", lambda e, b=b: e.matmul(ps[b], blk, sq, start=True, stop=True), reads=[bsq, bc], writes=[bps[b]])
            P.op("act", lambda e, b=b: e.activation(out=sq, in_=ps[b], func=AF.Sqrt), reads=[bps[b]], writes=[bsq])
            P.op("dve", lambda e: e.tensor_scalar(out=sq, in0=sq, scalar1=1e-12, scalar2=None, op0=ALU.max), reads=[bsq], writes=[bsq])
            P.op("dve", lambda e: e.reciprocal(out=sq, in_=sq), reads=[bsq], writes=[bsq])
            P.op("dve", lambda e: e.tensor_tensor(out=kk, in0=kk, in1=sq, op=ALU.mult), reads=[bkk, bsq], writes=[bkk])
            for d in range(2):
                ds_ = slice(d * 32, (d + 1) * 32)
                b = nb()
                P.op("pe", lambda e, b=b, pc=pc, ds_=ds_: e.matmul(ps[b], wup[ds_, pc], tw[ds_, :], start=True, stop=True),
                     reads=[bw, btw], writes=[bps[b]])
                P.op("act", lambda e, b=b, d=d, p=p: e.activation(out=sgm, in_=ps[b], func=AF.Sigmoid,
                                                                 bias=scol("rw_w0", (layer * 2 + d) * 4 + p)),
                     reads=[bps[b], bc], writes=[bsgm])
                b = nb()
                P.op("pe", lambda e, b=b, pc=pc, ds_=ds_: e.matmul(ps[b], aup[ds_, pc], adt[ds_, :], start=True, stop=True),
                     reads=[bw, badt], writes=[bps[b]])
                P.op("act", lambda e, b=b, d=d, p=p: e.activation(out=ad, in_=ps[b], func=AF.Sigmoid,
                                                                 bias=scol("rw_a0", (layer * 2 + d) * 4 + p)),
                     reads=[bps[b], bc], writes=[bad])
                P.op("dve", lambda e: e.tensor_scalar(out=sgm, in0=sgm, scalar1=NEG_EXP_HALF, scalar2=None, op0=ALU.mult),
                     reads=[bsgm], writes=[bsgm])
                P.op("dve", lambda e: e.tensor_tensor_scan(out=cum, data0=rmask, data1=sgm, initial=0.0, op0=ALU.mult, op1=ALU.add),
                     reads=[brm, bsgm], writes=[bcum])
                cum3 = cum.rearrange("p (c l) -> p c l", l=128)
                ltot = cum3[:, :, 127:128]
                P.op("act", lambda e, ltot=ltot: e.activation(out=glt.unsqueeze(2), in_=ltot, func=AF.Exp), reads=[bcum], writes=[bglt])
                P.dma(X.RGL[d, pc, blkk * CB:(blkk + 1) * CB], glt, reads=[bglt])
                if d == 0:
                    P.op("pool", lambda e: e.tensor_tensor(out=et, in0=cum, in1=sgm, op=ALU.subtract), reads=[bcum, bsgm], writes=[bet])
                    cc, bcc = cum, bcum
                else:
                    P.op("pool", lambda e, ltot=ltot, cum3=cum3: e.tensor_tensor(
                        out=et.rearrange("p (c l) -> p c l", l=128), in0=ltot.broadcast_to([128, CB, 128]), in1=cum3, op=ALU.subtract),
                        reads=[bcum], writes=[bet])
                    P.op("pool", lambda e: e.tensor_tensor(out=ct, in0=et, in1=sgm, op=ALU.add), reads=[bet, bsgm], writes=[bct])
                    cc, bcc = ct, bct
                P.op("act", lambda e: e.activation(out=E1, in_=et, func=AF.Exp), reads=[bet], writes=[bE1])
                P.op("act", lambda e, cc=cc: e.activation(out=E2, in_=cc, func=AF.Exp, scale=-1.0), reads=[bcc], writes=[bE2])
                P.op("act", lambda e, cc=cc: e.activation(out=E3, in_=cc, func=AF.Exp), reads=[bcc], writes=[bE3])
                P.op("dve", lambda e: e.scalar_tensor_tensor(out=E1, in0=kk, scalar=-1.0, in1=E1, op0=ALU.mult, op1=ALU.mult),
                     reads=[bkk, bE1], writes=[bE1])
                P.op("pool", lambda e: e.tensor_tensor(out=E3, in0=ur, in1=E3, op=ALU.mult), reads=[bur, bE3], writes=[bE3])
                P.op("dve", lambda e: e.tensor_tensor(out=be_, in0=kk, in1=ad, op=ALU.mult), reads=[bkk, bad], writes=[bbe])
                P.op("pool", lambda e: e.tensor_tensor(out=be_, in0=be_, in1=E2, op=ALU.mult), reads=[bbe, bE2], writes=[bbe])
                P.op("dve", lambda e, p=p: e.tensor_scalar(out=ad, in0=ad, scalar1=scol("rw_ka", layer * 4 + p), scalar2=oka[:, p:p + 1],
                                                           op0=ALU.mult, op1=ALU.add), reads=[bad, bc, bmu], writes=[bad])
                P.op("dve", lambda e: e.tensor_tensor(out=kd, in0=uk, in1=ad, op=ALU.mult), reads=[buk, bad], writes=[bkd])
                if d == 0:
                    P.op("pool", lambda e: e.tensor_copy(out=kbs, in_=kd), reads=[bkd], writes=[bkbs])
                else:
                    P.op("pool", lambda e: e.tensor_tensor(out=kbs, in0=kbs, in1=kd, op=ALU.add), reads=[bkd, bkbs], writes=[bkbs])
                P.op("dve", lambda e: e.tensor_tensor(out=kd, in0=kd, in1=E2, op=ALU.mult), reads=[bkd, bE2], writes=[bkd])
                for q, (src, bsrc) in enumerate(((E1, bE1), (E3, bE3), (be_, bbe), (kd, bkd))):
                    P.dma(X.RWQ[d, pc, blkk * CB:(blkk + 1) * CB, q, :], src.rearrange("p (c l) -> p c l", l=128), reads=[bsrc])
            P.op("dve", lambda e, p=p: e.scalar_tensor_tensor(out=kbs, in0=kbs, scalar=hrk[:, p:p + 1], in1=ur, op0=ALU.mult, op1=ALU.mult),
                 reads=[bkbs, bur, bmu], writes=[bkbs])
            P.dma(X.RKB[pc, t0:t0 + TB], kbs, reads=[bkbs])
    P.barrier()
    A.reset(m1)
    if os.environ.get("RW_STOP") == "A":
        A.reset(m0)
        return
    GL = A.alloc(4 * NT)
    bGL = Buf()
    qt = [A.alloc(2048) for _ in range(2)]
    bqt = [Buf() for _ in range(2)]
    vin = [A.alloc(512) for _ in range(2)]
    bvin = [Buf() for _ in range(2)]
    OFFS["Pb"] = A.off
    Pb = A.alloc(1024)
    bPb = [Buf() for _ in range(2)]
    OFFS["QT"] = A.off
    QT = A.alloc(2048)
    bQ = [Buf() for _ in range(2)]
    bT = [Buf() for _ in range(2)]
    ARB = A.alloc(1024)
    bARB = Buf()
    Qf = A.alloc(1024)
    bQf = [Buf() for _ in range(2)]
    PD = A.alloc(2048)
    bPk = [Buf() for _ in range(2)]
    bD = [Buf() for _ in range(2)]
    Ws = [A.alloc(1024) for _ in range(2)]
    bWs = [[Buf() for _ in range(2)] for _ in range(2)]
    Ls = [A.alloc(1024) for _ in range(2)]
    bLs = [[Buf() for _ in range(2)] for _ in range(2)]
    AK = A.alloc(2048)
    bAK = Buf()
    vtm = A.alloc(512)
    bvtm = Buf()
    btmk = A.alloc(1024)
    bbtmk = Buf()
    Xs = A.alloc(512)
    bXs = Buf()
    Us = A.alloc(512)
    bUs = Buf()
    Hst = [A.alloc(512) for _ in range(2)]
    qm = A.alloc(2048)
    bqm = Buf()
    bH = [Buf() for _ in range(2)]
    Ysb = [A.alloc(512) for _ in range(2)]
    bY = [Buf() for _ in range(2)]
    y0 = [A.alloc(512) for _ in range(2)]
    by0 = [Buf() for _ in range(2)]
    Pv = Pb.rearrange("p (h l) -> p h l", h=8)
    QTv = QT.rearrange("p (h a l) -> p h a l", h=8, a=2)
    AKv = AK.rearrange("p (h a l) -> p h a l", h=8, a=2)
    ARBv = ARB.rearrange("p (h l) -> p h l", h=8)
    Qfv = Qf.rearrange("p (h l) -> p h l", h=8)
    PDv = PD.rearrange("p (h a l) -> p h a l", h=8, a=2)
    lnwb = A.alloc(1024)
    blnwb = Buf()
    P.dma(lnwb.rearrange("p (a c) -> p a c", a=2), X.rw_lnwb[layer:layer + 1].broadcast_to([128, 2, 512]), writes=[blnwb])
    rkbin = [A.alloc(512) for _ in range(2)]
    brkbin = [Buf() for _ in range(2)]
    gin = [A.alloc(512) for _ in range(2)]
    bgin = [Buf() for _ in range(2)]
    st8 = A.alloc(64)
    bst8 = Buf()
    ysq = A.alloc(512)
    bysq = Buf()
    oT = A.alloc(512)
    boT = Buf()
    for d in range(2):
        maskP = low4 if d == 0 else up4
        maskQ2 = upupi2 if d == 0 else lowlowi2
        P.dma(GL.rearrange("p (a c) -> p a c", a=4), X.RGL[d].rearrange("(a p) c -> p a c", p=128), writes=[bGL])
        cur = 0
        P.op("pool", lambda e: e.memset(Hst[0], 0.0), writes=[bH[0]])
        P.op("pool", lambda e: e.memset(Hst[1], 0.0), writes=[bH[1]])
        order = range(NT) if d == 0 else range(NT - 1, -1, -1)
        for ci_, c in enumerate(order):
            if d * NT + ci_ >= int(os.environ.get("RW_NCH", "1000")):
                break
            j = c % 2
            cs = slice(c * 128, (c + 1) * 128)
            P.dma(qt[j].rearrange("p (a q l) -> p a q l", a=4, q=4), X.RWQ[d, :, c, :, :].rearrange("(a p) q l -> p a q l", p=128), writes=[bqt[j]])
            P.dma(vin[j].rearrange("p (a l) -> p a l", a=4), X.RVT[:, cs].rearrange("(a p) l -> p a l", p=128), writes=[bvin[j]])
            q4 = qt[j].rearrange("p (a q l) -> p a q l", a=4, q=4)

            def hq(h, q0, q1=None, q4=q4):
                a, half = h // 2, h % 2
                rows = slice(half * 64, half * 64 + 64)
                if q1 is None:
                    return q4[rows, a, q0, :]
                return q4[rows, a, q0:q1, :].rearrange("p q l -> p (q l)")
            qmv = qm.rearrange("p (x a q l) -> p x a q l", x=2, a=4, q=2)
            for half in range(2):
                P.op("dve" if half == 0 else "pool", lambda e, half=half, q4=q4, qmv=qmv: e.tensor_scalar(
                    out=qmv[:, half], in0=q4[:, :, 2:4, :], scalar1=halfsel[:, half:half + 1], scalar2=None, op0=ALU.mult),
                    reads=[bqt[j], bc], writes=[bqm])
            b = nb()
            for a in range(4):
                P.op("pe", lambda e, b=b, a=a, j=j: e.matmul(ps[b][:, a * 128:(a + 1) * 128], vin[j][:, a * 128:(a + 1) * 128], ident,
                                                             start=True, stop=True), reads=[bvin[j], bc], writes=[bps[b]])
            P.op("act", lambda e, b=b: e.activation(out=vtm, in_=ps[b], func=AF.Copy), reads=[bps[b]], writes=[bvtm])
            for qi, q in enumerate((2, 3)):
                b = nb()
                for a in range(4):
                    P.op("pe", lambda e, b=b, a=a, q=q, q4=q4: e.matmul(ps[b][:, a * 128:(a + 1) * 128], q4[:, a, q, :], ident,
                                                                       start=True, stop=True), reads=[bqt[j], bc], writes=[bps[b]])
                eng = "dve" if qi == 0 else "act"
                if eng == "dve":
                    P.op("dve", lambda e, b=b, qi=qi: e.tensor_copy(out=btmk[:, qi * 512:(qi + 1) * 512], in_=ps[b]), reads=[bps[b]], writes=[bbtmk])
                else:
                    P.op("act", lambda e, b=b, qi=qi: e.activation(out=btmk[:, qi * 512:(qi + 1) * 512], in_=ps[b], func=AF.Copy),
                         reads=[bps[b]], writes=[bbtmk])
            for gi in range(2):
                hs = range(gi * 4, gi * 4 + 4)
                b = nb()
                for h in hs:
                    P.op("pe", lambda e, b=b, h=h, q4=q4, qmv=qmv: e.matmul(ps[b][:, (h % 4) * 128:(h % 4 + 1) * 128], q4[:, h // 2, 0, :],
                                                                            qmv[:, h % 2, h // 2, 0, :], start=True, stop=True),
                         reads=[bqt[j], bqm], writes=[bps[b]])
                P.op("dve", lambda e, b=b, gi=gi, maskP=maskP: e.tensor_tensor(out=Pb[:, gi * 512:(gi + 1) * 512], in0=ps[b], in1=maskP, op=ALU.mult),
                     reads=[bps[b], bc], writes=[bPb[gi]])
                for hp in range(2):
                    h0 = gi * 4 + hp * 2
                    b = nb()
                    for hh in range(2):
                        h = h0 + hh
                        P.op("pe", lambda e, b=b, h=h, hh=hh, q4=q4, qmv=qmv: e.matmul(
                            ps[b][:, hh * 256:(hh + 1) * 256], qmv[:, h % 2, h // 2, 0, :], q4[:, h // 2, 0:2, :].rearrange("p q l -> p (q l)"),
                            start=True, stop=True), reads=[bqt[j], bqm], writes=[bps[b]])
                    pv = ps[b].rearrange("p (h a l) -> p h a l", h=2, a=2)
                    mv = maskQ2.rearrange("p (h a l) -> p h a l", h=2, a=2)
                    P.op("dve", lambda e, pv=pv, mv=mv, h0=h0: e.tensor_tensor(out=Qfv[:, h0:h0 + 2, :], in0=pv[:, :, 0, :], in1=mv[:, :, 0, :],
                                                                               op=ALU.mult), reads=[bps[b], bc], writes=[bQf[gi]])
                    P.op("dve", lambda e, pv=pv, mv=mv, h0=h0: e.tensor_tensor(out=ARBv[:, h0:h0 + 2, :], in0=pv[:, :, 1, :], in1=mv[:, :, 1, :],
                                                                               op=ALU.mult), reads=[bps[b], bc], writes=[bARB])
                    b = nb()
                    for hh in range(2):
                        h = h0 + hh
                        P.op("pe", lambda e, b=b, h=h, hh=hh, q4=q4, qmv=qmv: e.matmul(
                            ps[b][:, hh * 256:(hh + 1) * 256], qmv[:, h % 2, h // 2, 1, :], q4[:, h // 2, 0:2, :].rearrange("p q l -> p (q l)"),
                            start=True, stop=True), reads=[bqt[j], bqm], writes=[bps[b]])
                    P.op("dve", lambda e, b=b, h0=h0, maskQ2=maskQ2: e.tensor_tensor(out=AK[:, h0 * 256:(h0 + 2) * 256], in0=ps[b], in1=maskQ2, op=ALU.mult),
                         reads=[bps[b], bc], writes=[bAK])
            if os.environ.get("RW_DBG") == "1":
                P.dma(X.DBG[:, 0:1024], Pb, reads=bPb)
                P.dma(X.DBG[:, 1024:3072], QT, reads=bQ + bT)
                P.dma(X.DBG[:, 3072:5120], AK, reads=[bAK])
                P.dma(X.DBG[:, 5120:6144], ARB, reads=[bARB])
                P.barrier(); A.reset(m0); return
            lo_ = d == 0
            bdP = cs_["rw_bd_low4"] if lo_ else cs_["rw_bd_up4"]
            bdQ = cs_["rw_bd_up4"] if lo_ else cs_["rw_bd_low4"]
            lvP = [cs_["rw_l32_low4"] if lo_ else cs_["rw_l32_up4"], cs_["rw_l64_low4"] if lo_ else cs_["rw_l64_up4"]]
            lvQ = [cs_["rw_l32_up4"] if lo_ else cs_["rw_l32_low4"], None]
            id4 = ident.unsqueeze(1).broadcast_to([128, 4, 128])
            for gi in range(2):
                g4 = slice(gi * 4, gi * 4 + 4)
                gs = slice(gi * 512, (gi + 1) * 512)
                P.op("dve", lambda e, g4=g4, gs=gs, bdP=bdP: e.tensor_tensor(out=PDv[:, g4, 0, :], in0=Pb[:, gs].rearrange("p (h l) -> p h l", h=4),
                                                                             in1=bdP.rearrange("p (h l) -> p h l", h=4), op=ALU.mult),
                     reads=[bPb[gi], bc], writes=[bPk[gi]])
                P.op("pool", lambda e, g4=g4, gs=gs, bdQ=bdQ: e.tensor_tensor(out=QTv[:, g4, 0, :], in0=Qf[:, gs].rearrange("p (h l) -> p h l", h=4),
                                                                              in1=bdQ.rearrange("p (h l) -> p h l", h=4), op=ALU.mult),
                     reads=[bQf[gi], bc], writes=[bQ[gi]])
                P.op("pool", lambda e, g4=g4: e.tensor_tensor(out=PDv[:, g4, 1, :], in0=PDv[:, g4, 0, :], in1=id4, op=ALU.add),
                     reads=[bPk[gi], bc], writes=[bD[gi]])
                P.op("dve", lambda e, g4=g4: e.tensor_tensor(out=QTv[:, g4, 1, :], in0=QTv[:, g4, 0, :], in1=id4, op=ALU.add),
                     reads=[bQ[gi], bc], writes=[bT[gi]])
            for step in range(1, 6):
                for gi in range(2):
                    g4 = slice(gi * 4, gi * 4 + 4)
                    if step in (1, 5):
                        bA, bB = nb(), nb()
                        for h in range(gi * 4, gi * 4 + 4):
                            hl = h % 4
                            oa = ps[bA][:, hl * 128:(hl + 1) * 128]
                            ob = ps[bB][:, hl * 128:(hl + 1) * 128]
                            ra = QTv[:, h, 0, :] if step == 1 else QTv[:, h, 1, :]
                            rb = PDv[:, h, 0, :] if step == 1 else PDv[:, h, 1, :]
                            P.op("pe", lambda e, oa=oa, h=h, ra=ra: e.matmul(oa, PDv[:, h, 0, :], ra, start=True, stop=True),
                                 reads=[bPk[gi], bQ[gi], bT[gi]], writes=[bps[bA]])
                            P.op("pe", lambda e, ob=ob, h=h, rb=rb: e.matmul(ob, QTv[:, h, 0, :], rb, start=True, stop=True),
                                 reads=[bQ[gi], bPk[gi], bD[gi]], writes=[bps[bB]])
                        pa = ps[bA].rearrange("p (h l) -> p h l", h=4)
                        pb_ = ps[bB].rearrange("p (h l) -> p h l", h=4)
                        if step == 1:
                            P.op("dve", lambda e, pa=pa, g4=g4: e.tensor_copy(out=QTv[:, g4, 0, :], in_=pa), reads=[bps[bA]], writes=[bQ[gi]])
                            P.op("act", lambda e, pb_=pb_, g4=g4: e.activation(out=PDv[:, g4, 0, :], in_=pb_, func=AF.Copy), reads=[bps[bB]], writes=[bPk[gi]])
                        else:
                            P.op("dve", lambda e, pa=pa, g4=g4: e.tensor_tensor(out=QTv[:, g4, 1, :], in0=pa, in1=QTv[:, g4, 1, :], op=ALU.add),
                                 reads=[bps[bA], bT[gi]], writes=[bT[gi]])
                            P.op("dve", lambda e, pb_=pb_, g4=g4: e.tensor_tensor(out=PDv[:, g4, 1, :], in0=pb_, in1=PDv[:, g4, 1, :], op=ALU.add),
                                 reads=[bps[bB], bD[gi]], writes=[bD[gi]])
                    else:
                        bQD = nb2()
                        bPD_ = nb2()
                        for h in range(gi * 4, gi * 4 + 4):
                            hl = h % 4
                            oq = ps[bQD + hl // 2][:, (hl % 2) * 256:(hl % 2 + 1) * 256]
                            op_ = ps[bPD_ + hl // 2][:, (hl % 2) * 256:(hl % 2 + 1) * 256]
                            P.op("pe", lambda e, oq=oq, h=h: e.matmul(oq, PDv[:, h, 0, :], QT[:, h * 256:(h + 1) * 256], start=True, stop=True),
                                 reads=[bPk[gi], bQ[gi], bT[gi]], writes=[bps[bQD + hl // 2]])
                            P.op("pe", lambda e, op_=op_, h=h: e.matmul(op_, QTv[:, h, 0, :], PD[:, h * 256:(h + 1) * 256], start=True, stop=True),
                                 reads=[bQ[gi], bPk[gi], bD[gi]], writes=[bps[bPD_ + hl // 2]])
                        for k2 in range(2):
                            h0 = gi * 4 + k2 * 2
                            pq = ps[bQD + k2].rearrange("p (h a l) -> p h a l", h=2, a=2)
                            pp = ps[bPD_ + k2].rearrange("p (h a l) -> p h a l", h=2, a=2)
                            P.op("dve", lambda e, pq=pq, h0=h0: e.tensor_copy(out=QTv[:, h0:h0 + 2, 0, :], in_=pq[:, :, 0, :]),
                                 reads=[bps[bQD + k2]], writes=[bQ[gi]])
                            P.op("dve", lambda e, pq=pq, h0=h0: e.tensor_tensor(out=QTv[:, h0:h0 + 2, 1, :], in0=pq[:, :, 1, :], in1=QTv[:, h0:h0 + 2, 1, :],
                                                                                op=ALU.add), reads=[bps[bQD + k2], bT[gi]], writes=[bT[gi]])
                            P.op("act", lambda e, pp=pp, h0=h0: e.activation(out=PDv[:, h0:h0 + 2, 0, :], in_=pp[:, :, 0, :], func=AF.Copy),
                                 reads=[bps[bPD_ + k2]], writes=[bPk[gi]])
                            P.op("dve", lambda e, pp=pp, h0=h0: e.tensor_tensor(out=PDv[:, h0:h0 + 2, 1, :], in0=pp[:, :, 1, :], in1=PDv[:, h0:h0 + 2, 1, :],
                                                                                op=ALU.add), reads=[bps[bPD_ + k2], bD[gi]], writes=[bD[gi]])
            for lv in range(2):
                for gi in range(2):
                    g4 = slice(gi * 4, gi * 4 + 4)
                    gs = slice(gi * 512, (gi + 1) * 512)
                    mP, mQ = lvP[lv], lvQ[lv]
                    P.op("pool", lambda e, gs=gs, mP=mP: e.tensor_tensor(out=Ls[0][:, gs], in0=Pb[:, gs], in1=mP, op=ALU.mult),
                         reads=[bPb[gi], bc], writes=[bLs[0][gi]])
                    bA = nb()
                    for h in range(gi * 4, gi * 4 + 4):
                        hl = h % 4
                        P.op("pe", lambda e, bA=bA, h=h, hl=hl: e.matmul(ps[bA][:, hl * 128:(hl + 1) * 128], Ls[0][:, h * 128:(h + 1) * 128], QTv[:, h, 1, :],
                                                                       start=True, stop=True), reads=[bLs[0][gi], bT[gi]], writes=[bps[bA]])
                    P.op("act", lambda e, bA=bA, gs=gs: e.activation(out=Ws[0][:, gs], in_=ps[bA], func=AF.Copy), reads=[bps[bA]], writes=[bWs[0][gi]])
                    if lv == 0:
                        P.op("pool", lambda e, gs=gs, mQ=mQ: e.tensor_tensor(out=Ls[1][:, gs], in0=Qf[:, gs], in1=mQ, op=ALU.mult),
                             reads=[bQf[gi], bc], writes=[bLs[1][gi]])
                        bB = nb()
                        for h in range(gi * 4, gi * 4 + 4):
                            hl = h % 4
                            P.op("pe", lambda e, bB=bB, h=h, hl=hl: e.matmul(ps[bB][:, hl * 128:(hl + 1) * 128], Ls[1][:, h * 128:(h + 1) * 128], PDv[:, h, 1, :],
                                                                           start=True, stop=True), reads=[bLs[1][gi], bD[gi]], writes=[bps[bB]])
                        P.op("dve", lambda e, bB=bB, gs=gs: e.tensor_copy(out=Ws[1][:, gs], in_=ps[bB]), reads=[bps[bB]], writes=[bWs[1][gi]])
                    bA2 = nb()
                    for h in range(gi * 4, gi * 4 + 4):
                        hl = h % 4
                        P.op("pe", lambda e, bA2=bA2, h=h, hl=hl: e.matmul(ps[bA2][:, hl * 128:(hl + 1) * 128], PDv[:, h, 1, :], Ws[0][:, h * 128:(h + 1) * 128],
                                                                         start=True, stop=True), reads=[bD[gi], bWs[0][gi]], writes=[bps[bA2]])
                    if lv == 0:
                        bB2 = nb()
                        for h in range(gi * 4, gi * 4 + 4):
                            hl = h % 4
                            P.op("pe", lambda e, bB2=bB2, h=h, hl=hl: e.matmul(ps[bB2][:, hl * 128:(hl + 1) * 128], QTv[:, h, 1, :], Ws[1][:, h * 128:(h + 1) * 128],
                                                                             start=True, stop=True), reads=[bT[gi], bWs[1][gi]], writes=[bps[bB2]])
                    P.op("dve", lambda e, bA2=bA2, g4=g4: e.tensor_tensor(out=QTv[:, g4, 1, :], in0=ps[bA2].rearrange("p (h l) -> p h l", h=4),
                                                                          in1=QTv[:, g4, 1, :], op=ALU.add), reads=[bps[bA2], bT[gi]], writes=[bT[gi]])
                    if lv == 0:
                        P.op("dve", lambda e, bB2=bB2, g4=g4: e.tensor_tensor(out=PDv[:, g4, 1, :], in0=ps[bB2].rearrange("p (h l) -> p h l", h=4),
                                                                              in1=PDv[:, g4, 1, :], op=ALU.add), reads=[bps[bB2], bD[gi]], writes=[bD[gi]])
            if os.environ.get("RW_DBG") == "3":
                P.dma(X.DBG[:, 0:1024], Pb, reads=bPb)
                P.dma(X.DBG[:, 1024:3072], QT, reads=bQ + bT)
                P.barrier(); A.reset(m0); return
            Hc, bHc = Hst[cur], bH[cur]
            Hn, bHn = Hst[1 - cur], bH[1 - cur]
            bX = nb()
            for a in range(4):
                P.op("pe", lambda e, a=a, q4=q4, Hc=Hc, bX=bX: e.matmul(ps[bX][:, a * 128:(a + 1) * 128], q4[:, a, 0, :], Hc[:, a * 128:(a + 1) * 128],
                                                                      start=True, stop=False), reads=[bqt[j], bHc], writes=[bps[bX]])
                for half in range(2):
                    h = a * 2 + half
                    P.op("pe", lambda e, h=h, half=half, bX=bX: e.matmul(ps[bX][:, h * 64:(h + 1) * 64], AKv[:, h, 0, :], vtm[:, h * 64:(h + 1) * 64],
                                                                       start=False, stop=(half == 1)), reads=[bAK, bvtm], writes=[bps[bX]])
            P.op("act", lambda e, bX=bX: e.activation(out=Xs, in_=ps[bX], func=AF.Copy), reads=[bps[bX]], writes=[bXs])
            if os.environ.get('RW_B3') == '1':
                P.barrier(); A.reset(m0); return
            bU = nb()
            for h in range(8):
                P.op("pe", lambda e, bU=bU, h=h: e.matmul(ps[bU][:, h * 64:(h + 1) * 64], QTv[:, h, 1, :], Xs[:, h * 64:(h + 1) * 64],
                                                          start=True, stop=True), reads=[bT[h // 4], bXs], writes=[bps[bU]])
            P.op("dve", lambda e, bU=bU: e.tensor_copy(out=Us, in_=ps[bU]), reads=[bps[bU]], writes=[bUs])
            if os.environ.get('RW_B3') == '2':
                P.barrier(); A.reset(m0); return
            bYp = nb()
            for a in range(4):
                P.op("pe", lambda e, a=a, q4=q4, Hc=Hc, bYp=bYp: e.matmul(ps[bYp][:, a * 128:(a + 1) * 128], q4[:, a, 1, :], Hc[:, a * 128:(a + 1) * 128],
                                                                        start=True, stop=False), reads=[bqt[j], bHc], writes=[bps[bYp]])
                for half in range(2):
                    h = a * 2 + half
                    o = ps[bYp][:, h * 64:(h + 1) * 64]
                    P.op("pe", lambda e, o=o, h=h: e.matmul(o, ARBv[:, h, :], Us[:, h * 64:(h + 1) * 64], start=False, stop=False),
                         reads=[bARB, bUs], writes=[bps[bYp]])
                    P.op("pe", lambda e, o=o, h=h, half=half: e.matmul(o, AKv[:, h, 1, :], vtm[:, h * 64:(h + 1) * 64], start=False, stop=(half == 1)),
                         reads=[bAK, bvtm], writes=[bps[bYp]])
            if os.environ.get('RW_B3') == '3':
                P.barrier(); A.reset(m0); return
            bHp = nb()
            for a in range(4):
                o = ps[bHp][:, a * 128:(a + 1) * 128]
                P.op("pe", lambda e, o=o, a=a: e.matmul(o, btmk[:, a * 128:(a + 1) * 128], Us[:, a * 128:(a + 1) * 128], start=True, stop=False),
                     reads=[bbtmk, bUs], writes=[bps[bHp]])
                P.op("pe", lambda e, o=o, a=a: e.matmul(o, btmk[:, 512 + a * 128:512 + (a + 1) * 128], vtm[:, a * 128:(a + 1) * 128],
                                                        start=False, stop=True), reads=[bbtmk, bvtm], writes=[bps[bHp]])
            for half in range(2):
                rows = slice(half * 64, half * 64 + 64)
                pin = ps[bHp].rearrange("p (a x i) -> p a x i", a=4, x=2)[rows, :, half, :]
                P.op("dve", lambda e, pin=pin, rows=rows, half=half, Hn=Hn, Hc=Hc: e.tensor_tensor(
                    out=Hn.rearrange("p (a x i) -> p a x i", a=4, x=2)[rows, :, half, :], in0=pin,
                    in1=Hc.rearrange("p (a x i) -> p a x i", a=4, x=2)[rows, :, half, :], op=ALU.add),
                    reads=[bps[bHp], bHc], writes=[bHn])
            P.op("dve", lambda e, Hn=Hn, c=c: e.tensor_tensor(
                out=Hn.rearrange("p (a i) -> p a i", a=4), in0=Hn.rearrange("p (a i) -> p a i", a=4),
                in1=GL.rearrange("p (a c) -> p a c", a=4)[:, :, c:c + 1].broadcast_to([128, 4, 128]), op=ALU.mult),
                reads=[bHn, bGL], writes=[bHn])
            cur = 1 - cur
            if d == 0:
                P.op("act", lambda e, bYp=bYp, j=j: e.activation(out=Ysb[j], in_=ps[bYp], func=AF.Copy), reads=[bps[bYp]], writes=[bY[j]])
                P.dma(X.RY0[cs, :], Ysb[j], reads=[bY[j]])
                continue
            P.dma(y0[j], X.RY0[cs, :], writes=[by0[j]])
            P.dma(rkbin[j].rearrange("p (a l) -> p a l", a=4), X.RKB[:, cs].rearrange("(a p) l -> p a l", p=128), writes=[brkbin[j]])
            P.dma(gin[j].rearrange("p (a l) -> p a l", a=4), X.RGT[:, cs].rearrange("(a p) l -> p a l", p=128), writes=[bgin[j]])
            Y = Ysb[j]
            bYj = bY[j]
            P.op("dve", lambda e, bYp=bYp, Y=Y, j=j: e.tensor_tensor(out=Y, in0=ps[bYp], in1=y0[j], op=ALU.add),
                 reads=[bps[bYp], by0[j]], writes=[bYj])
            Y3 = Y.rearrange("p (h i) -> p h i", h=8)
            s8 = st8.rearrange("p (k h) -> p k h", k=8)
            P.op("dve", lambda e, Y3=Y3, s8=s8: e.tensor_reduce(out=s8[:, 0, :], in_=Y3, axis=AX.X, op=ALU.add), reads=[bYj], writes=[bst8])
            P.op("act", lambda e, Y=Y: e.activation(out=ysq, in_=Y, func=AF.Square), reads=[bYj], writes=[bysq])
            P.op("dve", lambda e, s8=s8: e.tensor_reduce(out=s8[:, 1, :], in_=ysq.rearrange("p (h i) -> p h i", h=8), axis=AX.X, op=ALU.add),
                 reads=[bysq], writes=[bst8])
            P.op("dve", lambda e, s8=s8: e.tensor_scalar(out=s8[:, 2, :], in0=s8[:, 0, :], scalar1=1.0 / 64, scalar2=None, op0=ALU.mult),
                 reads=[bst8], writes=[bst8])
            P.op("dve", lambda e, s8=s8: e.tensor_tensor(out=s8[:, 5, :], in0=s8[:, 2, :], in1=s8[:, 2, :], op=ALU.mult), reads=[bst8], writes=[bst8])
            P.op("dve", lambda e, s8=s8: e.scalar_tensor_tensor(out=s8[:, 3, :], in0=s8[:, 1, :], scalar=1.0 / 64, in1=s8[:, 5, :],
                                                                op0=ALU.mult, op1=ALU.subtract), reads=[bst8], writes=[bst8])
            P.op("dve", lambda e, s8=s8: e.tensor_scalar(out=s8[:, 3, :], in0=s8[:, 3, :], scalar1=GN_EPS, scalar2=None, op0=ALU.add),
                 reads=[bst8], writes=[bst8])
            P.op("act", lambda e, s8=s8: e.activation(out=s8[:, 3, :], in_=s8[:, 3, :], func=AF.Sqrt), reads=[bst8], writes=[bst8])
            P.op("dve", lambda e, s8=s8: e.reciprocal(out=s8[:, 3, :], in_=s8[:, 3, :]), reads=[bst8], writes=[bst8])
            P.op("dve", lambda e, Y3=Y3, s8=s8: e.tensor_tensor(out=Y3, in0=Y3, in1=s8[:, 2, :].unsqueeze(2).broadcast_to([128, 8, 64]),
                                                                op=ALU.subtract), reads=[bYj, bst8], writes=[bYj])
            P.op("dve", lambda e, Y3=Y3, s8=s8: e.tensor_tensor(out=Y3, in0=Y3, in1=s8[:, 3, :].unsqueeze(2).broadcast_to([128, 8, 64]),
                                                                op=ALU.mult), reads=[bYj, bst8], writes=[bYj])
            P.op("pool", lambda e, Y=Y: e.tensor_tensor(out=Y, in0=Y, in1=lnwb[:, 0:512], op=ALU.mult), reads=[bYj, blnwb], writes=[bYj])
            P.op("pool", lambda e, Y=Y: e.tensor_tensor(out=Y, in0=Y, in1=lnwb[:, 512:1024], op=ALU.add), reads=[bYj, blnwb], writes=[bYj])
            bB = nb()
            for a in range(4):
                P.op("pe", lambda e, bB=bB, a=a, j=j: e.matmul(ps[bB][:, a * 2:(a + 1) * 2], rkbin[j][:, a * 128:(a + 1) * 128], halfsel,
                                                               start=True, stop=True), reads=[brkbin[j], bc], writes=[bps[bB]])
            P.op("act", lambda e, bB=bB, s8=s8: e.activation(out=s8[:, 4, :], in_=ps[bB][:, 0:8], func=AF.Copy), reads=[bps[bB]], writes=[bst8])
            P.op("dve", lambda e, s8=s8: e.tensor_tensor(out=ysq.rearrange("p (h i) -> p h i", h=8), in0=vtm.rearrange("p (h i) -> p h i", h=8),
                                                         in1=s8[:, 4, :].unsqueeze(2).broadcast_to([128, 8, 64]), op=ALU.mult),
                 reads=[bvtm, bst8, bysq], writes=[bysq])
            P.op("pool", lambda e, Y=Y: e.tensor_tensor(out=Y, in0=Y, in1=ysq, op=ALU.add), reads=[bYj, bysq], writes=[bYj])
            bO = nb()
            for a in range(4):
                P.op("pe", lambda e, bO=bO, a=a, Y=Y: e.matmul(ps[bO][:, a * 128:(a + 1) * 128], Y[:, a * 128:(a + 1) * 128], ident,
                                                               start=True, stop=True), reads=[bYj, bc], writes=[bps[bO]])
            P.op("dve", lambda e, bO=bO, j=j: e.tensor_tensor(out=oT, in0=ps[bO], in1=gin[j], op=ALU.mult), reads=[bps[bO], bgin[j]], writes=[boT])
            P.dma(X.OT[OT_RWKV:OT_RWKV + 512, cs].rearrange("(a p) l -> p a l", p=128), oT.rearrange("p (a l) -> p a l", a=4), reads=[boT])
        P.barrier()
    A.reset(m0)


def stage_out(X, layer):
    nc, P, A = X.nc, X.P, X.A
    T = X.T
    NTB = T // 512
    so, small, bc = X.so, X.small, X.bconst
    ps, bps = X.ps, X.bps
    ones = X.ones
    m0 = A.mark()
    psi = [0]

    def nb():
        b = psi[0] % 8
        psi[0] += 1
        return b
    x1 = A.alloc(8 * 512)
    bx1 = Buf()
    h2 = A.alloc(8 * 512)
    bh2 = Buf()
    R = A.alloc(32 * 512)
    ot = R[:, 0:16 * 512]
    mg = R[:, 16 * 512:24 * 512]
    xb = R[:, 24 * 512:32 * 512]
    sq = R[:, 0:8 * 512]
    u = R
    bot, bmg, bxb, bsq, bu = Buf(), Buf(), Buf(), Buf(), Buf()
    wbs = [A.alloc(16 * 128) for _ in range(2)]
    bwbs = [Buf() for _ in range(2)]
    wo = [A.alloc(8 * 128) for _ in range(2)]
    bwo = [Buf() for _ in range(2)]
    w1 = [A.alloc(8 * 128) for _ in range(3)]
    bw1 = [Buf() for _ in range(3)]
    w2s = [A.alloc(32 * 128) for _ in range(2)]
    bw2s = [Buf() for _ in range(2)]
    gt = [A.alloc(512) for _ in range(3)]
    bgt = [Buf() for _ in range(3)]
    tmp = A.alloc(512)
    btmp = Buf()
    rstd = A.alloc(512)
    brstd = Buf()
    KT = (4, 8, 4)
    K0 = (0, 4, 12)
    gi_ = 0
    w1i = 0
    for tb in range(NTB):
        t0 = tb * 512
        ts = slice(t0, t0 + 512)
        P.dma(ot.rearrange("p (k t) -> p k t", k=16), X.OT[:, ts].rearrange("(k p) t -> p k t", p=128), writes=[bot])
        P.dma(xb.rearrange("p (k t) -> p k t", k=8), X.xcur[:, ts].rearrange("(k p) t -> p k t", p=128), writes=[bxb])
        for m in range(8):
            wb, bwb = wbs[m % 2], bwbs[m % 2]
            P.dma(wb.rearrange("p (k c) -> p k c", k=16), X.w_branch[layer, :, m * 128:(m + 1) * 128].rearrange("(k p) c -> p k c", p=128),
                  writes=[bwb])
            for i in range(3):
                g = gi_ % 3
                gi_ += 1
                r0 = C_GATE + i * 1024 + m * 128
                P.dma(gt[g], X.PT[r0:r0 + 128, ts], writes=[bgt[g]])
                P.op("act", lambda e, g=g: e.activation(out=gt[g], in_=gt[g], func=AF.Sigmoid), reads=[bgt[g]], writes=[bgt[g]])
                b = nb()
                for kk_ in range(KT[i]):
                    k = K0[i] + kk_
                    P.op("pe", lambda e, b=b, k=k, kk_=kk_, i=i, wb=wb: e.matmul(ps[b], wb[:, k * 128:(k + 1) * 128], ot[:, k * 512:(k + 1) * 512],
                                                                         start=(kk_ == 0), stop=(kk_ == KT[i] - 1)), reads=[bwb, bot], writes=[bps[b]])
                if i == 0:
                    P.op("dve", lambda e, b=b, g=g, m=m: e.tensor_tensor(out=mg[:, m * 512:(m + 1) * 512], in0=ps[b], in1=gt[g], op=ALU.mult),
                         reads=[bps[b], bgt[g]], writes=[bmg])
                else:
                    P.op("dve", lambda e, b=b, g=g: e.tensor_tensor(out=tmp, in0=ps[b], in1=gt[g], op=ALU.mult),
                         reads=[bps[b], bgt[g]], writes=[btmp])
                    P.op("pool", lambda e, m=m: e.tensor_tensor(out=mg[:, m * 512:(m + 1) * 512], in0=mg[:, m * 512:(m + 1) * 512], in1=tmp, op=ALU.add),
                         reads=[btmp, bmg], writes=[bmg])
        for n in range(8):
            j = n % 2
            P.dma(wo[j].rearrange("p (k c) -> p k c", k=8), X.w_out[layer, :, n * 128:(n + 1) * 128].rearrange("(k p) c -> p k c", p=128),
                  writes=[bwo[j]])
            b = nb()
            for k in range(8):
                P.op("pe", lambda e, b=b, k=k, j=j: e.matmul(ps[b], wo[j][:, k * 128:(k + 1) * 128], mg[:, k * 512:(k + 1) * 512],
                                                             start=(k == 0), stop=(k == 7)), reads=[bwo[j], bmg], writes=[bps[b]])
            P.op("dve", lambda e, b=b, n=n: e.tensor_tensor(out=x1[:, n * 512:(n + 1) * 512], in0=ps[b], in1=xb[:, n * 512:(n + 1) * 512], op=ALU.add),
                 reads=[bps[b], bxb], writes=[bx1])
        P.op("act", lambda e: e.activation(out=sq, in_=x1, func=AF.Square), reads=[bx1, bot], writes=[bsq, bot])
        b = nb()
        for k in range(8):
            P.op("pe", lambda e, b=b, k=k: e.matmul(ps[b], ones, sq[:, k * 512:(k + 1) * 512], start=(k == 0), stop=(k == 7)),
                 reads=[bsq, bc], writes=[bps[b]])
        P.op("dve", lambda e, b=b: e.tensor_scalar(out=rstd, in0=ps[b], scalar1=1.0 / D, scalar2=EPS, op0=ALU.mult, op1=ALU.add),
             reads=[bps[b]], writes=[brstd])
        P.op("act", lambda e: e.activation(out=rstd, in_=rstd, func=AF.Sqrt), reads=[brstd], writes=[brstd])
        P.op("dve", lambda e: e.reciprocal(out=rstd, in_=rstd), reads=[brstd], writes=[brstd])
        mno = so["mlp_norm"] + layer * 8
        for k in range(8):
            P.op("dve", lambda e, k=k: e.scalar_tensor_tensor(out=h2[:, k * 512:(k + 1) * 512], in0=x1[:, k * 512:(k + 1) * 512],
                                                              scalar=small[:, mno + k:mno + k + 1], in1=rstd, op0=ALU.mult, op1=ALU.mult),
                 reads=[bx1, brstd, bc], writes=[bh2])
        for f in range(32):
            j = w1i % 3
            w1i += 1
            P.dma(w1[j].rearrange("p (k c) -> p k c", k=8), X.w_mlp_in[layer, :, f * 128:(f + 1) * 128].rearrange("(k p) c -> p k c", p=128),
                  writes=[bw1[j]])
            b = nb()
            for k in range(8):
                P.op("pe", lambda e, b=b, k=k, j=j: e.matmul(ps[b], w1[j][:, k * 128:(k + 1) * 128], h2[:, k * 512:(k + 1) * 512],
                                                             start=(k == 0), stop=(k == 7)), reads=[bw1[j], bh2], writes=[bps[b]])
            uf = u[:, f * 512:(f + 1) * 512]
            P.op("act", lambda e, b=b, uf=uf: e.activation(out=uf, in_=ps[b], func=AF.Relu),
                 reads=[bps[b], bot, bmg, bxb, bsq], writes=[bu])
            P.op("pool" if f % 2 else "dve", lambda e, uf=uf: e.tensor_tensor(out=uf, in0=uf, in1=uf, op=ALU.mult), reads=[bu], writes=[bu])
        for n in range(8):
            w2, bw2 = w2s[n % 2], bw2s[n % 2]
            P.dma(w2.rearrange("p (k c) -> p k c", k=32), X.w_mlp_out[layer, :, n * 128:(n + 1) * 128].rearrange("(k p) c -> p k c", p=128),
                  writes=[bw2])
            b = nb()
            for f in range(32):
                P.op("pe", lambda e, b=b, f=f, w2=w2: e.matmul(ps[b], w2[:, f * 128:(f + 1) * 128], u[:, f * 512:(f + 1) * 512],
                                                        start=(f == 0), stop=(f == 31)), reads=[bw2, bu], writes=[bps[b]])
            P.op("dve", lambda e, b=b, n=n: e.tensor_tensor(out=x1[:, n * 512:(n + 1) * 512], in0=ps[b], in1=x1[:, n * 512:(n + 1) * 512], op=ALU.add),
                 reads=[bps[b], bx1], writes=[bx1])
        P.dma(X.xnext[:, ts].rearrange("(k p) t -> p k t", p=128), x1.rearrange("p (k t) -> p k t", k=8), reads=[bx1],
              writes=[])
        bot.readers.update(bu.readers)
        bxb.readers.update(bu.readers)
        bot.last_w = bu.last_w
        bxb.last_w = bu.last_w
        bmg.last_w = bu.last_w
        bmg.readers.update(bu.readers)
    P.barrier()
    A.reset(m0)


def stage_final(X):
    nc, P, A = X.nc, X.P, X.A
    T = X.T
    NTB = T // 512
    so, small, bc = X.so, X.small, X.bconst
    ps, bps = X.ps, X.bps
    ones = X.ones
    m0 = A.mark()
    xt = [A.alloc(8 * 512) for _ in range(2)]
    bxt = [Buf() for _ in range(2)]
    sq = A.alloc(8 * 512)
    bsq = Buf()
    rstd = A.alloc(512)
    brstd = Buf()
    fo = so["final_norm"]
    for tb in range(NTB):
        j = tb % 2
        ts = slice(tb * 512, (tb + 1) * 512)
        P.dma(xt[j].rearrange("p (k t) -> p k t", k=8), X.xcur[:, ts].rearrange("(k p) t -> p k t", p=128), writes=[bxt[j]])
        P.op("act", lambda e, j=j: e.activation(out=sq, in_=xt[j], func=AF.Square), reads=[bxt[j]], writes=[bsq])
        b = tb % 8
        for k in range(8):
            P.op("pe", lambda e, b=b, k=k: e.matmul(ps[b], ones, sq[:, k * 512:(k + 1) * 512], start=(k == 0), stop=(k == 7)),
                 reads=[bsq, bc], writes=[bps[b]])
        P.op("dve", lambda e, b=b: e.tensor_scalar(out=rstd, in0=ps[b], scalar1=1.0 / D, scalar2=EPS, op0=ALU.mult, op1=ALU.add),
             reads=[bps[b]], writes=[brstd])
        P.op("act", lambda e: e.activation(out=rstd, in_=rstd, func=AF.Sqrt), reads=[brstd], writes=[brstd])
        P.op("dve", lambda e: e.reciprocal(out=rstd, in_=rstd), reads=[brstd], writes=[brstd])
        for k in range(8):
            P.op("dve", lambda e, k=k, j=j: e.scalar_tensor_tensor(out=xt[j][:, k * 512:(k + 1) * 512], in0=xt[j][:, k * 512:(k + 1) * 512],
                                                                   scalar=small[:, fo + k:fo + k + 1], in1=rstd, op0=ALU.mult, op1=ALU.mult),
                 reads=[bxt[j], brstd, bc], writes=[bxt[j]])
        P.dma(X.yT[:, ts].rearrange("(k p) t -> p k t", p=128), xt[j].rearrange("p (k t) -> p k t", k=8), reads=[bxt[j]])
    P.barrier()
    A.reset(m0)


def pack_small(inp):
    so = {}
    cols = []
    off = 0
    so["attn_norm"] = off
    a = inp["attn_norm"].reshape(DEPTH, 8, 128).transpose(2, 0, 1).reshape(128, DEPTH * 8)
    cols.append(a)
    off += a.shape[1]

    def add(name, arr):
        nonlocal off
        arr = np.asarray(arr, np.float32).reshape(128, -1)
        so[name] = off
        cols.append(arr)
        off += arr.shape[1]
    cw = inp["ssm_conv_w"].reshape(DEPTH, 5, 12, 128).transpose(3, 0, 2, 1)
    add("ssm_conv_w", cw)
    add("ssm_conv_b", inp["ssm_conv_b"].reshape(DEPTH, 12, 128).transpose(2, 0, 1))
    dtb = np.zeros((128, DEPTH), np.float32)
    alg = np.zeros((128, DEPTH), np.float32)
    for dr in range(2):
        dtb[dr * 32:dr * 32 + 16] = inp["ssm_dt_bias"][:, dr].T
        alg[dr * 32:dr * 32 + 16] = inp["ssm_a_log"][:, dr].T
    add("ssm_dt_bias", dtb)
    add("ssm_a_log", alg)
    add("ssm_d", np.broadcast_to(inp["ssm_d"].reshape(1, DEPTH * 16), (128, DEPTH * 16)))
    add("ssm_norm_w", inp["ssm_norm_w"].reshape(DEPTH, 8, 128).transpose(2, 0, 1))
    add("mlp_norm", inp["mlp_norm"].reshape(DEPTH, 8, 128).transpose(2, 0, 1))
    add("final_norm", inp["final_norm"].reshape(8, 128).T)
    mu = np.zeros((128, DEPTH, 16), np.float32)
    for L in range(DEPTH):
        m = inp["rwkv_mu"][L]
        for i in range(12):
            mu[:, L, i] = m[i * 128:(i + 1) * 128]
        mu[0:64, L, 12] = m[1536:1600]
        mu[0:64, L, 13] = m[1600:1664]
        mu[0:64, L, 14] = m[1664:1728]
    add("rw_mu", mu)

    def fm4(a):
        return np.asarray(a).reshape(DEPTH, 4, 128).transpose(2, 0, 1)
    add("rw_w0", np.stack([fm4(inp["rwkv_w0"][:, d]) for d in range(2)], axis=2))
    add("rw_a0", np.stack([fm4(inp["rwkv_a0"][:, d]) for d in range(2)], axis=2))
    add("rw_kk", fm4(inp["rwkv_k_k"]))
    add("rw_ka", fm4(inp["rwkv_k_a"]))
    add("rw_rk", fm4(inp["rwkv_r_k"].reshape(DEPTH, 512)))
    return np.ascontiguousarray(np.concatenate(cols, axis=1).astype(np.float32)), so


def build(T, NL, so, nsmall, co, ncst, dbg=(), stages=("in", "ret", "ssm", "rwkv", "out"), L0=0):
    nc = bass.Bass("TRN2", target_bir_lowering=False)
    X = Ctx()
    X.nc = nc
    X.T = T
    X.so = so
    X.co = co
    X.P = Prog(nc)
    X.xT = nc.dram_tensor("xT", [D, T], F32, kind="ExternalInput").ap()
    X.w_in = nc.dram_tensor("w_in", [DEPTH, D, N_IN], F32, kind="ExternalInput").ap()
    small_d = nc.dram_tensor("small", [128, nsmall], F32, kind="ExternalInput").ap()
    cst_d = nc.dram_tensor("cst", [128, ncst], F32, kind="ExternalInput").ap()
    X.rope = nc.dram_tensor("rope", [128, 2, T], F32, kind="ExternalInput").ap()

    def scratch(name, shape):
        kind = "ExternalOutput" if name in dbg else "Internal"
        return nc.dram_tensor(name, shape, F32, kind=kind).ap()
    X.PT = scratch("PT", [N_IN, T])
    X.Vtm = scratch("Vtm", [T, 512])
    X.OT = scratch("OT", [2048, T])
    X.RWQ = scratch("RWQ", [2, 512, T // 128, 4, 128])
    X.RVT = scratch("RVT", [512, T])
    X.RGT = scratch("RGT", [512, T])
    X.RKB = scratch("RKB", [512, T])
    X.RGL = scratch("RGL", [2, 512, T // 128])
    X.RY0 = scratch("RY0", [T, 512])
    X.DBG = scratch("DBG", [128, 8192])
    X.rw_w_up = nc.dram_tensor("rwkv_w_up", [DEPTH, 2, 32, 512], F32, kind="ExternalInput").ap()
    X.rw_a_up = nc.dram_tensor("rwkv_a_up", [DEPTH, 2, 32, 512], F32, kind="ExternalInput").ap()
    X.rw_g_up = nc.dram_tensor("rwkv_g_up", [DEPTH, 64, 512], F32, kind="ExternalInput").ap()
    X.rw_lnwb = nc.dram_tensor("rw_lnwb", [DEPTH, 2, 512], F32, kind="ExternalInput").ap()
    X.w_branch = nc.dram_tensor("w_branch", [DEPTH, 2048, D], F32, kind="ExternalInput").ap()
    X.w_out = nc.dram_tensor("w_out", [DEPTH, D, D], F32, kind="ExternalInput").ap()
    X.w_mlp_in = nc.dram_tensor("w_mlp_in", [DEPTH, D, 4096], F32, kind="ExternalInput").ap()
    X.w_mlp_out = nc.dram_tensor("w_mlp_out", [DEPTH, 4096, D], F32, kind="ExternalInput").ap()
    X.RSF = scratch("RSF", [T // 128, 128, 512])
    X.XA = scratch("XA", [D, T])
    X.XB = scratch("XB", [D, T])
    X.XC = scratch("XC", [1536, T])
    X.Q4 = scratch("Q4", [64, 4, T])
    X.YP = scratch("YP", [T, 1024])
    X.HINF = scratch("HINF", [T // 128, 128, 1024])
    X.HINB = scratch("HINB", [T // 128, 128, 1024])
    X.yT = nc.dram_tensor("yT", [D, T], F32, kind="ExternalOutput").ap()
    X.A = Arena(nc, 50500)
    A = X.A
    psall = nc.alloc_psum_tensor("psall", [128, 4096], F32)
    X.psall = psall
    X.ps = [psall[:, i * 512:(i + 1) * 512] for i in range(8)]
    X.bps = [Buf(excl=True) for _ in range(8)]
    X.bconst = Buf()
    X.cst_d = cst_d
    X.cst = None
    X.small = A.alloc(nsmall)
    X.ones = A.alloc(128)
    X.ident = A.alloc(128)
    P = X.P
    P.dma(X.ones, cst_d[:, co["ones"]:co["ones"] + 128], writes=[X.bconst])
    P.dma(X.ident, cst_d[:, co["ident"]:co["ident"] + 128], writes=[X.bconst])
    P.dma(X.small, small_d, writes=[X.bconst])
    P.barrier()
    X.xcur = X.xT
    for layer in range(L0, L0 + NL):
        X.xnext = X.XA if layer % 2 == 0 else X.XB
        if "in" in stages:
            stage_in(X, layer)
        if "ret" in stages:
            stage_ret(X, layer)
        if "ssm" in stages:
            stage_ssm(X, layer)
        if "rwkv" in stages:
            stage_rwkv(X, layer)
        if "out" in stages:
            stage_out(X, layer)
            X.xcur = X.xnext
    if "out" in stages:
        stage_final(X)
    P.emit()
    return nc, X


def make_inputs(inp, T, b):
    x = np.asarray(inp["x"])[b, :T]
    return {"xT": np.ascontiguousarray(x.T)}


def kernel(**inp):
    inp = {k: np.asarray(v) for k, v in inp.items()}
    B, T = inp["x"].shape[0], inp["x"].shape[1]
    small, so = pack_small(inp)
    cst, co, rope = host_consts(T)
    nc, X = build(T, DEPTH, so, small.shape[1], co, cst.shape[1])
    shared = {"w_in": inp["w_in"], "small": small, "cst": cst, "rope": rope,
              "rwkv_w_up": inp["rwkv_w_up"], "rwkv_a_up": inp["rwkv_a_up"], "rwkv_g_up": inp["rwkv_g_up"],
              "rw_lnwb": np.ascontiguousarray(np.stack([inp["rwkv_ln_w"], inp["rwkv_ln_b"]], axis=1)),
              "w_branch": inp["w_branch"], "w_out": inp["w_out"], "w_mlp_in": inp["w_mlp_in"], "w_mlp_out": inp["w_mlp_out"]}
    in_maps = []
    for b in range(B):
        m = dict(shared)
        m["xT"] = np.ascontiguousarray(inp["x"][b].T)
        in_maps.append(m)
    res = run_bass_kernel_spmd(nc, in_maps, core_ids=list(range(B)))
    out = np.stack([np.ascontiguousarray(r["yT"].T) for r in res.results], axis=0)
    return out.astype(np.float32)
```
